# Optimizing a Trainium2 kernel written in Bass

```python
import math
import jax, jax.numpy as jnp
from jax import lax
import numpy as np

D_MODEL = 2048
BATCH = 8
SEQ = 2048
DEPTH = 2

CTX_LEN = 256
GRID_W = 64
F32 = jnp.float32
EPS = 1e-6
N_MOD = 9
FFN_RESIDUAL = 0.5
D_FF = 5632
N_BRANCH = 4
BRANCH_W = 512
HEAD_DIM = 128
SG_GROUPS = 4
SG_CHUNK = 128
ML_HEADS = 4
ML_CHUNK = 128
GD_HEADS = 4
GD_CHUNK = 64
GD_CONV = 3
AT_Q_HEADS = 4
AT_KV_HEADS = 2
AT_BLOCK = 128
ROPE_THETA = 10000.0

COLS = (
    ("ml_kv", 2 * BRANCH_W),
    ("ml_if", 4 * ML_HEADS),
    ("gd_kv", 2 * BRANCH_W),
    ("gd_ba", 4 * GD_HEADS),
    ("at_kv", 2 * AT_KV_HEADS * HEAD_DIM),
    ("sg_uv", 2 * BRANCH_W),
    ("ml_q", BRANCH_W),
    ("ml_o", BRANCH_W),
    ("gd_q", BRANCH_W),
    ("gd_z", BRANCH_W),
    ("at_q", AT_Q_HEADS * HEAD_DIM),
    ("gate", N_BRANCH * D_MODEL),
)
KV_COLS = 2 * BRANCH_W + 4 * ML_HEADS + 2 * BRANCH_W + 4 * GD_HEADS + 2 * AT_KV_HEADS * HEAD_DIM
IN_COLS = KV_COLS + 2 * BRANCH_W + 5 * BRANCH_W + N_BRANCH * D_MODEL

kernel_name = "hybrid_gated_branch_dit_block"


def _col(z, name):
    off = 0
    for n, w in COLS:
        if n == name:
            return z[..., off:off + w]
        off += w
    raise KeyError(name)


def rms_norm(x, g):
    xf = x.astype(F32)
    y = xf * lax.rsqrt(jnp.mean(xf * xf, axis=-1, keepdims=True) + EPS)
    return (y * g.astype(F32)).astype(x.dtype)


def _l2norm(x):
    return x * lax.rsqrt(jnp.sum(x * x, axis=-1, keepdims=True) + EPS)


def modulate(h, shift, scale):
    return h * (1.0 + scale) + shift


def swiglu(h, w_in, w_out):
    a, b = jnp.split(h @ w_in, 2, axis=-1)
    return (jax.nn.silu(a) * b) @ w_out


def _heads(t, n_heads):
    Bn, T, W = t.shape
    return t.reshape(Bn, T, n_heads, W // n_heads).transpose(0, 2, 1, 3)


def _dwconv(x, w):
    K, C = w.shape
    return lax.conv_general_dilated(
        x, w[:, None, :].astype(x.dtype), window_strides=(1,), padding=[(K // 2, K // 2)],
        dimension_numbers=("NWC", "WIO", "NWC"), feature_group_count=C)


def axial_rope_tables(rows):
    n = HEAD_DIM // 4
    inv = ROPE_THETA ** (-jnp.arange(n, dtype=F32) / n)
    r = jnp.repeat(jnp.arange(rows, dtype=F32), GRID_W)
    cl = jnp.tile(jnp.arange(GRID_W, dtype=F32), rows)
    ar = r[:, None] * inv
    ac = cl[:, None] * inv
    return (jnp.cos(ar), jnp.sin(ar), jnp.cos(ac), jnp.sin(ac))


def _rope_1d(x, cos, sin):
    x1, x2 = jnp.split(x, 2, axis=-1)
    return jnp.concatenate([x1 * cos - x2 * sin, x2 * cos + x1 * sin], axis=-1)


def axial_rope(x, tabs):
    cos_r, sin_r, cos_c, sin_c = tabs
    e = lambda t: t[None, :, None, :]
    xr, xc = jnp.split(x.astype(F32), 2, axis=-1)
    y = jnp.concatenate([_rope_1d(xr, e(cos_r), e(sin_r)), _rope_1d(xc, e(cos_c), e(sin_c))], axis=-1)
    return y.astype(x.dtype)


def spatial_gating(z_uv, norm_g, w_s, b_s):
    Bn, T, _ = z_uv.shape
    u, v = jnp.split(jax.nn.gelu(z_uv), 2, axis=-1)
    v = rms_norm(v, norm_g)
    v = v.reshape(Bn, T // SG_CHUNK, SG_CHUNK, SG_GROUPS, BRANCH_W // SG_GROUPS)
    s = jnp.einsum("gts,bnsgc->bntgc", w_s.astype(v.dtype), v) + b_s.T[None, None, :, :, None]
    return u * s.reshape(Bn, T, BRANCH_W)


def _bidir(scan_fn, q, k, v, a_fw, b_fw, a_bw, b_bw, st_fw, st_bw, with_out):
    out_f, fin_f = scan_fn(q, k, v, a_fw, b_fw, st_fw, with_out)
    fl = lambda t: None if t is None else jnp.flip(t, axis=2)
    out_b, fin_b = scan_fn(fl(q), fl(k), fl(v), jnp.flip(a_bw, -1), jnp.flip(b_bw, -1), st_bw, with_out)
    out = (out_f + jnp.flip(out_b, axis=2)) if with_out else None
    return out, fin_f, fin_b


def mlstm_scan(q, k, v, logi, logf, state, with_out=True):
    Bn, H, T, d = k.shape
    L = ML_CHUNK
    N = T // L
    ch = lambda t: t.reshape(Bn, H, N, L, *t.shape[3:])
    k = ch(k) * (d ** -0.5)
    v = ch(v)
    li = ch(logi)
    lf = ch(logf)
    b = jnp.cumsum(lf, axis=-1)
    b_end = b[..., -1]
    a = b_end[..., None] - b + li
    m_loc = jnp.max(a, axis=-1)
    wgt = jnp.exp(a - m_loc[..., None])
    C_loc = jnp.einsum("bhnl,bhnle,bhnld->bhned", wgt, v, k)
    n_loc = jnp.einsum("bhnl,bhnld->bhnd", wgt, k)

    def step(carry, xs):
        C, n, m = carry
        Cl, nl, ml, be = xs
        m_new = jnp.maximum(be + m, ml)
        sp = jnp.exp(be + m - m_new)
        sl = jnp.exp(ml - m_new)
        new = (sp[..., None, None] * C + sl[..., None, None] * Cl, sp[..., None] * n + sl[..., None] * nl, m_new)
        return new, (carry if with_out else None)

    mv = lambda t: jnp.moveaxis(t, 2, 0)
    final, prev = lax.scan(step, state, (mv(C_loc), mv(n_loc), mv(m_loc), mv(b_end)))
    if not with_out:
        return None, final
    C_prev, n_prev, m_prev = (jnp.moveaxis(t, 0, 2) for t in prev)
    q = ch(q)
    incl = jnp.tril(jnp.ones((L, L), dtype=bool))
    Dm = jnp.where(incl, b[..., :, None] - b[..., None, :] + li[..., None, :], -jnp.inf)
    g = b + m_prev[..., None]
    m_t = jnp.maximum(jnp.max(Dm, axis=-1), g)
    P = jnp.exp(Dm - m_t[..., None]) * jnp.einsum("bhntd,bhnsd->bhnts", q, k)
    inter = jnp.exp(g - m_t)
    num = jnp.einsum("bhnts,bhnse->bhnte", P, v) + inter[..., None] * jnp.einsum("bhned,bhntd->bhnte", C_prev, q)
    den = jnp.sum(P, axis=-1) + inter * jnp.einsum("bhnd,bhntd->bhnt", n_prev, q)
    h = num / jnp.maximum(jnp.abs(den), jnp.exp(-m_t))[..., None]
    return h.reshape(Bn, H, T, d), final


def mlstm_branch(zc, zl, p, ctx_out):
    def prep(z, with_q):
        k, v = jnp.split(_col(z, "ml_kv"), 2, axis=-1)
        k = _heads(k, ML_HEADS).astype(F32)
        v = _heads(v, ML_HEADS).astype(F32)
        pre = (_col(z, "ml_if") + p["ml_if_bias"]).astype(F32)
        Bn, T, _ = pre.shape
        pre = pre.reshape(Bn, T, 4, ML_HEADS).transpose(2, 0, 3, 1)
        gates = (pre[0], jax.nn.log_sigmoid(pre[2]), pre[1], jax.nn.log_sigmoid(pre[3]))
        q = _heads(_col(z, "ml_q"), ML_HEADS).astype(F32) if with_q else None
        return q, k, v, gates

    def out(h, z):
        Bn, H, T, d = h.shape
        h = rms_norm(h.transpose(0, 2, 1, 3), p["ml_norm"].reshape(ML_HEADS, HEAD_DIM))
        return (jax.nn.sigmoid(_col(z, "ml_o").astype(F32)) * h.reshape(Bn, T, H * d)).astype(z.dtype)

    qc, kc, vc, gc = prep(zc, ctx_out)
    Bn = kc.shape[0]
    zero = (jnp.zeros((Bn, ML_HEADS, HEAD_DIM, HEAD_DIM), F32), jnp.zeros((Bn, ML_HEADS, HEAD_DIM), F32),
            jnp.zeros((Bn, ML_HEADS), F32))
    hc, st_fw, st_bw = _bidir(mlstm_scan, qc, kc, vc, *gc, zero, zero, ctx_out)
    ql, kl, vl, gl = prep(zl, True)
    hl, _, _ = _bidir(mlstm_scan, ql, kl, vl, *gl, st_fw, st_bw, True)
    return (out(hc, zc) if ctx_out else None), out(hl, zl)


def gdn_scan(q, k, v, g, beta, S0, with_out=True):
    Bn, H, T, d = k.shape
    L = GD_CHUNK
    N = T // L
    ch = lambda t: t.reshape(Bn, H, N, L, *t.shape[3:])
    k, v, g, beta = ch(k), ch(v), ch(g), ch(beta)
    gc = jnp.cumsum(g, axis=-1)
    incl = jnp.tril(jnp.ones((L, L), dtype=bool))
    decay = jnp.exp(jnp.where(incl, gc[..., :, None] - gc[..., None, :], -jnp.inf))
    kb = k * beta[..., None]
    M = jnp.tril(jnp.einsum("bhnid,bhnjd->bhnij", kb, k) * decay, -1)
    eye = jnp.eye(L, dtype=M.dtype)
    Tinv = lax.linalg.triangular_solve(M + eye, jnp.broadcast_to(eye, M.shape), left_side=True,
                                       lower=True, unit_diagonal=True)
    u = jnp.einsum("bhnij,bhnjd->bhnid", Tinv, v * beta[..., None])
    w = jnp.einsum("bhnij,bhnjd->bhnid", Tinv, kb * jnp.exp(gc)[..., None])
    k_end = k * jnp.exp(gc[..., -1:] - gc)[..., None]
    g_end = jnp.exp(gc[..., -1])
    mv = lambda t: jnp.moveaxis(t, 2, 0)
    if with_out:
        q = ch(q) * (d ** -0.5)
        qk = jnp.einsum("bhnid,bhnjd->bhnij", q, k) * decay
        q_dec = q * jnp.exp(gc)[..., None]

        def step(S, xs):
            u_n, w_n, ke_n, ge_n, qk_n, qd_n = xs
            v_new = u_n - jnp.einsum("bhld,bhde->bhle", w_n, S)
            o = jnp.einsum("bhld,bhde->bhle", qd_n, S) + jnp.einsum("bhij,bhje->bhie", qk_n, v_new)
            S = S * ge_n[..., None, None] + jnp.einsum("bhld,bhle->bhde", ke_n, v_new)
            return S, o

        S_fin, o = lax.scan(step, S0, (mv(u), mv(w), mv(k_end), mv(g_end), mv(qk), mv(q_dec)))
        return jnp.moveaxis(o, 0, 2).reshape(Bn, H, T, d), S_fin

    def step_state(S, xs):
        u_n, w_n, ke_n, ge_n = xs
        v_new = u_n - jnp.einsum("bhld,bhde->bhle", w_n, S)
        return S * ge_n[..., None, None] + jnp.einsum("bhld,bhle->bhde", ke_n, v_new), None

    S_fin, _ = lax.scan(step_state, S0, (mv(u), mv(w), mv(k_end), mv(g_end)))
    return None, S_fin


def gdn_branch(zc, zl, p, ctx_out):
    def prep(z, with_q):
        kv = jax.nn.silu(_dwconv(_col(z, "gd_kv"), p["gd_conv"][:, BRANCH_W:]))
        k, v = jnp.split(kv, 2, axis=-1)
        k = _l2norm(_heads(k, GD_HEADS).astype(F32))
        v = _heads(v, GD_HEADS).astype(F32)
        ba = _col(z, "gd_ba").astype(F32)
        Bn, T, _ = ba.shape
        ba = ba.reshape(Bn, T, 4, GD_HEADS).transpose(2, 0, 3, 1)
        A = jnp.exp(p["gd_a_log"].astype(F32))[:, None, :, None]
        dtb = p["gd_dt_bias"].astype(F32)[:, None, :, None]
        g = -A * jax.nn.softplus(ba[2:] + dtb)
        beta = jax.nn.sigmoid(ba[:2])
        q = None
        if with_q:
            q = jax.nn.silu(_dwconv(_col(z, "gd_q"), p["gd_conv"][:, :BRANCH_W]))
            q = _l2norm(_heads(q, GD_HEADS).astype(F32))
        return q, k, v, (g[0], beta[0], g[1], beta[1])

    def out(o, z):
        Bn, H, T, d = o.shape
        o = rms_norm(o.transpose(0, 2, 1, 3), p["gd_norm"])
        return (o.reshape(Bn, T, H * d) * jax.nn.silu(_col(z, "gd_z").astype(F32))).astype(z.dtype)

    qc, kc, vc, gc = prep(zc, ctx_out)
    Bn = kc.shape[0]
    zero = jnp.zeros((Bn, GD_HEADS, HEAD_DIM, HEAD_DIM), F32)
    oc, st_fw, st_bw = _bidir(gdn_scan, qc, kc, vc, *gc, zero, zero, ctx_out)
    ql, kl, vl, gl = prep(zl, True)
    ol, _, _ = _bidir(gdn_scan, ql, kl, vl, *gl, st_fw, st_bw, True)
    return (out(oc, zc) if ctx_out else None), out(ol, zl)


def block_attention(q, k, v):
    Bn, Tq, Hq, d = q.shape
    Hkv = k.shape[2]
    G = Hq // Hkv
    nb = Tq // AT_BLOCK
    qb = q.reshape(Bn, nb, AT_BLOCK, Hkv, G, d).transpose(1, 0, 3, 4, 2, 5)
    kt = k.transpose(0, 2, 1, 3)
    vt = v.transpose(0, 2, 1, 3)
    scale = d ** -0.5

    def one(qblk):
        s = jnp.einsum("bhgqd,bhkd->bhgqk", qblk, kt).astype(F32) * scale
        pr = jax.nn.softmax(s, axis=-1).astype(vt.dtype)
        return jnp.einsum("bhgqk,bhkd->bhgqd", pr, vt)

    o = lax.map(one, qb)
    return o.transpose(1, 0, 4, 2, 3, 5).reshape(Bn, Tq, Hq * d)


def attn_branch(zc, zl, p, rope, ctx_out):
    def kv(z):
        k, v = jnp.split(_col(z, "at_kv"), 2, axis=-1)
        Bn, T, _ = k.shape
        k = rms_norm(k.reshape(Bn, T, AT_KV_HEADS, HEAD_DIM), p["at_k_norm"])
        return k, v.reshape(Bn, T, AT_KV_HEADS, HEAD_DIM)

    def qry(z):
        q = _col(z, "at_q")
        Bn, T, _ = q.shape
        return rms_norm(q.reshape(Bn, T, AT_Q_HEADS, HEAD_DIM), p["at_q_norm"])

    kc, vc = kv(zc)
    kl, vl = kv(zl)
    kl = axial_rope(kl, rope)
    ql = axial_rope(qry(zl), rope)
    yl = block_attention(ql, jnp.concatenate([kc, kl], axis=1), jnp.concatenate([vc, vl], axis=1))
    yc = block_attention(qry(zc), kc, vc) if ctx_out else None
    return yc, yl


def merge_branches(ys, z, w_branch, w_out):
    gates = _col(z, "gate")
    Bn, T, _ = gates.shape
    gates = gates.reshape(Bn, T, N_BRANCH, D_MODEL)
    merged = sum(jax.nn.sigmoid(gates[:, :, n]) * (ys[n].astype(z.dtype) @ w_branch[n]) for n in range(N_BRANCH))
    return merged @ w_out


def trunk_layer(x, xc, c, c_ctx, p, rope, last):
    mod_l = (jax.nn.silu(c) @ p["mod_w"] + p["mod_b"])[:, None, :]
    mod_c = (jax.nn.silu(c_ctx) @ p["mod_w"] + p["mod_b"])[None, None, :]
    ml = jnp.split(mod_l, N_MOD, axis=-1)
    mc = jnp.split(mod_c, N_MOD, axis=-1)
    ctx_out = not last

    x = x + FFN_RESIDUAL * ml[2] * swiglu(modulate(rms_norm(x, p["ffn1_norm"]), ml[0], ml[1]),
                                          p["ffn1_w_in"], p["ffn1_w_out"])
    xc = xc + FFN_RESIDUAL * mc[2] * swiglu(modulate(rms_norm(xc, p["ffn1_norm"]), mc[0], mc[1]),
                                            p["ffn1_w_in"], p["ffn1_w_out"])

    hl = modulate(rms_norm(x, p["mix_norm"]), ml[3], ml[4])
    hc = modulate(rms_norm(xc, p["mix_norm"]), mc[3], mc[4])
    zl = hl @ p["w_in"]
    zc = hc @ (p["w_in"] if ctx_out else p["w_in"][:, :KV_COLS])
    y_sg_l = spatial_gating(_col(zl, "sg_uv"), p["sg_norm"], p["sg_w"], p["sg_b"])
    y_ml_c, y_ml_l = mlstm_branch(zc, zl, p, ctx_out)
    y_gd_c, y_gd_l = gdn_branch(zc, zl, p, ctx_out)
    y_at_c, y_at_l = attn_branch(zc, zl, p, rope, ctx_out)
    x = x + ml[5] * merge_branches([y_sg_l, y_ml_l, y_gd_l, y_at_l], zl, p["w_branch"], p["w_out"])
    if ctx_out:
        y_sg_c = spatial_gating(_col(zc, "sg_uv"), p["sg_norm"], p["sg_w"], p["sg_b"])
        xc = xc + mc[5] * merge_branches([y_sg_c, y_ml_c, y_gd_c, y_at_c], zc, p["w_branch"], p["w_out"])

    x = x + FFN_RESIDUAL * ml[8] * swiglu(modulate(rms_norm(x, p["ffn2_norm"]), ml[6], ml[7]),
                                          p["ffn2_w_in"], p["ffn2_w_out"])
    if ctx_out:
        xc = xc + FFN_RESIDUAL * mc[8] * swiglu(modulate(rms_norm(xc, p["ffn2_norm"]), mc[6], mc[7]),
                                                p["ffn2_w_in"], p["ffn2_w_out"])
    return x, xc


def setup_inputs(seed: int = 0) -> dict:
    key = jax.random.key(seed)
    ks = iter(jax.random.split(key, 40))
    L = DEPTH

    def normal(shape, scale):
        return scale * jax.random.normal(next(ks), shape, F32)

    def gain(shape):
        return 1.0 + 0.02 * jax.random.normal(next(ks), shape, F32)

    x = normal((BATCH, SEQ, D_MODEL), 1.0)
    c = normal((BATCH, D_MODEL), 1.0)
    ctx = normal((BATCH, CTX_LEN, D_MODEL), 1.0)
    c_ctx = normal((D_MODEL,), 1.0)
    mod_w = normal((L, D_MODEL, N_MOD * D_MODEL), 0.5 * D_MODEL ** -0.5)
    mod_b = normal((L, N_MOD * D_MODEL), 0.01)
    ffn1_norm = gain((L, D_MODEL))
    ffn1_w_in = normal((L, D_MODEL, 2 * D_FF), D_MODEL ** -0.5)
    ffn1_w_out = normal((L, D_FF, D_MODEL), D_FF ** -0.5)
    mix_norm = gain((L, D_MODEL))
    w_in = normal((L, D_MODEL, IN_COLS), D_MODEL ** -0.5)
    sg_norm = gain((L, BRANCH_W))
    sg_w = normal((L, SG_GROUPS, SG_CHUNK, SG_CHUNK), SG_CHUNK ** -0.5)
    sg_b = normal((L, SG_GROUPS, SG_CHUNK), 0.02)
    ml_if_bias = jnp.concatenate([normal((L, 2 * ML_HEADS), 0.1),
                                  3.0 + normal((L, 2 * ML_HEADS), 0.1)], axis=-1)
    ml_norm = gain((L, BRANCH_W))
    gd_conv = normal((L, GD_CONV, 3 * BRANCH_W), GD_CONV ** -0.5)
    gd_a_log = jnp.log(jax.random.uniform(next(ks), (L, 2, GD_HEADS), F32, 1.0, 16.0))
    dt = jnp.exp(jax.random.uniform(next(ks), (L, 2, GD_HEADS), F32, math.log(1e-3), math.log(1e-1)))
    gd_dt_bias = dt + jnp.log(-jnp.expm1(-dt))
    gd_norm = gain((L, HEAD_DIM))
    at_q_norm = gain((L, HEAD_DIM))
    at_k_norm = gain((L, HEAD_DIM))
    w_branch = normal((L, N_BRANCH, BRANCH_W, D_MODEL), BRANCH_W ** -0.5)
    w_out = normal((L, D_MODEL, D_MODEL), D_MODEL ** -0.5)
    ffn2_norm = gain((L, D_MODEL))
    ffn2_w_in = normal((L, D_MODEL, 2 * D_FF), D_MODEL ** -0.5)
    ffn2_w_out = normal((L, D_FF, D_MODEL), D_FF ** -0.5)
    return {"x": x, "c": c, "ctx": ctx, "c_ctx": c_ctx, "mod_w": mod_w, "mod_b": mod_b,
            "ffn1_norm": ffn1_norm, "ffn1_w_in": ffn1_w_in, "ffn1_w_out": ffn1_w_out,
            "mix_norm": mix_norm, "w_in": w_in, "sg_norm": sg_norm, "sg_w": sg_w, "sg_b": sg_b,
            "ml_if_bias": ml_if_bias, "ml_norm": ml_norm, "gd_conv": gd_conv, "gd_a_log": gd_a_log,
            "gd_dt_bias": gd_dt_bias, "gd_norm": gd_norm, "at_q_norm": at_q_norm, "at_k_norm": at_k_norm,
            "w_branch": w_branch, "w_out": w_out, "ffn2_norm": ffn2_norm, "ffn2_w_in": ffn2_w_in,
            "ffn2_w_out": ffn2_w_out}


def reference(x, c, ctx, c_ctx, mod_w, mod_b, ffn1_norm, ffn1_w_in, ffn1_w_out, mix_norm, w_in,
              sg_norm, sg_w, sg_b, ml_if_bias, ml_norm, gd_conv, gd_a_log, gd_dt_bias, gd_norm,
              at_q_norm, at_k_norm, w_branch, w_out, ffn2_norm, ffn2_w_in, ffn2_w_out):
    ROWS = x.shape[1] // GRID_W
    rope = axial_rope_tables(ROWS)
    xc = ctx
    for l in range(DEPTH):
        p = dict(mod_w=mod_w[l], mod_b=mod_b[l], ffn1_norm=ffn1_norm[l], ffn1_w_in=ffn1_w_in[l],
                 ffn1_w_out=ffn1_w_out[l], mix_norm=mix_norm[l], w_in=w_in[l], sg_norm=sg_norm[l],
                 sg_w=sg_w[l], sg_b=sg_b[l], ml_if_bias=ml_if_bias[l], ml_norm=ml_norm[l],
                 gd_conv=gd_conv[l], gd_a_log=gd_a_log[l], gd_dt_bias=gd_dt_bias[l], gd_norm=gd_norm[l],
                 at_q_norm=at_q_norm[l], at_k_norm=at_k_norm[l], w_branch=w_branch[l], w_out=w_out[l],
                 ffn2_norm=ffn2_norm[l], ffn2_w_in=ffn2_w_in[l], ffn2_w_out=ffn2_w_out[l])
        x, xc = trunk_layer(x, xc, c, c_ctx, p, rope, l == DEPTH - 1)
    return x
```

```python
import math
from contextlib import ExitStack
import numpy as np
import concourse.bass as bass
import concourse.mybir as mybir
from concourse.bass_utils import run_bass_kernel_spmd

F32 = mybir.dt.float32
BF16 = mybir.dt.bfloat16
AF = mybir.ActivationFunctionType
ALU = mybir.AluOpType

D = 2048
T_LAT = 2048
T_CTX = 256
TOK = T_CTX + T_LAT
DEPTH = 2
DFF = 5632
KC = D // 128
IN_COLS = 14368
EPS = 1e-6


class Res:
    __slots__ = ("name", "w", "rs")

    def __init__(self, name=""):
        self.name = name
        self.w = None
        self.rs = []


class Op:
    __slots__ = ("id", "eng", "fn", "deps", "dma", "prod", "cnt", "dsem", "dval", "dprev")

    def __init__(self, id, eng, fn, deps, dma):
        self.id = id
        self.eng = eng
        self.fn = fn
        self.deps = deps
        self.dma = dma
        self.prod = False
        self.cnt = None
        self.dsem = None
        self.dval = None
        self.dprev = None


ENGS = ("tensor", "vector", "scalar", "gpsimd", "sync")


class Prog:
    def __init__(self, nc, ndsem=6):
        self.nc = nc
        self.ops = []
        self.ndsem = ndsem

    def add(self, eng, fn, reads=(), writes=(), dma=False):
        oid = len(self.ops)
        deps = set()
        for r in reads:
            if r.w is not None:
                deps.add(r.w)
        for w in writes:
            if w.w is not None:
                deps.add(w.w)
            for x in w.rs:
                deps.add(x)
        deps.discard(oid)
        op = Op(oid, eng, fn, deps, dma)
        self.ops.append(op)
        for r in reads:
            r.rs.append(oid)
        for w in writes:
            w.w = oid
            w.rs = []
        return op

    def mm(self, out, lhsT, rhs, start, stop, reads, writes):
        self.add("tensor", lambda e: e.matmul(out, lhsT, rhs, start=start, stop=stop), reads, writes)

    def tr(self, out, in_, ident, reads, writes):
        self.add("tensor", lambda e: e.transpose(out, in_, ident), reads, writes)

    def act(self, eng, out, in_, func, reads, writes, bias=None, scale=None):
        kw = {}
        if bias is not None:
            kw["bias"] = bias
        if scale is not None:
            kw["scale"] = scale
        self.add(eng, lambda e: e.activation(out, in_, func, **kw), reads, writes)

    def tt(self, eng, out, a, b, op, reads, writes):
        self.add(eng, lambda e: e.tensor_tensor(out, a, b, op), reads, writes)

    def ts(self, eng, out, a, s1, s2, op0, op1, reads, writes):
        if op1 is None:
            self.add(eng, lambda e: e.tensor_scalar(out, a, s1, s2, op0), reads, writes)
        else:
            self.add(eng, lambda e: e.tensor_scalar(out, a, s1, s2, op0, op1), reads, writes)

    def stt(self, eng, out, a, sc, b, op0, op1, reads, writes):
        self.add(eng, lambda e: e.scalar_tensor_tensor(out, a, sc, b, op0, op1), reads, writes)

    def cp(self, eng, out, a, reads, writes):
        self.add(eng, lambda e: e.tensor_copy(out, a), reads, writes)

    def dma(self, eng, out, in_, reads, writes, **kw):
        self.add(eng, lambda e: e.dma_start(out=out, in_=in_, **kw), reads, writes, dma=True)

    def setup(self, stack):
        nc = self.nc
        self.esem = {e: stack.enter_context(nc.semaphore("s_" + e)) for e in ENGS}
        self.dsems = {e: [stack.enter_context(nc.semaphore("d_%s%d" % (e, i))) for i in range(self.ndsem)]
                      for e in ("sync", "gpsimd", "scalar")}
        self.ecnt = {e: 0 for e in ENGS}
        self.stack_ = stack
        self.SEM_CAP = 12000
        self.dcount = {e: [0] * self.ndsem for e in self.dsems}
        self.dnext = {e: 0 for e in self.dsems}
        self.emitted = 0
        self.nphase = 0

    def emit(self, final=False):
        nc = self.nc
        ops = self.ops
        lo = self.emitted
        cur = ops[lo:]
        self.emitted = len(ops)
        self.nphase += 1
        bar_e = dict(self.ecnt)
        bar_sem = dict(self.esem)
        for e_ in ENGS:
            if self.ecnt[e_] > self.SEM_CAP:
                self.esem[e_] = self.stack_.enter_context(nc.semaphore("s_%s_%d" % (e_, self.nphase)))
                self.ecnt[e_] = 0
        bar_d = {q: list(v) for q, v in self.dcount.items()}
        for op in cur:
            for d in op.deps:
                if d < lo:
                    continue
                p = ops[d]
                if not p.dma:
                    if p.eng == "tensor" and op.eng == "tensor" and not op.dma:
                        continue
                    p.prod = True
        lastc = {}
        for op in cur:
            if not op.dma:
                lastc[op.eng] = op
        for op in lastc.values():
            op.prod = True
        for op in cur:
            if op.dma:
                i = self.dnext[op.eng] % self.ndsem
                self.dnext[op.eng] += 1
                op.dsem = self.dsems[op.eng][i]
                op.dprev = self.dcount[op.eng][i]
                self.dcount[op.eng][i] += 16
                op.dval = self.dcount[op.eng][i]
            elif op.prod:
                self.ecnt[op.eng] += 1
                op.cnt = self.ecnt[op.eng]
        per = {e: [o for o in cur if o.eng == e] for e in ENGS}
        esem, dsems = dict(self.esem), self.dsems

        trace = getattr(self, "trace", None)
        if trace is None:
            trace = self.trace = {e: [] for e in ENGS}
        semname = getattr(self, "semname", None)
        if semname is None:
            semname = self.semname = {}
        for e_ in ENGS:
            semname[id(self.esem[e_])] = "s_%s_%d" % (e_, id(self.esem[e_]))
        for q_ in self.dsems:
            for i_, sm in enumerate(self.dsems[q_]):
                semname[id(sm)] = "d_%s%d" % (q_, i_)

        def run(e, engname):
            waited = {}
            tr = trace[engname]

            def wait(sem, val):
                k = id(sem)
                if waited.get(k, 0) >= val:
                    return
                waited[k] = val
                tr.append(("wait", semname[k], val))
                e.wait_ge(sem, val)

            for e2 in ENGS:
                if e2 != engname and bar_e[e2] > 0:
                    wait(bar_sem[e2], bar_e[e2])
            for q in dsems:
                for i in range(self.ndsem):
                    if bar_d[q][i] > 0:
                        wait(dsems[q][i], bar_d[q][i])
            for op in per[engname]:
                for d in sorted(op.deps):
                    p = ops[d]
                    if p.dma:
                        wait(p.dsem, p.dval)
                    elif d >= lo:
                        if p.eng == "tensor" and engname == "tensor" and not op.dma:
                            continue
                        wait(esem[p.eng], p.cnt)
                if op.dma:
                    if op.dprev > 0:
                        wait(op.dsem, op.dprev)
                    op.fn(e).then_inc(op.dsem, 16)
                    tr.append(("inc", semname[id(op.dsem)], 16))
                else:
                    ins = op.fn(e)
                    if op.prod:
                        ins.then_inc(esem[engname], 1)
                        tr.append(("inc", semname[id(esem[engname])], 1))
            if final and engname == "sync":
                for q in dsems:
                    for i in range(self.ndsem):
                        if self.dcount[q][i] > 0:
                            e.wait_ge(dsems[q][i], self.dcount[q][i])
                for e2 in ENGS:
                    if e2 != "sync" and self.ecnt[e2] > 0:
                        e.wait_ge(esem[e2], self.ecnt[e2])

        with nc.Block() as block:
            @block.tensor
            def _(e):
                run(e, "tensor")

            @block.vector
            def _(e):
                run(e, "vector")

            @block.scalar
            def _(e):
                run(e, "scalar")

            @block.gpsimd
            def _(e):
                run(e, "gpsimd")

            @block.sync
            def _(e):
                run(e, "sync")


class Ring:
    def __init__(self, tiles):
        self.tiles = tiles
        self.res = [Res() for _ in tiles]
        self.i = 0

    def next(self):
        k = self.i % len(self.tiles)
        self.i += 1
        return self.tiles[k], self.res[k]


class Ctx:
    def __init__(self, nc, stack):
        self.nc = nc
        self.stack = stack
        self.P = Prog(nc)
        self.n = 0

    _uid = [0]

    def sb(self, shape, dt, name=None):
        Ctx._uid[0] += 1
        return self.stack.enter_context(self.nc.sbuf_tensor("%s_%d" % (name or "sb", Ctx._uid[0]), list(shape), dt))

    def ps(self, shape, dt=F32, name=None):
        Ctx._uid[0] += 1
        return self.stack.enter_context(self.nc.psum_tensor("%s_%d" % (name or "ps", Ctx._uid[0]), list(shape), dt))

    def ring_sb(self, n, shape, dt, name=None):
        return Ring([self.sb(shape, dt, name) for _ in range(n)])

    def ring_ps(self, n, shape, dt=F32, name=None):
        return Ring([self.ps(shape, dt, name) for _ in range(n)])

    def dram(self, name, shape, dt):
        return self.nc.dram_tensor(name, list(shape), dt).ap()


TOK_TILES = [(0, 256), (256, 512), (768, 512), (1280, 512), (1792, 512)]
NORM_TILES = [(c, 256) for c in range(0, 2304, 256)]


class Builder:
    def __init__(self, nc, stack, debug=()):
        self.nc = nc
        self.C = Ctx(nc, stack)
        self.P = self.C.P
        self.P.setup(stack)
        self.debug = debug
        self.inp = {}
        self.dres = {}

    def din(self, name, shape, dt=F32):
        if any(x in self.debug for x in ("only_gdn", "only_ml", "only_attn", "only_sg")) and int(np.prod(shape)) > 4000000:
            shape = [1] * len(shape)
        ap = self.nc.dram_tensor(name, list(shape), dt, kind="ExternalInput").ap()
        self.inp[name] = ap
        return ap

    def dscr(self, name, shape, dt=F32):
        kind = "ExternalOutput" if name in self.debug else "Internal"
        return self.nc.dram_tensor(name, list(shape), dt, kind=kind).ap()

    def R(self, key):
        r = self.dres.get(key)
        if r is None:
            r = self.dres[key] = Res(str(key))
        return r

    def phase(self):
        b = self

        class Ph:
            def __enter__(s):
                s.st = ExitStack()
                s.st.__enter__()
                s.C = Ctx(b.nc, s.st)
                s.C.P = b.P
                return s.C

            def __exit__(s, *a):
                if a[0] is None:
                    b.P.emit()
                return s.st.__exit__(*a)

        return Ph()


def build_program(debug=(), stop_after=None, nlayers=DEPTH):
    nc = bass.Bass("TRN2", target_bir_lowering=False)
    top = ExitStack()
    with top:
        B = Builder(nc, top, debug)
        P = B.P
        R = B.R
        xT_in = B.din("xT", [D, TOK])
        cvec = B.din("cvec", [128, KC, 2])
        mod_w = B.din("mod_w", [DEPTH, D, 9 * D])
        mod_b = B.din("mod_b", [DEPTH, 128, 144])
        norms = B.din("norms", [DEPTH, 128, 3, KC])
        ffn_w_in = [B.din("ffn1_w_in", [DEPTH, D, 2 * DFF]), B.din("ffn2_w_in", [DEPTH, D, 2 * DFF])]
        ffn_w_out = [B.din("ffn1_w_out", [DEPTH, DFF, D]), B.din("ffn2_w_out", [DEPTH, DFF, D])]
        w_in_all = B.din("w_in", [DEPTH, D, IN_COLS])
        w_branch = B.din("w_branch", [DEPTH, 4, 512, D])
        w_out_all = B.din("w_out", [DEPTH, D, D])
        sgwT = B.din("sgwT", [DEPTH, 4, 128, 128])
        sg_b = B.din("sg_b", [DEPTH, 4, 128])
        vecs = B.din("vecs", [128, DEPTH, 12])
        mlb_in = B.din("mlb", [128, DEPTH, 16])
        gdc_in = B.din("gdc", [128, DEPTH, 16])
        cw_in = B.din("cw", [128, DEPTH, 3, 12])
        lm_in = B.din("lm", [64, 7, 64])
        consts = B.din("consts", [128, 4, 128])
        ropetab = B.din("ropetab", [128, 2, T_LAT])
        outT = nc.dram_tensor("outT", [D, T_LAT], F32, kind="ExternalOutput").ap()
        Xs = [B.dscr("X%d" % i, [D, TOK]) for i in range(3)]
        GT = B.dscr("GT", [DFF, TOK], BF16)
        if any(x in debug for x in ("only_gdn", "only_ml", "only_attn", "only_sg")):
            ZT = nc.dram_tensor("ZT", [IN_COLS, TOK], F32, kind="ExternalInput").ap()
        else:
            ZT = B.dscr("ZT", [IN_COLS, TOK])
        YT = B.dscr("YT", [4, 512, TOK], BF16)

        PC = B.C
        ones_f = PC.sb([128, 128], F32, "ones_f"); r_const = Res()
        modT = PC.sb([128, 144, 2], F32, "modT"); r_mod = Res()
        Avec = PC.sb([128, 3, KC, 2], F32, "Avec")
        Hg = PC.sb([128, 3, KC, 2], F32, "Hg")
        nrm = PC.sb([128, DEPTH, 3, KC], F32, "nrm")
        epst = PC.sb([128, 1], F32, "epst")
        ones_b = PC.sb([128, 128], BF16, "ones_b")
        cst = PC.sb([128, 4, 128], F32, "cst")
        mlb = PC.sb([128, DEPTH, 16], F32, "mlb")
        gdc = PC.sb([128, DEPTH, 16], F32, "gdc")
        cw = PC.sb([128, DEPTH, 3, 12], F32, "cw")
        lm = PC.sb([64, 7, 64], F32, "lm")
        ident_b = PC.sb([128, 128], BF16, "ident_b")
        ident_f = cst[:, 0, :]
        pm_f = cst[:, 1, :]
        ROPE = {}
        vec_t = PC.sb([128, DEPTH, 12], F32, "vec_t")
        atn = vec_t
        sgn = vec_t[:, :, 4:8]

        with B.phase() as C:
            P.add("vector", lambda e: e.memset(ones_f[:], 1.0), [], [r_const])
            P.add("vector", lambda e: e.memset(epst[:], EPS), [], [r_const])
            P.dma("sync", nrm[:], norms.rearrange("l p i k -> p l i k"), [], [r_const])
            P.dma("sync", cst[:], consts, [], [r_const])
            P.dma("sync", vec_t[:], vecs, [], [r_const])
            P.dma("sync", mlb[:], mlb_in, [], [r_const])
            P.dma("sync", gdc[:], gdc_in, [], [r_const])
            P.dma("sync", cw[:], cw_in, [], [r_const])
            P.dma("sync", lm[:], lm_in, [], [r_const])
            P.add("vector", lambda e: e.memset(ones_b[:], 1.0), [], [r_const])
            P.add("vector", lambda e: e.tensor_copy(ident_b[:], cst[:, 0, :]), [r_const], [r_const])

        def mod_phase(l):
            with B.phase() as C:
                sc = C.sb([128, KC, 2], F32, "sc"); r_sc = Res()
                mb = C.sb([128, 144], F32, "mb"); r_mb = Res()
                wr = C.ring_sb(2, [128, KC, 512], F32, "modw")
                pr = C.ring_ps(2, [128, 2], F32, "modps")
                P.dma("sync", sc[:], cvec, [], [r_sc])
                P.dma("sync", mb[:], mod_b[l], [], [r_mb])
                P.act("scalar", sc[:], sc[:], AF.Silu, [r_sc], [r_sc])
                for blk in range(36):
                    wt, rw = wr.next()
                    P.dma("sync" if blk % 2 == 0 else "scalar", wt[:],
                          mod_w[l, :, blk * 512:(blk + 1) * 512].rearrange("(kc p) n -> p kc n", p=128), [], [rw])
                    for s in range(4):
                        ps, rp = pr.next()
                        for kc in range(KC):
                            P.mm(ps[:], wt[:, kc, s * 128:(s + 1) * 128], sc[:, kc, :], kc == 0, kc == KC - 1,
                                 [rw, r_sc], [rp])
                        ch = blk * 4 + s
                        P.add("vector", lambda e, ps=ps, ch=ch: e.tensor_scalar(
                            modT[:, ch, :], ps[:], mb[:, ch:ch + 1], None, ALU.add), [rp, r_mb], [r_mod])
                for i in range(3):
                    P.add("vector", lambda e, i=i: e.tensor_scalar(
                        Avec[:, i, :, :], modT[:, (3 * i + 1) * KC:(3 * i + 2) * KC, :], 1.0, None, ALU.add),
                        [r_mod], [r_mod])
                    for j in range(2):
                        P.add("vector", lambda e, i=i, j=j: e.tensor_tensor(
                            Avec[:, i, :, j], Avec[:, i, :, j], nrm[:, l, i, :], ALU.mult), [r_mod, r_const], [r_mod])
                    P.add("vector", lambda e, i=i: e.tensor_scalar(
                        Hg[:, i, :, :], modT[:, (3 * i + 2) * KC:(3 * i + 3) * KC, :], 0.5 if i != 1 else 1.0, None,
                        ALU.mult), [r_mod], [r_mod])

        def norm_tiles(C, Xsrc, xkey, i, xn, r_xn, tiles=NORM_TILES):
            xr = C.ring_sb(2, [128, KC, 256], F32, "xt")
            sq = C.ring_sb(3, [128, 512], F32, "sq")
            tmp = C.ring_sb(3, [128, 512], F32, "ntmp")
            rs_ = C.ring_sb(2, [128, 512], F32, "rstd")
            pss = C.ring_ps(2, [128, 512], F32, "ssq")
            for ti, (c0, w) in enumerate(tiles):
                j = 1 if c0 < T_CTX else 0
                xt, rx = xr.next()
                P.dma("sync", xt[:, :, :w], Xsrc[:, c0:c0 + w].rearrange("(kc p) n -> p kc n", p=128),
                      [R((xkey, kc_)) for kc_ in range(KC)], [rx])
                ps, rp = pss.next()
                for kc in range(KC):
                    s, rsq = sq.next()
                    P.act("scalar", s[:, :w], xt[:, kc, :w], AF.Square, [rx], [rsq])
                    P.mm(ps[:, :w], ones_f[:], s[:, :w], kc == 0, kc == KC - 1, [r_const, rsq], [rp])
                rstd, rr = rs_.next()
                P.act("scalar", rstd[:, :w], ps[:, :w], AF.Sqrt, [rp, r_const], [rr], bias=epst[:, 0:1], scale=1.0 / D)
                P.add("vector", lambda e, rstd=rstd, w=w: e.reciprocal(rstd[:, :w], rstd[:, :w]), [rr], [rr])
                for kc in range(KC):
                    t, rt = tmp.next()
                    P.tt("vector", t[:, :w], xt[:, kc, :w], rstd[:, :w], ALU.mult, [rx, rr], [rt])
                    if kc % 2:
                        P.ts("vector", xn[:, kc, c0:c0 + w], t[:, :w], Avec[:, i, kc, j:j + 1], modT[:, 3 * i * KC + kc, j:j + 1],
                             ALU.mult, ALU.add, [rt, r_mod], [r_xn])
                    else:
                        P.act("scalar", xn[:, kc, c0:c0 + w], t[:, :w], AF.Identity, [rt, r_mod], [r_xn],
                              bias=modT[:, 3 * i * KC + kc, j:j + 1], scale=Avec[:, i, kc, j:j + 1])

        def ffn(l, f, Xsrc, xkey, Xdst, dkey, final_out=False):
            i = 0 if f == 0 else 2
            w_in = ffn_w_in[f]
            w_out = ffn_w_out[f]
            with B.phase() as C:
                xn = C.sb([128, KC, TOK], BF16, "xn"); r_xn = Res()
                ftiles = TOK_TILES[1:] if final_out else TOK_TILES
                norm_tiles(C, Xsrc, xkey, i, xn, r_xn, tiles=(NORM_TILES[1:] if final_out else NORM_TILES))
                war = C.ring_sb(2, [128, KC, 256], BF16, "wa")
                wbr = C.ring_sb(2, [128, KC, 256], BF16, "wb")
                pa = C.ring_ps(2, [128, 512], F32, "pa")
                pb = C.ring_ps(2, [128, 512], F32, "pb")
                sar = C.ring_sb(3, [128, 512], F32, "sa")
                gst = C.ring_sb(2, [128, TOK], BF16, "gst")
                for blk in range(DFF // 256):
                    wa, rwa = war.next()
                    wb, rwb = wbr.next()
                    P.dma("gpsimd", wa[:], w_in[l, :, blk * 256:(blk + 1) * 256].rearrange("(kc p) n -> p kc n", p=128),
                          [], [rwa])
                    P.dma("gpsimd", wb[:], w_in[l, :, DFF + blk * 256:DFF + (blk + 1) * 256].rearrange(
                        "(kc p) n -> p kc n", p=128), [], [rwb])
                    for s_ in range(2):
                        g, rg = gst.next()
                        for (c0, w) in ftiles:
                            a, ra = pa.next()
                            b_, rb = pb.next()
                            for kc in range(KC):
                                P.mm(a[:, :w], wa[:, kc, s_ * 128:(s_ + 1) * 128], xn[:, kc, c0:c0 + w], kc == 0,
                                     kc == KC - 1, [rwa, r_xn], [ra])
                            for kc in range(KC):
                                P.mm(b_[:, :w], wb[:, kc, s_ * 128:(s_ + 1) * 128], xn[:, kc, c0:c0 + w], kc == 0,
                                     kc == KC - 1, [rwb, r_xn], [rb])
                            sa, rsa = sar.next()
                            P.act("scalar", sa[:, :w], a[:, :w], AF.Silu, [ra], [rsa])
                            P.add("vector", lambda e, g=g, sa=sa, b_=b_, c0=c0, w=w: e.tensor_tensor(
                                g[:, c0:c0 + w], sa[:, :w], b_[:, :w], ALU.mult), [rsa, rb], [rg])
                        row = blk * 256 + s_ * 128
                        P.dma("sync", GT[row:row + 128, :], g[:], [rg], [R(("GT", row // 128))])
            groups = [[(0, 256), (256, 512), (768, 512)], [(1280, 512), (1792, 512)]]
            if final_out:
                groups = [[(256, 512), (768, 512)], [(1280, 512), (1792, 512)]]
            with B.phase() as C:
                NK = DFF // 128
                gt = C.sb([128, NK, 1280], BF16, "gt"); r_gt = Res()
                wor = C.ring_sb(3, [128, NK, 256], BF16, "wo")
                py = C.ring_ps(3, [128, 512], F32, "py")
                xrr = C.ring_sb(3, [128, 512], F32, "xres")
                orr = C.ring_sb(3, [128, 512], F32, "ores")
                for grp in groups:
                    g0 = grp[0][0]
                    gw = sum(w for _, w in grp)
                    P.dma("sync", gt[:, :, :gw], GT[:, g0:g0 + gw].rearrange("(k p) n -> p k n", p=128),
                          [R(("GT", k)) for k in range(NK)], [r_gt])
                    for nb in range(D // 256):
                        wo, rwo = wor.next()
                        P.dma("gpsimd", wo[:], w_out[l, :, nb * 256:(nb + 1) * 256].rearrange("(k p) n -> p k n", p=128),
                              [], [rwo])
                        for s_ in range(2):
                            fc = nb * 2 + s_
                            for (c0, w) in grp:
                                j = 1 if c0 < T_CTX else 0
                                y, ry = py.next()
                                for k in range(NK):
                                    P.mm(y[:, :w], wo[:, k, s_ * 128:(s_ + 1) * 128], gt[:, k, c0 - g0:c0 - g0 + w],
                                         k == 0, k == NK - 1, [rwo, r_gt], [ry])
                                xr_, rxr = xrr.next()
                                P.dma("scalar", xr_[:, :w], Xsrc[fc * 128:(fc + 1) * 128, c0:c0 + w],
                                      [R((xkey, fc))], [rxr])
                                o, ro = orr.next()
                                P.add("vector", lambda e, o=o, y=y, xr_=xr_, fc=fc, j=j, w=w: e.scalar_tensor_tensor(
                                    o[:, :w], y[:, :w], Hg[:, i, fc, j:j + 1], xr_[:, :w], ALU.mult, ALU.add),
                                    [ry, rxr, r_mod], [ro])
                                if final_out:
                                    if c0 >= T_CTX:
                                        P.dma("sync", outT[fc * 128:(fc + 1) * 128, c0 - T_CTX:c0 - T_CTX + w], o[:, :w],
                                              [ro], [R(("out", fc))])
                                else:
                                    P.dma("sync", Xdst[fc * 128:(fc + 1) * 128, c0:c0 + w], o[:, :w],
                                          [ro], [R((dkey, fc))])

        def inproj(l, Xsrc, xkey):
            with B.phase() as C:
                xn = C.sb([128, KC, TOK], BF16, "xn"); r_xn = Res()
                norm_tiles(C, Xsrc, xkey, 1, xn, r_xn)
                wr = C.ring_sb(2, [128, KC, 256], BF16, "wi")
                pz = C.ring_ps(4, [128, 512], F32, "pz")
                zst = C.ring_sb(2, [128, TOK], F32, "zst")
                cnt = 0
                for c0 in range(0, IN_COLS, 256):
                    ncol = min(256, IN_COLS - c0)
                    wt, rw = wr.next()
                    P.dma("gpsimd", wt[:, :, :ncol], w_in_all[l, :, c0:c0 + ncol].rearrange("(kc p) n -> p kc n", p=128),
                          [], [rw])
                    for s0 in range(0, ncol, 128):
                        m = min(128, ncol - s0)
                        z, rz = zst.next()
                        for (t0, w) in TOK_TILES:
                            ps, rp = pz.next()
                            for kc in range(KC):
                                P.mm(ps[:m, :w], wt[:, kc, s0:s0 + m], xn[:, kc, t0:t0 + w], kc == 0, kc == KC - 1,
                                     [rw, r_xn], [rp])
                            cnt += 1
                            if cnt % 2:
                                P.act("scalar", z[:m, t0:t0 + w], ps[:m, :w], AF.Copy, [rp], [rz])
                            else:
                                P.add("vector", lambda e, z=z, ps=ps, m=m, t0=t0, w=w: e.tensor_copy(
                                    z[:m, t0:t0 + w], ps[:m, :w]), [rp], [rz])
                        row = c0 + s0
                        P.dma("sync", ZT[row:row + m, :], z[:m, :], [rz], [R("ZT")])

        KNRES = {}

        def qk_norm_rope(C, rows0, gain_ap, dst, r_dst, rings):
            zr, sqr, p6, rsr, knr, t1r = rings
            zt, rz = zr.next()
            P.dma("sync", zt[:], ZT[rows0:rows0 + 128, :], [R("ZT")], [rz])
            kn, _ = knr.next()

            def tile(t0, w):
                rkn = KNRES.setdefault((kn.name, t0), Res())
                s, rsq = sqr.next()
                P.act("scalar", s[:, :w], zt[:, t0:t0 + w], AF.Square, [rz], [rsq])
                yield
                ps, rp = p6.next()
                P.mm(ps[:, :w], ones_f[:], s[:, :w], True, True, [r_const, rsq], [rp])
                yield
                rstd, rr = rsr.next()
                P.act("scalar", rstd[:, :w], ps[:, :w], AF.Sqrt, [rp, r_const], [rr], bias=epst[:, 0:1], scale=1.0 / 128)
                yield
                P.add("vector", lambda e: e.reciprocal(rstd[:, :w], rstd[:, :w]), [rr], [rr])
                yield
                P.stt("vector", kn[:, t0:t0 + w], zt[:, t0:t0 + w], gain_ap, rstd[:, :w], ALU.mult, ALU.mult,
                      [rz, rr, r_const], [rkn])
                yield
                if t0 < T_CTX:
                    P.cp("vector", dst[:, t0:t0 + w], kn[:, t0:t0 + w], [rkn], [r_dst])
                else:
                    pr, rpr = p6.next()
                    P.mm(pr[:, :w], pm_f[:], kn[:, t0:t0 + w], True, True, [r_const, rkn], [rpr])
                    t1, rt1 = t1r.next()
                    l0 = t0 - T_CTX
                    P.tt("gpsimd", t1[:, :w], kn[:, t0:t0 + w], ROPE['cos'][:, l0:l0 + w], ALU.mult, [rkn, ROPE['r']], [rt1])
                    yield
                    t2, rt2 = t1r.next()
                    P.tt("vector", t2[:, :w], pr[:, :w], ROPE['sin'][:, l0:l0 + w], ALU.mult, [rpr, ROPE['r']], [rt2])
                    yield
                    P.tt("vector", dst[:, t0:t0 + w], t1[:, :w], t2[:, :w], ALU.add, [rt1, rt2], [r_dst])
                yield

            lockstep([tile(t0, w) for (t0, w) in TOK_TILES])

        def attn_phase(l):
            ZK, ZV, ZQ = 2080, 2336, 5664
            with B.phase() as C:
                rtab = C.sb([128, 2, T_LAT], F32, "rtab")
                ROPE['cos'] = rtab[:, 0, :]; ROPE['sin'] = rtab[:, 1, :]; ROPE['r'] = Res()
                P.dma("sync", rtab[:], ropetab, [], [ROPE['r']])
                kT = [C.sb([128, TOK], BF16, "kT") for _ in range(2)]; r_k = [Res(), Res()]
                qT = [C.sb([128, TOK], BF16, "qT") for _ in range(4)]; r_q = [Res() for _ in range(4)]
                vtm = [C.sb([128, 18, 128], BF16, "vtm") for _ in range(2)]; r_v = [Res(), Res()]
                p6 = C.ring_ps(6, [128, 512], F32, "p6")
                psr = Ring(p6.tiles[2:4]); psr.res = p6.res[2:4]
                po, r_po = p6.tiles[4], p6.res[4]
                pd, r_pd = p6.tiles[5], p6.res[5]
                rings = (C.ring_sb(2, [128, TOK], F32, "zr"), C.ring_sb(5, [128, 512], F32, "sq"),
                         p6, C.ring_sb(5, [128, 512], F32, "rs"),
                         C.ring_sb(2, [128, TOK], F32, "kn"),
                         C.ring_sb(10, [128, 512], F32, "t1"))
                for hk in range(2):
                    qk_norm_rope(C, ZK + hk * 128, atn[:, l, 1:2], kT[hk], r_k[hk], rings)
                for h in range(4):
                    qk_norm_rope(C, ZQ + h * 128, atn[:, l, 0:1], qT[h], r_q[h], rings)
                vb = C.ring_sb(2, [128, TOK], BF16, "vb")
                ptr = C.ring_ps(2, [128, 128], BF16, "ptr")
                for hk in range(2):
                    zt, rz = rings[0].next()
                    P.dma("sync", zt[:], ZT[ZV + hk * 128:ZV + (hk + 1) * 128, :], [R("ZT")], [rz])
                    v, rv = vb.next()
                    P.add("vector", lambda e, v=v, zt=zt: e.tensor_copy(v[:], zt[:]), [rz], [rv])
                    for kc in range(18):
                        pt, rpt = ptr.next()
                        P.tr(pt[:], v[:, kc * 128:(kc + 1) * 128], ident_b[:], [rv, r_const], [rpt])
                        P.add("vector", lambda e, pt=pt, hk=hk, kc=kc: e.tensor_copy(
                            vtm[hk][:, kc, :], pt[:]), [rpt], [r_v[hk]])
                er = C.ring_sb(3, [128, 512], BF16, "e")
                rdr = C.ring_sb(2, [128, 512], F32, "rden")
                yst = C.ring_sb(2, [128, 512], BF16, "yst")
                for h in range(4):
                    hk = h // 2
                    for (t0, w) in TOK_TILES:
                        nkc = 2 if t0 < T_CTX else 18
                        for kc in range(nkc):
                            ps, rp = psr.next()
                            P.mm(ps[:, :w], kT[hk][:, kc * 128:(kc + 1) * 128], qT[h][:, t0:t0 + w], True, True,
                                 [r_k[hk], r_q[h]], [rp])
                            e_, re_ = er.next()
                            P.act("scalar", e_[:, :w], ps[:, :w], AF.Exp, [rp], [re_], scale=128 ** -0.5)
                            P.mm(po[:, :w], vtm[hk][:, kc, :], e_[:, :w], kc == 0, kc == nkc - 1, [r_v[hk], re_], [r_po])
                            P.mm(pd[:, :w], ones_b[:], e_[:, :w], kc == 0, kc == nkc - 1, [r_const, re_], [r_pd])
                        rd, rrd = rdr.next()
                        P.add("vector", lambda e, rd=rd, w=w: e.reciprocal(rd[:, :w], pd[:, :w]), [r_pd], [rrd])
                        y, ry = yst.next()
                        P.add("vector", lambda e, y=y, rd=rd, w=w: e.tensor_tensor(y[:, :w], po[:, :w], rd[:, :w], ALU.mult),
                              [r_po, rrd], [ry])
                        P.dma("sync", YT[3, h * 128:(h + 1) * 128, t0:t0 + w], y[:, :w], [ry], [R(("YT", 3))])

        def gelu_tiles(C, src, r_src, dst, r_dst, rings):
            g1 = rings
            for (t0, w) in TOK_TILES:
                a, ra = g1.next()
                P.act("scalar", a[:, :w], src[:, t0:t0 + w], AF.Square, [r_src], [ra])
                P.add("vector", lambda e, a=a, w=w: e.tensor_scalar(a[:, :w], a[:, :w], 0.044715, 1.0, ALU.mult, ALU.add),
                      [ra], [ra])
                P.add("vector", lambda e, a=a, w=w, t0=t0: e.tensor_tensor(a[:, :w], a[:, :w], src[:, t0:t0 + w], ALU.mult),
                      [ra, r_src], [ra])
                P.act("scalar", a[:, :w], a[:, :w], AF.Sigmoid, [ra], [ra], scale=1.5957691216057308)
                P.add("vector", lambda e, a=a, w=w, t0=t0: e.tensor_tensor(dst[:, t0:t0 + w], a[:, :w], src[:, t0:t0 + w],
                                                                      ALU.mult), [ra, r_src], [r_dst])

        def sg_phase(l):
            ZU, ZVV = 2592, 3104
            with B.phase() as C:
                g1 = C.ring_sb(3, [128, 512], F32, "g1")
                zr = C.ring_sb(2, [128, TOK], F32, "zr")
                gu = [C.sb([128, TOK], F32, "gu") for _ in range(4)]; r_gu = [Res() for _ in range(4)]
                gv = [C.sb([128, TOK], F32, "gv") for _ in range(4)]; r_gv = [Res() for _ in range(4)]
                wst = C.sb([128, 4, 128], BF16, "wst"); r_w = Res()
                bs = C.sb([1, 4, 128], BF16, "bs"); r_b = Res()
                P.dma("gpsimd", wst[:], sgwT[l].rearrange("g s t -> s g t"), [], [r_w])
                P.dma("gpsimd", bs[:], sg_b[l:l + 1], [], [r_b])
                for g in range(4):
                    zt, rz = zr.next()
                    P.dma("sync", zt[:], ZT[ZU + g * 128:ZU + (g + 1) * 128, :], [R("ZT")], [rz])
                    gelu_tiles(C, zt, rz, gu[g], r_gu[g], g1)
                    zt, rz = zr.next()
                    P.dma("sync", zt[:], ZT[ZVV + g * 128:ZVV + (g + 1) * 128, :], [R("ZT")], [rz])
                    gelu_tiles(C, zt, rz, gv[g], r_gv[g], g1)
                rstd = C.sb([128, TOK], F32, "rstd"); r_rs = Res()
                pss = C.ring_ps(2, [128, 512], F32, "pss")
                for (t0, w) in TOK_TILES:
                    ps, rp = pss.next()
                    for g in range(4):
                        a, ra = g1.next()
                        P.act("scalar", a[:, :w], gv[g][:, t0:t0 + w], AF.Square, [r_gv[g]], [ra])
                        P.mm(ps[:, :w], ones_f[:], a[:, :w], g == 0, g == 3, [r_const, ra], [rp])
                    P.act("scalar", rstd[:, t0:t0 + w], ps[:, :w], AF.Sqrt, [rp, r_const], [r_rs], bias=epst[:, 0:1],
                          scale=1.0 / 512)
                    P.add("vector", lambda e, t0=t0, w=w: e.reciprocal(rstd[:, t0:t0 + w], rstd[:, t0:t0 + w]), [r_rs], [r_rs])
                vnr = C.ring_sb(2, [128, TOK], BF16, "vn")
                ptr = C.ring_ps(3, [128, 128], BF16, "ptr")
                vtr = C.ring_sb(3, [128, 128], BF16, "vt")
                pso = C.ring_ps(3, [128, 128], F32, "pso")
                ysr = C.ring_sb(2, [128, TOK], BF16, "ys")
                for g in range(4):
                    vn, rvn = vnr.next()
                    P.stt("vector", vn[:], gv[g][:], sgn[:, l, g:g + 1], rstd[:], ALU.mult, ALU.mult, [r_gv[g], r_rs, r_const], [rvn])
                    ys, rys = ysr.next()

                    def sgu(g, n, vn, rvn, ys, rys):
                        pt, rpt = ptr.next()
                        P.tr(pt[:], vn[:, n * 128:(n + 1) * 128], ident_b[:], [rvn, r_const], [rpt])
                        yield
                        vt, rvt = vtr.next()
                        P.cp("vector", vt[:], pt[:], [rpt], [rvt])
                        yield
                        po_, rpo = pso.next()
                        P.mm(po_[:], vt[:], wst[:, g, :], True, False, [rvt, r_w], [rpo])
                        P.mm(po_[:], ones_b[0:1, :], bs[0:1, g, :], False, True, [r_const, r_b], [rpo])
                        yield
                        P.tt("vector", ys[:, n * 128:(n + 1) * 128], po_[:], gu[g][:, n * 128:(n + 1) * 128], ALU.mult,
                             [rpo, r_gu[g]], [rys])
                        yield
                    for n0 in range(0, 18, 3):
                        lockstep([sgu(g, n, vn, rvn, ys, rys) for n in range(n0, n0 + 3)])
                    P.dma("sync", YT[0, g * 128:(g + 1) * 128, :], ys[:], [rys], [R(("YT", 0))])

        def lockstep(gens):
            gens = list(gens)
            while gens:
                alive = []
                for g_ in gens:
                    try:
                        next(g_)
                        alive.append(g_)
                    except StopIteration:
                        pass
                gens = alive

        def mlstm_phase(l):
            ZK, ZV_, ZIF, ZQ, ZO = 0, 512, 1024, 3616, 4128
            with B.phase() as C:
                prs = [C.ring_ps(2, [128, 512], F32, "mp") for _ in range(4)]
                pr = prs[0]
                zr = C.ring_sb(2, [128, TOK], F32, "zr")
                ift_t, r_ift = zr.next()
                P.dma("sync", ift_t[0:16, :], ZT[ZIF:ZIF + 16, :], [R("ZT")], [r_ift])
                if_tm = C.sb([128, 18, 16], F32, "if_tm"); r_if = Res()
                lf_tm = C.sb([128, 18, 8], F32, "lf_tm"); r_lf = Res()
                b_tm = C.sb([128, 18, 8], F32, "b_tm"); r_b = Res()
                w_tm = C.sb([128, 18, 8], F32, "w_tm"); r_w = Res()
                for n in range(18):
                    pt, rpt = pr.next()
                    P.mm(pt[:, 0:16], ift_t[0:16, n * 128:(n + 1) * 128], cst[0:16, 0, 0:16], True, True, [r_ift, r_const], [rpt])
                    P.tt("vector", if_tm[:, n, :], pt[:, 0:16], mlb[:, l, :], ALU.add, [rpt, r_const], [r_if])
                P.act("scalar", lf_tm[:], if_tm[:, :, 8:16], AF.Exp, [r_if], [r_lf], scale=-1.0)
                P.act("scalar", lf_tm[:], lf_tm[:], AF.Ln, [r_lf, r_const], [r_lf], bias=ones_f[:, 0:1])
                P.ts("vector", lf_tm[:], lf_tm[:], -1.0, None, ALU.mult, None, [r_lf], [r_lf])
                for n in range(18):
                    pt, rpt = pr.next()
                    P.mm(pt[:, 0:4], cst[:, 3, :], lf_tm[:, n, 0:4], True, True, [r_const, r_lf], [rpt])
                    P.mm(pt[:, 4:8], cst[:, 2, :], lf_tm[:, n, 4:8], True, True, [r_const, r_lf], [rpt])
                    P.cp("vector", b_tm[:, n, :], pt[:, 0:8], [rpt], [r_b])
                P.tt("vector", w_tm[:], if_tm[:, :, 0:8], b_tm[:], ALU.subtract, [r_if, r_b], [r_w])
                P.act("scalar", w_tm[:], w_tm[:], AF.Exp, [r_w], [r_w])
                kTs = [C.sb([128, TOK], BF16, "kT") for _ in range(2)]; r_kT = [Res(), Res()]
                qTs = [C.sb([128, TOK], BF16, "qT") for _ in range(2)]; r_qT = [Res(), Res()]
                ktms = [C.sb([128, 18, 128], BF16, "ktm") for _ in range(2)]; r_ktm = [Res(), Res()]
                vtms = [C.sb([128, 18, 128], BF16, "vtm") for _ in range(2)]; r_vtm = [Res(), Res()]
                hTs = [C.sb([128, 2, TOK], BF16, "hT") for _ in range(2)]; r_hT = [Res(), Res()]
                hsum = C.ring_sb(1, [128, TOK], F32, "hsum")
                sqr = C.ring_sb(2, [128, 512], F32, "sq")
                rsr = C.ring_sb(2, [128, 512], F32, "rs")
                ysr = C.ring_sb(1, [128, TOK], BF16, "ys")

                class T_:
                    pass
                chains_t = []
                for ci in range(4):
                    t = T_()
                    mk = lambda name, shape, dt, n=2: C.ring_sb(n, shape, dt, name)
                    t.lfr = mk("lfrep", [128, 128], F32); t.er = mk("erep", [128, 128], F32)
                    t.ptr = mk("ptm", [128, 128], BF16); t.vpr = mk("vp", [128, 129], BF16)
                    t.wrr = mk("wrep", [128, 128], BF16); t.t1r = mk("t1", [128, 128], F32); t.t2r = mk("t2", [128, 128], F32)
                    t.CTa = C.sb([128, 129], F32, "CTa"); t.r_CT = Res()
                    t.CTb = C.sb([128, 128], BF16, "CTb"); t.r_CTb = Res()
                    t.Nrep = C.sb([128, 128], BF16, "Nrep"); t.r_N = Res()
                    t.pr = prs[ci]
                    chains_t.append(t)
                orders = [list(range(18)), [1, 0] + list(range(17, 1, -1))]

                def unit(t, hi, h, d_, c):
                    kT, rk = kTs[hi], r_kT[hi]
                    qT, rq = qTs[hi], r_qT[hi]
                    ktm, rktm = ktms[hi], r_ktm[hi]
                    vtm, rvtm = vtms[hi], r_vtm[hi]
                    hT, rh = hTs[hi], r_hT[hi]
                    pr_ = t.pr
                    cs = slice(c * 128, (c + 1) * 128)
                    col = d_ * 4 + h
                    U = cst[:, 3, :] if d_ == 0 else cst[:, 2, :]
                    wcol = w_tm[:, c, col:col + 1]
                    lfp, rlfp = t.lfr.next()
                    P.act("scalar", lfp[:], ones_f[:], AF.Identity, [r_const, r_lf], [rlfp], scale=lf_tm[:, c, col:col + 1])
                    vp, rvp = t.vpr.next()
                    P.ts("vector", vp[:, 0:128], vtm[:, c, :], wcol, None, ALU.mult, None, [rvtm, r_w], [rvp])
                    P.cp("gpsimd", vp[:, 128:129], wcol, [r_w], [rvp])
                    wrep, rwr = t.wrr.next()
                    P.act("scalar", wrep[:], ones_f[:], AF.Identity, [r_const, r_w], [rwr], scale=wcol)
                    pB, rpB = pr_.next()
                    P.mm(pB[:, 0:128], kT[:, cs], qT[:, cs], True, True, [rk, rq], [rpB])
                    yield
                    pA, rpA = pr_.next()
                    P.mm(pA[:, 0:128], lfp[:], U, True, True, [rlfp, r_const], [rpA])
                    ptm, rptm = t.ptr.next()
                    P.tt("vector", ptm[:], pB[:, 0:128], U, ALU.mult, [rpB, r_const], [rptm])
                    yield
                    erep, rer = t.er.next()
                    P.act("scalar", erep[:], pA[:, 0:128], AF.Exp, [rpA], [rer])
                    eb = erep[:, 127:128] if d_ == 0 else erep[:, 0:1]
                    pC, rpC = pr_.next()
                    P.mm(pC[:, 0:128], vp[:, 0:128], ptm[:], True, False, [rvp, rptm], [rpC])
                    P.mm(pC[:, 0:128], t.CTb[:], qT[:, cs], False, True, [t.r_CTb, rq], [rpC])
                    yield
                    pD, rpD = pr_.next()
                    P.mm(pD[:, 0:128], wrep[:], ptm[:], True, False, [rwr, rptm], [rpD])
                    P.mm(pD[:, 0:128], t.Nrep[:], qT[:, cs], False, True, [t.r_N, rq], [rpD])
                    t1, rt1 = t.t1r.next()
                    P.tt("vector", t1[:], pC[:, 0:128], erep[:], ALU.mult, [rpC, rer], [rt1])
                    yield
                    t2, rt2 = t.t2r.next()
                    P.tt("vector", t2[:], pD[:, 0:128], erep[:], ALU.mult, [rpD, rer], [rt2])
                    pE, rpE = pr_.next()
                    P.mm(pE[:, 0:129], ktm[:, c, :], vp[:], True, True, [rktm, rvp], [rpE])
                    yield
                    P.act("scalar", t2[:], t2[:], AF.Abs, [rt2], [rt2])
                    P.tt("vector", t.CTa[:], t.CTa[:], pE[:, 0:129], ALU.add, [rpE, t.r_CT], [t.r_CT])
                    yield
                    P.ts("vector", t2[:], t2[:], 1.0, None, ALU.max, None, [rt2], [rt2])
                    P.ts("vector", t.CTa[:], t.CTa[:], eb, None, ALU.mult, None, [t.r_CT, rer], [t.r_CT])
                    yield
                    P.add("vector", lambda e, t2=t2: e.reciprocal(t2[:], t2[:]), [rt2], [rt2])
                    P.act("scalar", t.CTb[:], t.CTa[:, 0:128], AF.Copy, [t.r_CT], [t.r_CTb])
                    P.ts("gpsimd", t.Nrep[:], ones_f[:], t.CTa[:, 128:129], None, ALU.mult, None, [r_const, t.r_CT], [t.r_N])
                    yield
                    P.tt("vector", hT[:, d_, cs], t1[:], t2[:], ALU.mult, [rt1, rt2], [rh])
                    yield

                def chain(t, hi, h, d_):
                    for step in range(18):
                        yield from unit(t, hi, h, d_, orders[d_][step])

                for heads in [(0, 1), (2, 3)]:
                    for hi, h in enumerate(heads):
                        zt, rz = zr.next()
                        P.dma("sync", zt[:], ZT[ZK + h * 128:ZK + (h + 1) * 128, :], [R("ZT")], [rz])
                        P.act("scalar", kTs[hi][:], zt[:], AF.Copy, [rz], [r_kT[hi]], scale=128 ** -0.5)
                        for n in range(18):
                            pt, rpt = prs[n % 4].next()
                            P.mm(pt[:, 0:128], zt[:, n * 128:(n + 1) * 128], cst[:, 0, :], True, True, [rz, r_const], [rpt])
                            if n % 2:
                                P.act("scalar", ktms[hi][:, n, :], pt[:, 0:128], AF.Copy, [rpt], [r_ktm[hi]], scale=128 ** -0.5)
                            else:
                                P.ts("vector", ktms[hi][:, n, :], pt[:, 0:128], 128 ** -0.5, None, ALU.mult, None, [rpt], [r_ktm[hi]])
                        zt, rz = zr.next()
                        P.dma("sync", zt[:], ZT[ZQ + h * 128:ZQ + (h + 1) * 128, :], [R("ZT")], [rz])
                        P.cp("vector", qTs[hi][:], zt[:], [rz], [r_qT[hi]])
                        zt, rz = zr.next()
                        P.dma("sync", zt[:], ZT[ZV_ + h * 128:ZV_ + (h + 1) * 128, :], [R("ZT")], [rz])
                        for n in range(18):
                            pt, rpt = prs[n % 4].next()
                            P.mm(pt[:, 0:128], zt[:, n * 128:(n + 1) * 128], cst[:, 0, :], True, True, [rz, r_const], [rpt])
                            if n % 2:
                                P.act("scalar", vtms[hi][:, n, :], pt[:, 0:128], AF.Copy, [rpt], [r_vtm[hi]])
                            else:
                                P.cp("vector", vtms[hi][:, n, :], pt[:, 0:128], [rpt], [r_vtm[hi]])
                    gens = []
                    for hi, h in enumerate(heads):
                        for d_ in range(2):
                            t = chains_t[hi * 2 + d_]
                            P.add("gpsimd", lambda e, t=t: e.memset(t.CTa[:], 0.0), [], [t.r_CT])
                            P.add("gpsimd", lambda e, t=t: e.memset(t.CTb[:], 0.0), [], [t.r_CTb])
                            P.add("gpsimd", lambda e, t=t: e.memset(t.Nrep[:], 0.0), [], [t.r_N])
                            gens.append(chain(t, hi, h, d_))
                    lockstep(gens)
                    for hi, h in enumerate(heads):
                        hT, rh = hTs[hi], r_hT[hi]
                        zt, rz = zr.next()
                        P.dma("sync", zt[:], ZT[ZO + h * 128:ZO + (h + 1) * 128, :], [R("ZT")], [rz])
                        P.act("scalar", zt[:], zt[:], AF.Sigmoid, [rz], [rz])
                        hs, rhs = hsum.next()
                        P.tt("vector", hs[:], hT[:, 0, :], hT[:, 1, :], ALU.add, [rh], [rhs])
                        ys, rys = ysr.next()
                        for (t0, w) in TOK_TILES:
                            s_, rsq = sqr.next()
                            P.act("scalar", s_[:, :w], hs[:, t0:t0 + w], AF.Square, [rhs], [rsq])
                            ps, rp = pr.next()
                            P.mm(ps[:, :w], ones_f[:], s_[:, :w], True, True, [r_const, rsq], [rp])
                            rstd, rr = rsr.next()
                            P.act("scalar", rstd[:, :w], ps[:, :w], AF.Sqrt, [rp, r_const], [rr], bias=epst[:, 0:1], scale=1.0 / 128)
                            P.add("vector", lambda e, rstd=rstd, w=w: e.reciprocal(rstd[:, :w], rstd[:, :w]), [rr], [rr])
                            P.stt("vector", rstd[:, :w], hs[:, t0:t0 + w], vec_t[:, l, 8 + h:9 + h], rstd[:, :w], ALU.mult, ALU.mult,
                                  [rhs, rr, r_const], [rr])
                            P.tt("vector", ys[:, t0:t0 + w], rstd[:, :w], zt[:, t0:t0 + w], ALU.mult, [rr, rz], [rys])
                        P.dma("sync", YT[1, h * 128:(h + 1) * 128, :], ys[:], [rys], [R(("YT", 1))])

        def gdn_phase(l):
            ZK, ZV_, ZBA, ZQ, ZZ = 1040, 1552, 2064, 4640, 5152
            NCH = 36
            with B.phase() as C:
                prs = [C.ring_ps(2, [128, 512], F32, "gp") for _ in range(4)]
                pr = prs[0]
                I64 = cst[0:64, 0, 0:64]
                zr = C.ring_sb(2, [128, TOK], F32, "zr")
                bat_t, r_bat = zr.next()
                bat = bat_t[0:16, :]
                P.dma("sync", bat, ZT[ZBA:ZBA + 16, :], [R("ZT")], [r_bat])
                ba_tm = C.sb([64, NCH, 16], F32, "ba_tm"); r_ba = Res()
                beta_tm = C.sb([64, NCH, 8], F32, "beta_tm"); r_be = Res()
                g_tm = C.sb([64, NCH, 8], F32, "g_tm"); r_g = Res()
                gc_tm = C.sb([64, NCH, 8], F32, "gc_tm"); r_gc = Res()
                bg_tm = C.sb([64, NCH, 8], F32, "bg_tm"); r_bg = Res()
                Aneg = C.sb([64, 8], F32, "Aneg"); r_A = Res()
                negm = C.sb([64, 2, 64], F32, "negm"); r_nm = Res()
                for n in range(NCH):
                    pt, rpt = pr.next()
                    P.mm(pt[0:64, 0:16], bat_t[0:16, n * 64:(n + 1) * 64], cst[0:16, 0, 0:16], True, True, [r_bat, r_const], [rpt])
                    P.add("vector", lambda e, pt=pt, n=n: e.tensor_copy(ba_tm[:, n, :], pt[0:64, 0:16]), [rpt], [r_ba])
                P.act("scalar", beta_tm[:], ba_tm[:, :, 0:8], AF.Sigmoid, [r_ba], [r_be])
                P.act("scalar", Aneg[:], gdc[0:64, l, 0:8], AF.Exp, [r_const], [r_A])
                P.add("vector", lambda e: e.tensor_scalar(Aneg[:], Aneg[:], -1.0, None, ALU.mult), [r_A], [r_A])
                for n in range(NCH):
                    P.add("vector", lambda e, n=n: e.tensor_tensor(g_tm[:, n, :], ba_tm[:, n, 8:16], gdc[0:64, l, 8:16], ALU.add),
                          [r_ba, r_const], [r_g])
                P.act("scalar", g_tm[:], g_tm[:], AF.Exp, [r_g], [r_g])
                P.act("scalar", g_tm[:], g_tm[:], AF.Ln, [r_g, r_const], [r_g], bias=ones_f[0:64, 0:1])
                for n in range(NCH):
                    P.add("vector", lambda e, n=n: e.tensor_tensor(g_tm[:, n, :], g_tm[:, n, :], Aneg[:], ALU.mult), [r_g, r_A], [r_g])
                for n in range(NCH):
                    pt, rpt = pr.next()
                    P.mm(pt[0:64, 0:4], cst[0:64, 3, 0:64], g_tm[:, n, 0:4], True, True, [r_const, r_g], [rpt])
                    P.mm(pt[0:64, 4:8], cst[0:64, 2, 0:64], g_tm[:, n, 4:8], True, True, [r_const, r_g], [rpt])
                    P.add("vector", lambda e, pt=pt, n=n: e.tensor_copy(gc_tm[:, n, :], pt[0:64, 0:8]), [rpt], [r_gc])
                P.act("scalar", bg_tm[:], gc_tm[:], AF.Exp, [r_gc], [r_bg])
                P.add("vector", lambda e: e.tensor_tensor(bg_tm[:], bg_tm[:], beta_tm[:], ALU.mult), [r_bg, r_be], [r_bg])
                P.add("vector", lambda e: e.tensor_scalar(negm[:, 0, :], cst[0:64, 2, 0:64], -1.0, 30000.0, ALU.add, ALU.mult),
                      [r_const], [r_nm])
                P.add("vector", lambda e: e.tensor_scalar(negm[:, 1, :], cst[0:64, 3, 0:64], -1.0, 30000.0, ALU.add, ALU.mult),
                      [r_const], [r_nm])
                cvr = C.ring_sb(2, [128, TOK], F32, "cv")
                kTs = [C.sb([128, TOK], BF16, "kT") for _ in range(2)]; r_kT = [Res(), Res()]
                qTs = [C.sb([128, TOK], BF16, "qT") for _ in range(2)]; r_qT = [Res(), Res()]
                ktms = [C.sb([64, NCH, 128], BF16, "ktm") for _ in range(2)]; r_ktm = [Res(), Res()]
                vtms = [C.sb([64, NCH, 128], BF16, "vtm") for _ in range(2)]; r_vtm = [Res(), Res()]
                oTs = [C.sb([128, 2, TOK], BF16, "oT") for _ in range(2)]; r_oT = [Res(), Res()]
                sqr = C.ring_sb(3, [128, 512], F32, "sq")
                rsr = C.ring_sb(3, [128, 512], F32, "rs")
                ysr = C.ring_sb(1, [128, TOK], BF16, "ys")
                class T_:
                    pass
                chains_t = []
                for ci in range(4):
                    t = T_()
                    mk = lambda name, shape, dt, n=2: C.ring_sb(n, shape, dt, name)
                    t.grr = mk("grep", [64, 128], F32); t.gcr = mk("gcrep", [128, 64], F32); t.egr = mk("egrep", [128, 64], F32)
                    t.Er = mk("E", [64, 64], F32); t.Mr = mk("M", [64, 64], F32); t.Mtr = mk("Mt", [64, 64], F32)
                    t.Br = mk("B", [64, 6, 64], F32); t.Btr = mk("Bt", [64, 5, 64], F32)
                    t.Xbr = mk("Xtb", [64, 64], BF16); t.Xgr = mk("Xtg", [64, 64], BF16)
                    t.Xr = mk("X", [64, 64], F32, 3); t.Xtr = mk("Xt", [64, 64], F32, 3)
                    t.Yr = mk("Y1", [64, 64], F32); t.Zr_ = mk("Z1", [64, 64], F32)
                    t.ur = mk("u", [64, 128], F32); t.wTr = mk("wT", [128, 64], BF16)
                    t.vnr = mk("vnew", [64, 128], BF16); t.qkr = mk("qkm", [64, 64], F32); t.qkTr = mk("qkT", [64, 64], BF16)
                    t.qdr = mk("qd", [128, 64], BF16); t.scr = mk("sc", [64, 1], F32); t.ker = mk("kend", [64, 128], BF16)
                    t.Sf = C.sb([128, 128], F32, "Sf"); t.r_S = Res()
                    t.Sb = C.sb([128, 128], BF16, "Sb"); t.r_Sb = Res()
                    t.pr = prs[ci]
                    chains_t.append(t)
                orders = [list(range(NCH)), [3, 2, 1, 0] + list(range(NCH - 1, 3, -1))]
                SEGS = [(0, T_CTX), (T_CTX, TOK)]

                def conv_silu(src, rsrc, ch):
                    y, ry = cvr.next()
                    P.add("vector", lambda e, y=y, src=src, ch=ch: e.tensor_scalar(y[:], src[:], cw[:, l, 1, ch:ch + 1], None, ALU.mult),
                          [rsrc, r_const], [ry])
                    for (a, b_) in SEGS:
                        P.add("vector", lambda e, y=y, src=src, ch=ch, a=a, b_=b_: e.scalar_tensor_tensor(
                            y[:, a + 1:b_], src[:, a:b_ - 1], cw[:, l, 0, ch:ch + 1], y[:, a + 1:b_], ALU.mult, ALU.add),
                            [rsrc, r_const, ry], [ry])
                        P.add("vector", lambda e, y=y, src=src, ch=ch, a=a, b_=b_: e.scalar_tensor_tensor(
                            y[:, a:b_ - 1], src[:, a + 1:b_], cw[:, l, 2, ch:ch + 1], y[:, a:b_ - 1], ALU.mult, ALU.add),
                            [rsrc, r_const, ry], [ry])
                    P.act("scalar", y[:], y[:], AF.Silu, [ry], [ry])
                    return y, ry

                prep_ring = Ring([t_ for r_ in prs for t_ in r_.tiles])
                prep_ring.res = [x_ for r_ in prs for x_ in r_.res]

                def l2norm(y, ry, dst, rdst, mul):
                    def tile(t0, w):
                        s_, rsq = sqr.next()
                        P.act("scalar", s_[:, :w], y[:, t0:t0 + w], AF.Square, [ry], [rsq])
                        yield
                        ps, rp = prep_ring.next()
                        P.mm(ps[:, :w], ones_f[:], s_[:, :w], True, True, [r_const, rsq], [rp])
                        yield
                        rstd, rr = rsr.next()
                        P.act("scalar", rstd[:, :w], ps[:, :w], AF.Sqrt, [rp, r_const], [rr], bias=epst[:, 0:1], scale=1.0)
                        yield
                        P.add("vector", lambda e: e.reciprocal(rstd[:, :w], rstd[:, :w]), [rr], [rr])
                        yield
                        P.tt("vector", y[:, t0:t0 + w], y[:, t0:t0 + w], rstd[:, :w], ALU.mult, [ry, rr], [ry])
                        yield
                    lockstep([tile(t0, w) for (t0, w) in TOK_TILES[0:3]])
                    lockstep([tile(t0, w) for (t0, w) in TOK_TILES[3:5]])
                    P.act("scalar", dst[:], y[:], AF.Copy, [ry], [rdst], scale=float(mul))

                def transposes(src, rsrc, dst, rdst):
                    for n in range(NCH):
                        pt, rpt = prs[n % 4].next()
                        P.mm(pt[0:64, 0:128], src[:, n * 64:(n + 1) * 64], cst[:, 0, :], True, True, [rsrc, r_const], [rpt])
                        if n % 2:
                            P.act("scalar", dst[:, n, :], pt[0:64, 0:128], AF.Copy, [rpt], [rdst])
                        else:
                            P.add("vector", lambda e, dst=dst, pt=pt, n=n: e.tensor_copy(dst[:, n, :], pt[0:64, 0:128]), [rpt], [rdst])

                def unit(t, hi, h, d_, c):
                    kT, rk = kTs[hi], r_kT[hi]
                    qT, rq = qTs[hi], r_qT[hi]
                    ktm, rktm = ktms[hi], r_ktm[hi]
                    vtm, rvtm = vtms[hi], r_vtm[hi]
                    oT, ro = oTs[hi], r_oT[hi]
                    pr_ = t.pr
                    cs = slice(c * 64, (c + 1) * 64)
                    col = d_ * 4 + h
                    U = cst[0:64, 3, 0:64] if d_ == 0 else cst[0:64, 2, 0:64]
                    Minc = cst[0:64, 2, 0:64] if d_ == 0 else cst[0:64, 3, 0:64]
                    END = 63 if d_ == 0 else 0
                    gcol = gc_tm[:, c, col:col + 1]
                    bcol = beta_tm[:, c, col:col + 1]
                    grp, rgrp = t.grr.next()
                    P.act("scalar", grp[:], ones_f[0:64, :], AF.Identity, [r_const, r_g], [rgrp], scale=g_tm[:, c, col:col + 1])
                    pk, rpk = pr_.next()
                    P.mm(pk[0:64, 0:64], kT[:, cs], kT[:, cs], True, True, [rk], [rpk])
                    yield
                    pg, rpg = pr_.next()
                    P.mm(pg[:, 0:64], grp[:], U, True, True, [rgrp, r_const], [rpg])
                    yield
                    gcrep, rgcr = t.gcr.next()
                    P.cp("vector", gcrep[:], pg[:, 0:64], [rpg], [rgcr])
                    yield
                    egrep, regr = t.egr.next()
                    P.act("scalar", egrep[:], gcrep[:], AF.Exp, [rgcr], [regr])
                    E, rE = t.Er.next()
                    P.ts("vector", E[:], gcrep[0:64, :], gcol, 0.0, ALU.subtract, ALU.max, [rgcr, r_gc], [rE])
                    sc, rsc = t.scr.next()
                    P.tt("gpsimd", sc[:], gcrep[0:64, END:END + 1], gcol, ALU.subtract, [rgcr, r_gc], [rsc])
                    yield
                    P.act("scalar", E[:], E[:], AF.Exp, [rE], [rE], scale=-1.0)
                    P.act("scalar", sc[:], sc[:], AF.Exp, [rsc], [rsc])
                    qd, rqd = t.qdr.next()
                    P.tt("gpsimd", qd[:], qT[:, cs], egrep[:], ALU.mult, [rq, regr], [rqd])
                    yield
                    P.tt("gpsimd", E[:], E[:], Minc, ALU.mult, [rE, r_const], [rE])
                    ke, rke = t.ker.next()
                    P.act("scalar", ke[:], ktm[:, c, :], AF.Identity, [rktm, rsc], [rke], scale=sc[:, 0:1])
                    yield
                    M, rM = t.Mr.next()
                    P.stt("vector", M[:], pk[0:64, 0:64], bcol, E[:], ALU.mult, ALU.mult, [rpk, rE, r_be], [rM])
                    pq, rpq = pr_.next()
                    P.mm(pq[0:64, 0:64], qT[:, cs], kT[:, cs], True, True, [rq, rk], [rpq])
                    yield
                    pmt, rpmt = pr_.next()
                    P.mm(pmt[0:64, 0:64], M[:], I64, True, True, [rM, r_const], [rpmt])
                    qkm, rqk = t.qkr.next()
                    P.tt("vector", qkm[:], pq[0:64, 0:64], E[:], ALU.mult, [rpq, rE], [rqk])
                    yield
                    pqt, rpqt = pr_.next()
                    P.mm(pqt[0:64, 0:64], qkm[:], I64, True, True, [rqk, r_const], [rpqt])
                    Ball, rB = t.Br.next()
                    P.tt("vector", Ball[:], M[:].unsqueeze(1).to_broadcast([64, 6, 64]), lm[:, 0:6, :], ALU.mult, [rM, r_const], [rB])
                    yield
                    Mt, rMt = t.Mtr.next()
                    P.cp("vector", Mt[:], pmt[0:64, 0:64], [rpmt], [rMt])
                    qkT, rqkT = t.qkTr.next()
                    P.act("scalar", qkT[:], pqt[0:64, 0:64], AF.Copy, [rpqt], [rqkT])
                    X, rX = t.Xr.next()
                    P.tt("vector", X[:], I64, Ball[:, 0, :], ALU.subtract, [rB, r_const], [rX])
                    yield
                    Btall, rBt = t.Btr.next()
                    P.tt("gpsimd", Btall[:], Mt[:].unsqueeze(1).to_broadcast([64, 5, 64]), lm[:, 0:5, :], ALU.mult, [rMt, r_const], [rBt])
                    yield
                    Xt, rXt = t.Xtr.next()
                    P.tt("gpsimd", Xt[:], I64, Btall[:, 0, :], ALU.subtract, [rBt, r_const], [rXt])
                    yield
                    for lv in range(1, 6):
                        pz1, rpz1 = pr_.next()
                        P.mm(pz1[0:64, 0:64], Ball[:, lv, :], Xt[:], True, True, [rB, rXt], [rpz1])
                        if lv < 5:
                            py1, rpy1 = pr_.next()
                            P.mm(py1[0:64, 0:64], Btall[:, lv, :], X[:], True, True, [rBt, rX], [rpy1])
                        yield
                        Z1, rZ1 = t.Zr_.next()
                        P.act("scalar", Z1[:], pz1[0:64, 0:64], AF.Copy, [rpz1], [rZ1])
                        if lv < 5:
                            Y1, rY1 = t.Yr.next()
                            P.cp("vector", Y1[:], py1[0:64, 0:64], [rpy1], [rY1])
                        yield
                        pz2, rpz2 = pr_.next()
                        P.mm(pz2[0:64, 0:64], X[:], Z1[:], True, True, [rX, rZ1], [rpz2])
                        if lv < 5:
                            py2, rpy2 = pr_.next()
                            P.mm(py2[0:64, 0:64], Xt[:], Y1[:], True, True, [rXt, rY1], [rpy2])
                        yield
                        Xtn, rXtn = t.Xtr.next()
                        P.tt("vector", Xtn[:], Xt[:], pz2[0:64, 0:64], ALU.subtract, [rXt, rpz2], [rXtn])
                        if lv < 5:
                            Xn, rXn = t.Xr.next()
                            P.tt("vector", Xn[:], X[:], py2[0:64, 0:64], ALU.subtract, [rX, rpy2], [rXn])
                            X, rX = Xn, rXn
                        Xt, rXt = Xtn, rXtn
                        yield
                    Xtb, rXtb = t.Xbr.next()
                    P.act("scalar", Xtb[:], Xt[:], AF.Identity, [rXt, r_be], [rXtb], scale=bcol)
                    Xtg, rXtg = t.Xgr.next()
                    P.act("scalar", Xtg[:], Xt[:], AF.Identity, [rXt, r_bg], [rXtg], scale=bg_tm[:, c, col:col + 1])
                    yield
                    pu, rpu = pr_.next()
                    P.mm(pu[0:64, 0:128], Xtb[:], vtm[:, c, :], True, True, [rXtb, rvtm], [rpu])
                    pw, rpw = pr_.next()
                    P.mm(pw[:, 0:64], ktm[:, c, :], Xtg[:], True, True, [rktm, rXtg], [rpw])
                    yield
                    u, ru = t.ur.next()
                    P.act("scalar", u[:], pu[0:64, 0:128], AF.Copy, [rpu], [ru])
                    wT, rwT = t.wTr.next()
                    P.cp("vector", wT[:], pw[:, 0:64], [rpw], [rwT])
                    yield
                    pws, rpws = pr_.next()
                    P.mm(pws[0:64, 0:128], wT[:], t.Sb[:], True, True, [rwT, t.r_Sb], [rpws])
                    po_, rpo = pr_.next()
                    P.mm(po_[:, 0:64], t.Sb[:], qd[:], True, False, [t.r_Sb, rqd], [rpo])
                    yield
                    vn, rvn = t.vnr.next()
                    P.tt("vector", vn[:], u[:], pws[0:64, 0:128], ALU.subtract, [ru, rpws], [rvn])
                    yield
                    P.mm(po_[:, 0:64], vn[:], qkT[:], False, True, [rvn, rqkT], [rpo])
                    pds, rpds = pr_.next()
                    P.mm(pds[:, 0:128], ke[:], vn[:], True, True, [rke, rvn], [rpds])
                    yield
                    P.act("scalar", oT[:, d_, cs], po_[:, 0:64], AF.Copy, [rpo], [ro])
                    P.stt("vector", t.Sf[:], t.Sf[:], egrep[:, END:END + 1], pds[:, 0:128], ALU.mult, ALU.add,
                          [t.r_S, regr, rpds], [t.r_S])
                    yield
                    P.act("scalar", t.Sb[:], t.Sf[:], AF.Copy, [t.r_S], [t.r_Sb])
                    yield

                def chain(t, hi, h, d_):
                    for step in range(NCH):
                        yield from unit(t, hi, h, d_, orders[d_][step])

                HP = [(0, 1), (2, 3)]
                if "gdh1" in debug:
                    HP = [(0,)]
                for heads in HP:
                    for hi, h in enumerate(heads):
                        zt, rz = zr.next()
                        P.dma("sync", zt[:], ZT[ZK + h * 128:ZK + (h + 1) * 128, :], [R("ZT")], [rz])
                        kf, rkf = conv_silu(zt, rz, 4 + h)
                        l2norm(kf, rkf, kTs[hi], r_kT[hi], 1.0)
                        transposes(kf, rkf, ktms[hi], r_ktm[hi])
                        zt, rz = zr.next()
                        P.dma("sync", zt[:], ZT[ZQ + h * 128:ZQ + (h + 1) * 128, :], [R("ZT")], [rz])
                        qf, rqf = conv_silu(zt, rz, h)
                        l2norm(qf, rqf, qTs[hi], r_qT[hi], 128 ** -0.5)
                        zt, rz = zr.next()
                        P.dma("sync", zt[:], ZT[ZV_ + h * 128:ZV_ + (h + 1) * 128, :], [R("ZT")], [rz])
                        vf, rvf = conv_silu(zt, rz, 8 + h)
                        transposes(vf, rvf, vtms[hi], r_vtm[hi])
                    gens = []
                    for hi, h in enumerate(heads):
                        for d_ in range(2):
                            t = chains_t[hi * 2 + d_]
                            P.add("gpsimd", lambda e, t=t: e.memset(t.Sf[:], 0.0), [], [t.r_S])
                            P.add("gpsimd", lambda e, t=t: e.memset(t.Sb[:], 0.0), [], [t.r_Sb])
                            gens.append(chain(t, hi, h, d_))
                    lockstep(gens)
                    for hi, h in enumerate(heads):
                        oT, ro = oTs[hi], r_oT[hi]
                        zt, rz = zr.next()
                        P.dma("sync", zt[:], ZT[ZZ + h * 128:ZZ + (h + 1) * 128, :], [R("ZT")], [rz])
                        P.act("scalar", zt[:], zt[:], AF.Silu, [rz], [rz])
                        osum, rosum = cvr.next()
                        P.add("vector", lambda e, oT=oT, osum=osum: e.tensor_tensor(osum[:], oT[:, 0, :], oT[:, 1, :], ALU.add), [ro], [rosum])
                        ys, rys = ysr.next()
                        for (t0, w) in TOK_TILES:
                            s_, rsq = sqr.next()
                            P.act("scalar", s_[:, :w], osum[:, t0:t0 + w], AF.Square, [rosum], [rsq])
                            ps, rp = pr.next()
                            P.mm(ps[:, :w], ones_f[:], s_[:, :w], True, True, [r_const, rsq], [rp])
                            rstd, rr = rsr.next()
                            P.act("scalar", rstd[:, :w], ps[:, :w], AF.Sqrt, [rp, r_const], [rr], bias=epst[:, 0:1], scale=1.0 / 128)
                            P.add("vector", lambda e, rstd=rstd, w=w: e.reciprocal(rstd[:, :w], rstd[:, :w]), [rr], [rr])
                            P.add("vector", lambda e, rstd=rstd, osum=osum, t0=t0, w=w: e.scalar_tensor_tensor(
                                rstd[:, :w], osum[:, t0:t0 + w], vec_t[:, l, 2:3], rstd[:, :w], ALU.mult, ALU.mult),
                                [rosum, rr, r_const], [rr])
                            P.add("vector", lambda e, ys=ys, rstd=rstd, zt=zt, t0=t0, w=w: e.tensor_tensor(
                                ys[:, t0:t0 + w], rstd[:, :w], zt[:, t0:t0 + w], ALU.mult), [rr, rz], [rys])
                        P.dma("sync", YT[2, h * 128:(h + 1) * 128, :], ys[:], [rys], [R(("YT", 2))])

        def merge_phase(l, Xsrc, xkey, Xdst, dkey, last=False):
            ZG = 6176
            with B.phase() as C:
                yt = C.sb([128, 16, TOK], BF16, "yt"); r_yt = Res()
                mg = C.sb([128, KC, TOK], BF16, "mg"); r_mg = Res()
                P.dma("sync", yt[:], YT.rearrange("n (k p) t -> p (n k) t", p=128), [R(("YT", n)) for n in range(4)], [r_yt])
                wbr = C.ring_sb(2, [128, 16, 128], BF16, "wbr")
                gr = C.ring_sb(4, [128, 4, 256], F32, "gate")
                pm = C.ring_ps(4, [128, 512], F32, "pm")
                acr = C.ring_sb(3, [128, 256], F32, "acc")
                tilesA = NORM_TILES[1:] if last else NORM_TILES
                tilesB = TOK_TILES[1:] if last else TOK_TILES
                gq = 0
                for fc in range(KC):
                    wb, rwb = wbr.next()
                    P.dma("gpsimd", wb[:], w_branch[l, :, :, fc * 128:(fc + 1) * 128].rearrange("n (k p) c -> p (n k) c", p=128),
                          [], [rwb])
                    for (t0, w) in tilesA:
                        gt_, rgt = gr.next()
                        gq += 1
                        P.dma("scalar" if gq % 2 else "sync", gt_[:, :, :w],
                              ZT[ZG:ZG + 4 * D, t0:t0 + w].rearrange("(n r) t -> r n t", r=D)[fc * 128:(fc + 1) * 128],
                              [R("ZT")], [rgt])
                        P.act("scalar", gt_[:, :, :w], gt_[:, :, :w], AF.Sigmoid, [rgt], [rgt])
                        acc, racc = acr.next()
                        for n in range(4):
                            ps, rp = pm.next()
                            for k in range(4):
                                P.mm(ps[:, :w], wb[:, n * 4 + k, :], yt[:, n * 4 + k, t0:t0 + w], k == 0, k == 3,
                                     [rwb, r_yt], [rp])
                            if n == 0:
                                P.tt("vector", acc[:, :w], gt_[:, 0, :w], ps[:, :w], ALU.mult, [rgt, rp], [racc])
                            else:
                                P.tt("vector", gt_[:, n, :w], gt_[:, n, :w], ps[:, :w], ALU.mult, [rgt, rp], [rgt])
                                if n < 3:
                                    P.tt("gpsimd" if n == 2 else "vector", acc[:, :w], acc[:, :w], gt_[:, n, :w], ALU.add, [rgt, racc], [racc])
                                else:
                                    P.tt("vector", mg[:, fc, t0:t0 + w], acc[:, :w], gt_[:, n, :w], ALU.add, [rgt, racc], [r_mg])
                wor = C.ring_sb(2, [128, KC, 128], BF16, "wo")
                xrr = C.ring_sb(2, [128, 512], F32, "xres")
                orr = C.ring_sb(2, [128, 512], F32, "ores")
                for fc in range(KC):
                    wo, rwo = wor.next()
                    P.dma("gpsimd", wo[:], w_out_all[l, :, fc * 128:(fc + 1) * 128].rearrange("(k p) n -> p k n", p=128),
                          [], [rwo])
                    if True:
                        for (t0, w) in tilesB:
                            j = 1 if t0 < T_CTX else 0
                            y, ry = pm.next()
                            for k in range(KC):
                                P.mm(y[:, :w], wo[:, k, :], mg[:, k, t0:t0 + w], k == 0, k == KC - 1,
                                     [rwo, r_mg], [ry])
                            xr_, rxr = xrr.next()
                            P.dma("scalar", xr_[:, :w], Xsrc[fc * 128:(fc + 1) * 128, t0:t0 + w], [R((xkey, fc))], [rxr])
                            o, ro = orr.next()
                            P.add("vector", lambda e, o=o, y=y, xr_=xr_, fc=fc, j=j, w=w: e.scalar_tensor_tensor(
                                o[:, :w], y[:, :w], Hg[:, 1, fc, j:j + 1], xr_[:, :w], ALU.mult, ALU.add),
                                [ry, rxr, r_mod], [ro])
                            P.dma("sync", Xdst[fc * 128:(fc + 1) * 128, t0:t0 + w], o[:, :w], [ro], [R((dkey, fc))])

        X0 = xT_in
        for l in range(nlayers):
            if "only_gdn" in debug:
                gdn_phase(0)
                break
            if "only_ml" in debug:
                mlstm_phase(0)
                break
            if "only_attn" in debug:
                attn_phase(0)
                break
            if "only_sg" in debug:
                sg_phase(0)
                break
            mod_phase(l)
            ffn(l, 0, X0 if l == 0 else Xs[2], "X0" if l == 0 else ("X2", l - 1), Xs[0], ("Xa", l))
            if stop_after == ("ffn1", l):
                break
            if "nomix" in debug:
                ffn(l, 1, Xs[0], ("Xa", l), Xs[2], ("X2", l), final_out=(l == nlayers - 1))
                continue
            inproj(l, Xs[0], ("Xa", l))
            if stop_after == ("inproj", l):
                break
            if "zero_y12" in debug:
                with B.phase() as C:
                    zt_ = C.sb([128, TOK], BF16, "zt_"); rz_ = Res()
                    P.add("vector", lambda e: e.memset(zt_[:], 0.0), [], [rz_])
                    for n_ in ([1] if "noml" in debug else []) + ([2] if "nogd" in debug else []):
                        for k_ in range(4):
                            P.dma("sync", YT[n_, k_ * 128:(k_ + 1) * 128, :], zt_[:], [rz_], [R(("YT", n_))])
            attn_phase(l)
            sg_phase(l)
            if "noml" not in debug:
                mlstm_phase(l)
            if "nogd" not in debug:
                gdn_phase(l)
            if stop_after == ("mix", l):
                break
            merge_phase(l, Xs[0], ("Xa", l), Xs[1], ("Xb", l), last=(l == nlayers - 1))
            if stop_after == ("merge", l):
                break
            ffn(l, 1, Xs[1], ("Xb", l), Xs[2], ("X2", l), final_out=(l == nlayers - 1))
        P.emit(final=True)
    return nc


def _host_consts():
    m = {}
    cst = np.zeros((128, 4, 128), np.float32)
    cst[:, 0, :] = np.eye(128)
    pm = np.zeros((128, 128), np.float32)
    for p in range(128):
        if p % 64 < 32:
            pm[p + 32, p] = -1.0
        else:
            pm[p - 32, p] = 1.0
    cst[:, 1, :] = pm
    cst[:, 2, :] = np.tril(np.ones((128, 128)))
    cst[:, 3, :] = np.triu(np.ones((128, 128)))
    m["consts"] = cst
    n = 32
    inv = (10000.0 ** (-np.arange(n, dtype=np.float32) / n)).astype(np.float32)
    t = np.arange(T_LAT)
    ar = (t // 64).astype(np.float32)[:, None] * inv
    ac = (t % 64).astype(np.float32)[:, None] * inv
    cos = np.concatenate([np.cos(ar), np.cos(ar), np.cos(ac), np.cos(ac)], axis=1).T
    sin = np.concatenate([np.sin(ar), np.sin(ar), np.sin(ac), np.sin(ac)], axis=1).T
    m["ropetab"] = np.ascontiguousarray(np.stack([cos, sin], axis=1).astype(np.float32))
    lm = np.zeros((64, 7, 64), np.float32)
    ii = np.arange(64)
    for lv in range(6):
        lm[:, lv, :] = ((ii[:, None] >> (lv + 1)) == (ii[None, :] >> (lv + 1))) & ((ii[:, None] >> lv) != (ii[None, :] >> lv))
    lm[:, 6, :] = (ii[:, None] != ii[None, :])
    m["lm"] = lm
    return m


def _host_inputs(inp, b, shared):
    m = dict(shared)
    m["xT"] = np.ascontiguousarray(np.concatenate([inp["ctx"][b], inp["x"][b]], axis=0).T)
    m["cvec"] = np.ascontiguousarray(np.stack([inp["c"][b].reshape(KC, 128).T, inp["c_ctx"].reshape(KC, 128).T], axis=-1))
    return m


def kernel(**inp):
    inp = {k: np.asarray(v) for k, v in inp.items()}
    shared = _host_consts()
    for k in ("mod_w", "ffn1_w_in", "ffn2_w_in", "ffn1_w_out", "ffn2_w_out", "w_in", "w_branch", "w_out", "sg_b"):
        shared[k] = inp[k]
    shared["mod_b"] = np.ascontiguousarray(inp["mod_b"].reshape(DEPTH, 144, 128).transpose(0, 2, 1))
    nr = np.stack([inp["ffn1_norm"], inp["mix_norm"], inp["ffn2_norm"]], axis=1)
    shared["norms"] = np.ascontiguousarray(nr.reshape(DEPTH, 3, KC, 128).transpose(0, 3, 1, 2))
    shared["sgwT"] = np.ascontiguousarray(inp["sg_w"].transpose(0, 1, 3, 2))
    v = np.zeros((128, DEPTH, 12), np.float32)
    v[:, :, 0] = inp["at_q_norm"].T
    v[:, :, 1] = inp["at_k_norm"].T
    v[:, :, 2] = inp["gd_norm"].T
    v[:, :, 4:8] = inp["sg_norm"].reshape(DEPTH, 4, 128).transpose(2, 0, 1)
    v[:, :, 8:12] = inp["ml_norm"].reshape(DEPTH, 4, 128).transpose(2, 0, 1)
    shared["vecs"] = v
    shared["mlb"] = np.ascontiguousarray(np.broadcast_to(inp["ml_if_bias"][None], (128, DEPTH, 16))).astype(np.float32)
    g = np.concatenate([inp["gd_a_log"].reshape(DEPTH, 8), inp["gd_dt_bias"].reshape(DEPTH, 8)], axis=1)
    shared["gdc"] = np.ascontiguousarray(np.broadcast_to(g[None], (128, DEPTH, 16))).astype(np.float32)
    shared["cw"] = np.ascontiguousarray(inp["gd_conv"].reshape(DEPTH, 3, 12, 128).transpose(3, 0, 1, 2))
    nc = build_program()
    n = inp["x"].shape[0]
    in_maps = [_host_inputs(inp, b, shared) for b in range(n)]
    res = run_bass_kernel_spmd(nc, in_maps, core_ids=list(range(n)))
    out = np.stack([np.ascontiguousarray(r["outT"].T) for r in res.results], axis=0)
    return out.astype(np.float32)
```

```python
import math
from contextlib import ExitStack
import numpy as np
import concourse.bass as bass
import concourse.mybir as mybir
from concourse.bass_utils import run_bass_kernel_spmd

F32 = mybir.dt.float32
BF16 = mybir.dt.bfloat16
AF = mybir.ActivationFunctionType
ALU = mybir.AluOpType

D = 2048
T_LAT = 2048
T_CTX = 256
TOK = T_CTX + T_LAT
DEPTH = 2
DFF = 5632
KC = D // 128
IN_COLS = 14368
EPS = 1e-6


class Res:
    __slots__ = ("name", "w", "rs")

    def __init__(self, name=""):
        self.name = name
        self.w = None
        self.rs = []


class Op:
    __slots__ = ("id", "eng", "fn", "deps", "dma", "prod", "cnt", "dsem", "dval", "dprev")

    def __init__(self, id, eng, fn, deps, dma):
        self.id = id
        self.eng = eng
        self.fn = fn
        self.deps = deps
        self.dma = dma
        self.prod = False
        self.cnt = None
        self.dsem = None
        self.dval = None
        self.dprev = None


ENGS = ("tensor", "vector", "scalar", "gpsimd", "sync")


class Prog:
    def __init__(self, nc, ndsem=6):
        self.nc = nc
        self.ops = []
        self.ndsem = ndsem

    def add(self, eng, fn, reads=(), writes=(), dma=False):
        oid = len(self.ops)
        deps = set()
        for r in reads:
            if r.w is not None:
                deps.add(r.w)
        for w in writes:
            if w.w is not None:
                deps.add(w.w)
            for x in w.rs:
                deps.add(x)
        deps.discard(oid)
        op = Op(oid, eng, fn, deps, dma)
        self.ops.append(op)
        for r in reads:
            r.rs.append(oid)
        for w in writes:
            w.w = oid
            w.rs = []
        return op

    def mm(self, out, lhsT, rhs, start, stop, reads, writes):
        self.add("tensor", lambda e: e.matmul(out, lhsT, rhs, start=start, stop=stop), reads, writes)

    def tr(self, out, in_, ident, reads, writes):
        self.add("tensor", lambda e: e.transpose(out, in_, ident), reads, writes)

    def act(self, eng, out, in_, func, reads, writes, bias=None, scale=None):
        kw = {}
        if bias is not None:
            kw["bias"] = bias
        if scale is not None:
            kw["scale"] = scale
        self.add(eng, lambda e: e.activation(out, in_, func, **kw), reads, writes)

    def tt(self, eng, out, a, b, op, reads, writes):
        self.add(eng, lambda e: e.tensor_tensor(out, a, b, op), reads, writes)

    def ts(self, eng, out, a, s1, s2, op0, op1, reads, writes):
        if op1 is None:
            self.add(eng, lambda e: e.tensor_scalar(out, a, s1, s2, op0), reads, writes)
        else:
            self.add(eng, lambda e: e.tensor_scalar(out, a, s1, s2, op0, op1), reads, writes)

    def stt(self, eng, out, a, sc, b, op0, op1, reads, writes):
        self.add(eng, lambda e: e.scalar_tensor_tensor(out, a, sc, b, op0, op1), reads, writes)

    def cp(self, eng, out, a, reads, writes):
        self.add(eng, lambda e: e.tensor_copy(out, a), reads, writes)

    def dma(self, eng, out, in_, reads, writes, **kw):
        self.add(eng, lambda e: e.dma_start(out=out, in_=in_, **kw), reads, writes, dma=True)

    def setup(self, stack):
        nc = self.nc
        self.esem = {e: stack.enter_context(nc.semaphore("s_" + e)) for e in ENGS}
        self.dsems = {e: [stack.enter_context(nc.semaphore("d_%s%d" % (e, i))) for i in range(self.ndsem)]
                      for e in ("sync", "gpsimd", "scalar")}
        self.ecnt = {e: 0 for e in ENGS}
        self.stack_ = stack
        self.SEM_CAP = 12000
        self.dcount = {e: [0] * self.ndsem for e in self.dsems}
        self.dnext = {e: 0 for e in self.dsems}
        self.emitted = 0
        self.nphase = 0

    def emit(self, final=False):
        nc = self.nc
        ops = self.ops
        lo = self.emitted
        cur = ops[lo:]
        self.emitted = len(ops)
        self.nphase += 1
        bar_e = dict(self.ecnt)
        bar_sem = dict(self.esem)
        for e_ in ENGS:
            if self.ecnt[e_] > self.SEM_CAP:
                self.esem[e_] = self.stack_.enter_context(nc.semaphore("s_%s_%d" % (e_, self.nphase)))
                self.ecnt[e_] = 0
        bar_d = {q: list(v) for q, v in self.dcount.items()}
        for op in cur:
            for d in op.deps:
                if d < lo:
                    continue
                p = ops[d]
                if not p.dma:
                    if p.eng == "tensor" and op.eng == "tensor" and not op.dma:
                        continue
                    p.prod = True
        lastc = {}
        for op in cur:
            if not op.dma:
                lastc[op.eng] = op
        for op in lastc.values():
            op.prod = True
        for op in cur:
            if op.dma:
                i = self.dnext[op.eng] % self.ndsem
                self.dnext[op.eng] += 1
                op.dsem = self.dsems[op.eng][i]
                op.dprev = self.dcount[op.eng][i]
                self.dcount[op.eng][i] += 16
                op.dval = self.dcount[op.eng][i]
            elif op.prod:
                self.ecnt[op.eng] += 1
                op.cnt = self.ecnt[op.eng]
        per = {e: [o for o in cur if o.eng == e] for e in ENGS}
        esem, dsems = dict(self.esem), self.dsems

        trace = getattr(self, "trace", None)
        if trace is None:
            trace = self.trace = {e: [] for e in ENGS}
        semname = getattr(self, "semname", None)
        if semname is None:
            semname = self.semname = {}
        for e_ in ENGS:
            semname[id(self.esem[e_])] = "s_%s_%d" % (e_, id(self.esem[e_]))
        for q_ in self.dsems:
            for i_, sm in enumerate(self.dsems[q_]):
                semname[id(sm)] = "d_%s%d" % (q_, i_)

        def run(e, engname):
            waited = {}
            tr = trace[engname]

            def wait(sem, val):
                k = id(sem)
                if waited.get(k, 0) >= val:
                    return
                waited[k] = val
                tr.append(("wait", semname[k], val))
                e.wait_ge(sem, val)

            for e2 in ENGS:
                if e2 != engname and bar_e[e2] > 0:
                    wait(bar_sem[e2], bar_e[e2])
            for q in dsems:
                for i in range(self.ndsem):
                    if bar_d[q][i] > 0:
                        wait(dsems[q][i], bar_d[q][i])
            for op in per[engname]:
                for d in sorted(op.deps):
                    p = ops[d]
                    if p.dma:
                        wait(p.dsem, p.dval)
                    elif d >= lo:
                        if p.eng == "tensor" and engname == "tensor" and not op.dma:
                            continue
                        wait(esem[p.eng], p.cnt)
                if op.dma:
                    if op.dprev > 0:
                        wait(op.dsem, op.dprev)
                    op.fn(e).then_inc(op.dsem, 16)
                    tr.append(("inc", semname[id(op.dsem)], 16))
                else:
                    ins = op.fn(e)
                    if op.prod:
                        ins.then_inc(esem[engname], 1)
                        tr.append(("inc", semname[id(esem[engname])], 1))
            if final and engname == "sync":
                for q in dsems:
                    for i in range(self.ndsem):
                        if self.dcount[q][i] > 0:
                            e.wait_ge(dsems[q][i], self.dcount[q][i])
                for e2 in ENGS:
                    if e2 != "sync" and self.ecnt[e2] > 0:
                        e.wait_ge(esem[e2], self.ecnt[e2])

        with nc.Block() as block:
            @block.tensor
            def _(e):
                run(e, "tensor")

            @block.vector
            def _(e):
                run(e, "vector")

            @block.scalar
            def _(e):
                run(e, "scalar")

            @block.gpsimd
            def _(e):
                run(e, "gpsimd")

            @block.sync
            def _(e):
                run(e, "sync")


class Ring:
    def __init__(self, tiles):
        self.tiles = tiles
        self.res = [Res() for _ in tiles]
        self.i = 0

    def next(self):
        k = self.i % len(self.tiles)
        self.i += 1
        return self.tiles[k], self.res[k]


class Ctx:
    def __init__(self, nc, stack):
        self.nc = nc
        self.stack = stack
        self.P = Prog(nc)
        self.n = 0

    _uid = [0]

    def sb(self, shape, dt, name=None):
        Ctx._uid[0] += 1
        return self.stack.enter_context(self.nc.sbuf_tensor("%s_%d" % (name or "sb", Ctx._uid[0]), list(shape), dt))

    def ps(self, shape, dt=F32, name=None):
        Ctx._uid[0] += 1
        return self.stack.enter_context(self.nc.psum_tensor("%s_%d" % (name or "ps", Ctx._uid[0]), list(shape), dt))

    def ring_sb(self, n, shape, dt, name=None):
        return Ring([self.sb(shape, dt, name) for _ in range(n)])

    def ring_ps(self, n, shape, dt=F32, name=None):
        return Ring([self.ps(shape, dt, name) for _ in range(n)])

    def dram(self, name, shape, dt):
        return self.nc.dram_tensor(name, list(shape), dt).ap()


TOK_TILES = [(0, 256), (256, 512), (768, 512), (1280, 512), (1792, 512)]
NORM_TILES = [(c, 256) for c in range(0, 2304, 256)]


class Builder:
    def __init__(self, nc, stack, debug=()):
        self.nc = nc
        self.C = Ctx(nc, stack)
        self.P = self.C.P
        self.P.setup(stack)
        self.debug = debug
        self.inp = {}
        self.dres = {}

    def din(self, name, shape, dt=F32):
        if any(x in self.debug for x in ("only_gdn", "only_ml", "only_attn", "only_sg")) and int(np.prod(shape)) > 4000000:
            shape = [1] * len(shape)
        ap = self.nc.dram_tensor(name, list(shape), dt, kind="ExternalInput").ap()
        self.inp[name] = ap
        return ap

    def dscr(self, name, shape, dt=F32):
        kind = "ExternalOutput" if name in self.debug else "Internal"
        return self.nc.dram_tensor(name, list(shape), dt, kind=kind).ap()

    def R(self, key):
        r = self.dres.get(key)
        if r is None:
            r = self.dres[key] = Res(str(key))
        return r

    def phase(self):
        b = self

        class Ph:
            def __enter__(s):
                s.st = ExitStack()
                s.st.__enter__()
                s.C = Ctx(b.nc, s.st)
                s.C.P = b.P
                return s.C

            def __exit__(s, *a):
                if a[0] is None:
                    b.P.emit()
                return s.st.__exit__(*a)

        return Ph()


def build_program(debug=(), stop_after=None, nlayers=DEPTH):
    nc = bass.Bass("TRN2", target_bir_lowering=False)
    top = ExitStack()
    with top:
        B = Builder(nc, top, debug)
        P = B.P
        R = B.R
        xT_in = B.din("xT", [D, TOK])
        cvec = B.din("cvec", [128, KC, 2])
        mod_w = B.din("mod_w", [DEPTH, D, 9 * D])
        mod_b = B.din("mod_b", [DEPTH, 128, 144])
        norms = B.din("norms", [DEPTH, 128, 3, KC])
        ffn_w_in = [B.din("ffn1_w_in", [DEPTH, D, 2 * DFF]), B.din("ffn2_w_in", [DEPTH, D, 2 * DFF])]
        ffn_w_out = [B.din("ffn1_w_out", [DEPTH, DFF, D]), B.din("ffn2_w_out", [DEPTH, DFF, D])]
        w_in_all = B.din("w_in", [DEPTH, D, IN_COLS])
        w_branch = B.din("w_branch", [DEPTH, 4, 512, D])
        w_out_all = B.din("w_out", [DEPTH, D, D])
        sgwT = B.din("sgwT", [DEPTH, 4, 128, 128])
        sg_b = B.din("sg_b", [DEPTH, 4, 128])
        vecs = B.din("vecs", [128, DEPTH, 12])
        mlb_in = B.din("mlb", [128, DEPTH, 16])
        gdc_in = B.din("gdc", [128, DEPTH, 16])
        cw_in = B.din("cw", [128, DEPTH, 3, 12])
        lm_in = B.din("lm", [64, 7, 64])
        consts = B.din("consts", [128, 4, 128])
        ropetab = B.din("ropetab", [128, 2, T_LAT])
        outT = nc.dram_tensor("outT", [D, T_LAT], F32, kind="ExternalOutput").ap()
        Xs = [B.dscr("X%d" % i, [D, TOK]) for i in range(3)]
        GT = B.dscr("GT", [DFF, TOK], BF16)
        if any(x in debug for x in ("only_gdn", "only_ml", "only_attn", "only_sg")):
            ZT = nc.dram_tensor("ZT", [IN_COLS, TOK], F32, kind="ExternalInput").ap()
        else:
            ZT = B.dscr("ZT", [IN_COLS, TOK])
        YT = B.dscr("YT", [4, 512, TOK], BF16)

        PC = B.C
        ones_f = PC.sb([128, 128], F32, "ones_f"); r_const = Res()
        modT = PC.sb([128, 144, 2], F32, "modT"); r_mod = Res()
        Avec = PC.sb([128, 3, KC, 2], F32, "Avec")
        Hg = PC.sb([128, 3, KC, 2], F32, "Hg")
        nrm = PC.sb([128, DEPTH, 3, KC], F32, "nrm")
        epst = PC.sb([128, 1], F32, "epst")
        ones_b = PC.sb([128, 128], BF16, "ones_b")
        cst = PC.sb([128, 4, 128], F32, "cst")
        mlb = PC.sb([128, DEPTH, 16], F32, "mlb")
        gdc = PC.sb([128, DEPTH, 16], F32, "gdc")
        cw = PC.sb([128, DEPTH, 3, 12], F32, "cw")
        lm = PC.sb([64, 7, 64], F32, "lm")
        ident_b = PC.sb([128, 128], BF16, "ident_b")
        ident_f = cst[:, 0, :]
        pm_f = cst[:, 1, :]
        ROPE = {}
        vec_t = PC.sb([128, DEPTH, 12], F32, "vec_t")
        atn = vec_t
        sgn = vec_t[:, :, 4:8]

        with B.phase() as C:
            P.add("vector", lambda e: e.memset(ones_f[:], 1.0), [], [r_const])
            P.add("vector", lambda e: e.memset(epst[:], EPS), [], [r_const])
            P.dma("sync", nrm[:], norms.rearrange("l p i k -> p l i k"), [], [r_const])
            P.dma("sync", cst[:], consts, [], [r_const])
            P.dma("sync", vec_t[:], vecs, [], [r_const])
            P.dma("sync", mlb[:], mlb_in, [], [r_const])
            P.dma("sync", gdc[:], gdc_in, [], [r_const])
            P.dma("sync", cw[:], cw_in, [], [r_const])
            P.dma("sync", lm[:], lm_in, [], [r_const])
            P.add("vector", lambda e: e.memset(ones_b[:], 1.0), [], [r_const])
            P.add("vector", lambda e: e.tensor_copy(ident_b[:], cst[:, 0, :]), [r_const], [r_const])

        def mod_phase(l):
            with B.phase() as C:
                sc = C.sb([128, KC, 2], F32, "sc"); r_sc = Res()
                mb = C.sb([128, 144], F32, "mb"); r_mb = Res()
                wr = C.ring_sb(2, [128, KC, 512], F32, "modw")
                pr = C.ring_ps(2, [128, 2], F32, "modps")
                P.dma("sync", sc[:], cvec, [], [r_sc])
                P.dma("sync", mb[:], mod_b[l], [], [r_mb])
                P.act("scalar", sc[:], sc[:], AF.Silu, [r_sc], [r_sc])
                for blk in range(36):
                    wt, rw = wr.next()
                    P.dma("sync" if blk % 2 == 0 else "scalar", wt[:],
                          mod_w[l, :, blk * 512:(blk + 1) * 512].rearrange("(kc p) n -> p kc n", p=128), [], [rw])
                    for s in range(4):
                        ps, rp = pr.next()
                        for kc in range(KC):
                            P.mm(ps[:], wt[:, kc, s * 128:(s + 1) * 128], sc[:, kc, :], kc == 0, kc == KC - 1,
                                 [rw, r_sc], [rp])
                        ch = blk * 4 + s
                        P.add("vector", lambda e, ps=ps, ch=ch: e.tensor_scalar(
                            modT[:, ch, :], ps[:], mb[:, ch:ch + 1], None, ALU.add), [rp, r_mb], [r_mod])
                for i in range(3):
                    P.add("vector", lambda e, i=i: e.tensor_scalar(
                        Avec[:, i, :, :], modT[:, (3 * i + 1) * KC:(3 * i + 2) * KC, :], 1.0, None, ALU.add),
                        [r_mod], [r_mod])
                    for j in range(2):
                        P.add("vector", lambda e, i=i, j=j: e.tensor_tensor(
                            Avec[:, i, :, j], Avec[:, i, :, j], nrm[:, l, i, :], ALU.mult), [r_mod, r_const], [r_mod])
                    P.add("vector", lambda e, i=i: e.tensor_scalar(
                        Hg[:, i, :, :], modT[:, (3 * i + 2) * KC:(3 * i + 3) * KC, :], 0.5 if i != 1 else 1.0, None,
                        ALU.mult), [r_mod], [r_mod])

        def norm_tiles(C, Xsrc, xkey, i, xn, r_xn, tiles=NORM_TILES):
            xr = C.ring_sb(2, [128, KC, 256], F32, "xt")
            sq = C.ring_sb(3, [128, 512], F32, "sq")
            tmp = C.ring_sb(3, [128, 512], F32, "ntmp")
            rs_ = C.ring_sb(2, [128, 512], F32, "rstd")
            pss = C.ring_ps(2, [128, 512], F32, "ssq")
            for ti, (c0, w) in enumerate(tiles):
                j = 1 if c0 < T_CTX else 0
                xt, rx = xr.next()
                P.dma("sync", xt[:, :, :w], Xsrc[:, c0:c0 + w].rearrange("(kc p) n -> p kc n", p=128),
                      [R((xkey, kc_)) for kc_ in range(KC)], [rx])
                ps, rp = pss.next()
                for kc in range(KC):
                    s, rsq = sq.next()
                    P.act("scalar", s[:, :w], xt[:, kc, :w], AF.Square, [rx], [rsq])
                    P.mm(ps[:, :w], ones_f[:], s[:, :w], kc == 0, kc == KC - 1, [r_const, rsq], [rp])
                rstd, rr = rs_.next()
                P.act("scalar", rstd[:, :w], ps[:, :w], AF.Sqrt, [rp, r_const], [rr], bias=epst[:, 0:1], scale=1.0 / D)
                P.add("vector", lambda e, rstd=rstd, w=w: e.reciprocal(rstd[:, :w], rstd[:, :w]), [rr], [rr])
                for kc in range(KC):
                    t, rt = tmp.next()
                    P.tt("vector", t[:, :w], xt[:, kc, :w], rstd[:, :w], ALU.mult, [rx, rr], [rt])
                    if kc % 2:
                        P.ts("vector", xn[:, kc, c0:c0 + w], t[:, :w], Avec[:, i, kc, j:j + 1], modT[:, 3 * i * KC + kc, j:j + 1],
                             ALU.mult, ALU.add, [rt, r_mod], [r_xn])
                    else:
                        P.act("scalar", xn[:, kc, c0:c0 + w], t[:, :w], AF.Identity, [rt, r_mod], [r_xn],
                              bias=modT[:, 3 * i * KC + kc, j:j + 1], scale=Avec[:, i, kc, j:j + 1])

        def ffn(l, f, Xsrc, xkey, Xdst, dkey, final_out=False):
            i = 0 if f == 0 else 2
            w_in = ffn_w_in[f]
            w_out = ffn_w_out[f]
            with B.phase() as C:
                xn = C.sb([128, KC, TOK], BF16, "xn"); r_xn = Res()
                ftiles = TOK_TILES[1:] if final_out else TOK_TILES
                norm_tiles(C, Xsrc, xkey, i, xn, r_xn, tiles=(NORM_TILES[1:] if final_out else NORM_TILES))
                war = C.ring_sb(2, [128, KC, 256], BF16, "wa")
                wbr = C.ring_sb(2, [128, KC, 256], BF16, "wb")
                pa = C.ring_ps(2, [128, 512], F32, "pa")
                pb = C.ring_ps(2, [128, 512], F32, "pb")
                sar = C.ring_sb(3, [128, 512], F32, "sa")
                gst = C.ring_sb(2, [128, TOK], BF16, "gst")
                for blk in range(DFF // 256):
                    wa, rwa = war.next()
                    wb, rwb = wbr.next()
                    P.dma("gpsimd", wa[:], w_in[l, :, blk * 256:(blk + 1) * 256].rearrange("(kc p) n -> p kc n", p=128),
                          [], [rwa])
                    P.dma("gpsimd", wb[:], w_in[l, :, DFF + blk * 256:DFF + (blk + 1) * 256].rearrange(
                        "(kc p) n -> p kc n", p=128), [], [rwb])
                    for s_ in range(2):
                        g, rg = gst.next()
                        for (c0, w) in ftiles:
                            a, ra = pa.next()
                            b_, rb = pb.next()
                            for kc in range(KC):
                                P.mm(a[:, :w], wa[:, kc, s_ * 128:(s_ + 1) * 128], xn[:, kc, c0:c0 + w], kc == 0,
                                     kc == KC - 1, [rwa, r_xn], [ra])
                            for kc in range(KC):
                                P.mm(b_[:, :w], wb[:, kc, s_ * 128:(s_ + 1) * 128], xn[:, kc, c0:c0 + w], kc == 0,
                                     kc == KC - 1, [rwb, r_xn], [rb])
                            sa, rsa = sar.next()
                            P.act("scalar", sa[:, :w], a[:, :w], AF.Silu, [ra], [rsa])
                            P.add("vector", lambda e, g=g, sa=sa, b_=b_, c0=c0, w=w: e.tensor_tensor(
                                g[:, c0:c0 + w], sa[:, :w], b_[:, :w], ALU.mult), [rsa, rb], [rg])
                        row = blk * 256 + s_ * 128
                        P.dma("sync", GT[row:row + 128, :], g[:], [rg], [R(("GT", row // 128))])
            groups = [[(0, 256), (256, 512), (768, 512)], [(1280, 512), (1792, 512)]]
            if final_out:
                groups = [[(256, 512), (768, 512)], [(1280, 512), (1792, 512)]]
            with B.phase() as C:
                NK = DFF // 128
                gt = C.sb([128, NK, 1280], BF16, "gt"); r_gt = Res()
                wor = C.ring_sb(3, [128, NK, 256], BF16, "wo")
                py = C.ring_ps(3, [128, 512], F32, "py")
                xrr = C.ring_sb(3, [128, 512], F32, "xres")
                orr = C.ring_sb(3, [128, 512], F32, "ores")
                for grp in groups:
                    g0 = grp[0][0]
                    gw = sum(w for _, w in grp)
                    P.dma("sync", gt[:, :, :gw], GT[:, g0:g0 + gw].rearrange("(k p) n -> p k n", p=128),
                          [R(("GT", k)) for k in range(NK)], [r_gt])
                    for nb in range(D // 256):
                        wo, rwo = wor.next()
                        P.dma("gpsimd", wo[:], w_out[l, :, nb * 256:(nb + 1) * 256].rearrange("(k p) n -> p k n", p=128),
                              [], [rwo])
                        for s_ in range(2):
                            fc = nb * 2 + s_
                            for (c0, w) in grp:
                                j = 1 if c0 < T_CTX else 0
                                y, ry = py.next()
                                for k in range(NK):
                                    P.mm(y[:, :w], wo[:, k, s_ * 128:(s_ + 1) * 128], gt[:, k, c0 - g0:c0 - g0 + w],
                                         k == 0, k == NK - 1, [rwo, r_gt], [ry])
                                xr_, rxr = xrr.next()
                                P.dma("scalar", xr_[:, :w], Xsrc[fc * 128:(fc + 1) * 128, c0:c0 + w],
                                      [R((xkey, fc))], [rxr])
                                o, ro = orr.next()
                                P.add("vector", lambda e, o=o, y=y, xr_=xr_, fc=fc, j=j, w=w: e.scalar_tensor_tensor(
                                    o[:, :w], y[:, :w], Hg[:, i, fc, j:j + 1], xr_[:, :w], ALU.mult, ALU.add),
                                    [ry, rxr, r_mod], [ro])
                                if final_out:
                                    if c0 >= T_CTX:
                                        P.dma("sync", outT[fc * 128:(fc + 1) * 128, c0 - T_CTX:c0 - T_CTX + w], o[:, :w],
                                              [ro], [R(("out", fc))])
                                else:
                                    P.dma("sync", Xdst[fc * 128:(fc + 1) * 128, c0:c0 + w], o[:, :w],
                                          [ro], [R((dkey, fc))])

        def inproj(l, Xsrc, xkey):
            with B.phase() as C:
                xn = C.sb([128, KC, TOK], BF16, "xn"); r_xn = Res()
                norm_tiles(C, Xsrc, xkey, 1, xn, r_xn)
                wr = C.ring_sb(2, [128, KC, 256], BF16, "wi")
                pz = C.ring_ps(4, [128, 512], F32, "pz")
                zst = C.ring_sb(2, [128, TOK], F32, "zst")
                cnt = 0
                for c0 in range(0, IN_COLS, 256):
                    ncol = min(256, IN_COLS - c0)
                    wt, rw = wr.next()
                    P.dma("gpsimd", wt[:, :, :ncol], w_in_all[l, :, c0:c0 + ncol].rearrange("(kc p) n -> p kc n", p=128),
                          [], [rw])
                    for s0 in range(0, ncol, 128):
                        m = min(128, ncol - s0)
                        z, rz = zst.next()
                        for (t0, w) in TOK_TILES:
                            ps, rp = pz.next()
                            for kc in range(KC):
                                P.mm(ps[:m, :w], wt[:, kc, s0:s0 + m], xn[:, kc, t0:t0 + w], kc == 0, kc == KC - 1,
                                     [rw, r_xn], [rp])
                            cnt += 1
                            if cnt % 2:
                                P.act("scalar", z[:m, t0:t0 + w], ps[:m, :w], AF.Copy, [rp], [rz])
                            else:
                                P.add("vector", lambda e, z=z, ps=ps, m=m, t0=t0, w=w: e.tensor_copy(
                                    z[:m, t0:t0 + w], ps[:m, :w]), [rp], [rz])
                        row = c0 + s0
                        P.dma("sync", ZT[row:row + m, :], z[:m, :], [rz], [R("ZT")])

        KNRES = {}

        def qk_norm_rope(C, rows0, gain_ap, dst, r_dst, rings):
            zr, sqr, p6, rsr, knr, t1r = rings
            zt, rz = zr.next()
            P.dma("sync", zt[:], ZT[rows0:rows0 + 128, :], [R("ZT")], [rz])
            kn, _ = knr.next()

            def tile(t0, w):
                rkn = KNRES.setdefault((kn.name, t0), Res())
                s, rsq = sqr.next()
                P.act("scalar", s[:, :w], zt[:, t0:t0 + w], AF.Square, [rz], [rsq])
                yield
                ps, rp = p6.next()
                P.mm(ps[:, :w], ones_f[:], s[:, :w], True, True, [r_const, rsq], [rp])
                yield
                rstd, rr = rsr.next()
                P.act("scalar", rstd[:, :w], ps[:, :w], AF.Sqrt, [rp, r_const], [rr], bias=epst[:, 0:1], scale=1.0 / 128)
                yield
                P.add("vector", lambda e: e.reciprocal(rstd[:, :w], rstd[:, :w]), [rr], [rr])
                yield
                P.stt("vector", kn[:, t0:t0 + w], zt[:, t0:t0 + w], gain_ap, rstd[:, :w], ALU.mult, ALU.mult,
                      [rz, rr, r_const], [rkn])
                yield
                if t0 < T_CTX:
                    P.cp("vector", dst[:, t0:t0 + w], kn[:, t0:t0 + w], [rkn], [r_dst])
                else:
                    pr, rpr = p6.next()
                    P.mm(pr[:, :w], pm_f[:], kn[:, t0:t0 + w], True, True, [r_const, rkn], [rpr])
                    t1, rt1 = t1r.next()
                    l0 = t0 - T_CTX
                    P.tt("gpsimd", t1[:, :w], kn[:, t0:t0 + w], ROPE['cos'][:, l0:l0 + w], ALU.mult, [rkn, ROPE['r']], [rt1])
                    yield
                    t2, rt2 = t1r.next()
                    P.tt("vector", t2[:, :w], pr[:, :w], ROPE['sin'][:, l0:l0 + w], ALU.mult, [rpr, ROPE['r']], [rt2])
                    yield
                    P.tt("vector", dst[:, t0:t0 + w], t1[:, :w], t2[:, :w], ALU.add, [rt1, rt2], [r_dst])
                yield

            lockstep([tile(t0, w) for (t0, w) in TOK_TILES])

        def attn_phase(l):
            ZK, ZV, ZQ = 2080, 2336, 5664
            with B.phase() as C:
                rtab = C.sb([128, 2, T_LAT], F32, "rtab")
                ROPE['cos'] = rtab[:, 0, :]; ROPE['sin'] = rtab[:, 1, :]; ROPE['r'] = Res()
                P.dma("sync", rtab[:], ropetab, [], [ROPE['r']])
                kT = [C.sb([128, TOK], BF16, "kT") for _ in range(2)]; r_k = [Res(), Res()]
                qT = [C.sb([128, TOK], BF16, "qT") for _ in range(4)]; r_q = [Res() for _ in range(4)]
                vtm = [C.sb([128, 18, 128], BF16, "vtm") for _ in range(2)]; r_v = [Res(), Res()]
                p6 = C.ring_ps(6, [128, 512], F32, "p6")
                psr = Ring(p6.tiles[2:4]); psr.res = p6.res[2:4]
                po, r_po = p6.tiles[4], p6.res[4]
                pd, r_pd = p6.tiles[5], p6.res[5]
                rings = (C.ring_sb(2, [128, TOK], F32, "zr"), C.ring_sb(5, [128, 512], F32, "sq"),
                         p6, C.ring_sb(5, [128, 512], F32, "rs"),
                         C.ring_sb(2, [128, TOK], F32, "kn"),
                         C.ring_sb(10, [128, 512], F32, "t1"))
                for hk in range(2):
                    qk_norm_rope(C, ZK + hk * 128, atn[:, l, 1:2], kT[hk], r_k[hk], rings)
                for h in range(4):
                    qk_norm_rope(C, ZQ + h * 128, atn[:, l, 0:1], qT[h], r_q[h], rings)
                vb = C.ring_sb(2, [128, TOK], BF16, "vb")
                ptr = C.ring_ps(2, [128, 128], BF16, "ptr")
                for hk in range(2):
                    zt, rz = rings[0].next()
                    P.dma("sync", zt[:], ZT[ZV + hk * 128:ZV + (hk + 1) * 128, :], [R("ZT")], [rz])
                    v, rv = vb.next()
                    P.add("vector", lambda e, v=v, zt=zt: e.tensor_copy(v[:], zt[:]), [rz], [rv])
                    for kc in range(18):
                        pt, rpt = ptr.next()
                        P.tr(pt[:], v[:, kc * 128:(kc + 1) * 128], ident_b[:], [rv, r_const], [rpt])
                        P.add("vector", lambda e, pt=pt, hk=hk, kc=kc: e.tensor_copy(
                            vtm[hk][:, kc, :], pt[:]), [rpt], [r_v[hk]])
                er = C.ring_sb(3, [128, 512], BF16, "e")
                rdr = C.ring_sb(2, [128, 512], F32, "rden")
                yst = C.ring_sb(2, [128, 512], BF16, "yst")
                for h in range(4):
                    hk = h // 2
                    for (t0, w) in TOK_TILES:
                        nkc = 2 if t0 < T_CTX else 18
                        for kc in range(nkc):
                            ps, rp = psr.next()
                            P.mm(ps[:, :w], kT[hk][:, kc * 128:(kc + 1) * 128], qT[h][:, t0:t0 + w], True, True,
                                 [r_k[hk], r_q[h]], [rp])
                            e_, re_ = er.next()
                            P.act("scalar", e_[:, :w], ps[:, :w], AF.Exp, [rp], [re_], scale=128 ** -0.5)
                            P.mm(po[:, :w], vtm[hk][:, kc, :], e_[:, :w], kc == 0, kc == nkc - 1, [r_v[hk], re_], [r_po])
                            P.mm(pd[:, :w], ones_b[:], e_[:, :w], kc == 0, kc == nkc - 1, [r_const, re_], [r_pd])
                        rd, rrd = rdr.next()
                        P.add("vector", lambda e, rd=rd, w=w: e.reciprocal(rd[:, :w], pd[:, :w]), [r_pd], [rrd])
                        y, ry = yst.next()
                        P.add("vector", lambda e, y=y, rd=rd, w=w: e.tensor_tensor(y[:, :w], po[:, :w], rd[:, :w], ALU.mult),
                              [r_po, rrd], [ry])
                        P.dma("sync", YT[3, h * 128:(h + 1) * 128, t0:t0 + w], y[:, :w], [ry], [R(("YT", 3))])

        def gelu_tiles(C, src, r_src, dst, r_dst, rings):
            g1 = rings
            for (t0, w) in TOK_TILES:
                a, ra = g1.next()
                P.act("scalar", a[:, :w], src[:, t0:t0 + w], AF.Square, [r_src], [ra])
                P.add("vector", lambda e, a=a, w=w: e.tensor_scalar(a[:, :w], a[:, :w], 0.044715, 1.0, ALU.mult, ALU.add),
                      [ra], [ra])
                P.add("vector", lambda e, a=a, w=w, t0=t0: e.tensor_tensor(a[:, :w], a[:, :w], src[:, t0:t0 + w], ALU.mult),
                      [ra, r_src], [ra])
                P.act("scalar", a[:, :w], a[:, :w], AF.Sigmoid, [ra], [ra], scale=1.5957691216057308)
                P.add("vector", lambda e, a=a, w=w, t0=t0: e.tensor_tensor(dst[:, t0:t0 + w], a[:, :w], src[:, t0:t0 + w],
                                                                      ALU.mult), [ra, r_src], [r_dst])

        def sg_phase(l):
            ZU, ZVV = 2592, 3104
            with B.phase() as C:
                g1 = C.ring_sb(3, [128, 512], F32, "g1")
                zr = C.ring_sb(2, [128, TOK], F32, "zr")
                gu = [C.sb([128, TOK], F32, "gu") for _ in range(4)]; r_gu = [Res() for _ in range(4)]
                gv = [C.sb([128, TOK], F32, "gv") for _ in range(4)]; r_gv = [Res() for _ in range(4)]
                wst = C.sb([128, 4, 128], BF16, "wst"); r_w = Res()
                bs = C.sb([1, 4, 128], BF16, "bs"); r_b = Res()
                P.dma("gpsimd", wst[:], sgwT[l].rearrange("g s t -> s g t"), [], [r_w])
                P.dma("gpsimd", bs[:], sg_b[l:l + 1], [], [r_b])
                for g in range(4):
                    zt, rz = zr.next()
                    P.dma("sync", zt[:], ZT[ZU + g * 128:ZU + (g + 1) * 128, :], [R("ZT")], [rz])
                    gelu_tiles(C, zt, rz, gu[g], r_gu[g], g1)
                    zt, rz = zr.next()
                    P.dma("sync", zt[:], ZT[ZVV + g * 128:ZVV + (g + 1) * 128, :], [R("ZT")], [rz])
                    gelu_tiles(C, zt, rz, gv[g], r_gv[g], g1)
                rstd = C.sb([128, TOK], F32, "rstd"); r_rs = Res()
                pss = C.ring_ps(2, [128, 512], F32, "pss")
                for (t0, w) in TOK_TILES:
                    ps, rp = pss.next()
                    for g in range(4):
                        a, ra = g1.next()
                        P.act("scalar", a[:, :w], gv[g][:, t0:t0 + w], AF.Square, [r_gv[g]], [ra])
                        P.mm(ps[:, :w], ones_f[:], a[:, :w], g == 0, g == 3, [r_const, ra], [rp])
                    P.act("scalar", rstd[:, t0:t0 + w], ps[:, :w], AF.Sqrt, [rp, r_const], [r_rs], bias=epst[:, 0:1],
                          scale=1.0 / 512)
                    P.add("vector", lambda e, t0=t0, w=w: e.reciprocal(rstd[:, t0:t0 + w], rstd[:, t0:t0 + w]), [r_rs], [r_rs])
                vnr = C.ring_sb(2, [128, TOK], BF16, "vn")
                ptr = C.ring_ps(3, [128, 128], BF16, "ptr")
                vtr = C.ring_sb(3, [128, 128], BF16, "vt")
                pso = C.ring_ps(3, [128, 128], F32, "pso")
                ysr = C.ring_sb(2, [128, TOK], BF16, "ys")
                for g in range(4):
                    vn, rvn = vnr.next()
                    P.stt("vector", vn[:], gv[g][:], sgn[:, l, g:g + 1], rstd[:], ALU.mult, ALU.mult, [r_gv[g], r_rs, r_const], [rvn])
                    ys, rys = ysr.next()

                    def sgu(g, n, vn, rvn, ys, rys):
                        pt, rpt = ptr.next()
                        P.tr(pt[:], vn[:, n * 128:(n + 1) * 128], ident_b[:], [rvn, r_const], [rpt])
                        yield
                        vt, rvt = vtr.next()
                        P.cp("vector", vt[:], pt[:], [rpt], [rvt])
                        yield
                        po_, rpo = pso.next()
                        P.mm(po_[:], vt[:], wst[:, g, :], True, False, [rvt, r_w], [rpo])
                        P.mm(po_[:], ones_b[0:1, :], bs[0:1, g, :], False, True, [r_const, r_b], [rpo])
                        yield
                        P.tt("vector", ys[:, n * 128:(n + 1) * 128], po_[:], gu[g][:, n * 128:(n + 1) * 128], ALU.mult,
                             [rpo, r_gu[g]], [rys])
                        yield
                    for n0 in range(0, 18, 3):
                        lockstep([sgu(g, n, vn, rvn, ys, rys) for n in range(n0, n0 + 3)])
                    P.dma("sync", YT[0, g * 128:(g + 1) * 128, :], ys[:], [rys], [R(("YT", 0))])

        def lockstep(gens):
            gens = list(gens)
            while gens:
                alive = []
                for g_ in gens:
                    try:
                        next(g_)
                        alive.append(g_)
                    except StopIteration:
                        pass
                gens = alive

        def mlstm_phase(l):
            ZK, ZV_, ZIF, ZQ, ZO = 0, 512, 1024, 3616, 4128
            with B.phase() as C:
                prs = [C.ring_ps(2, [128, 512], F32, "mp") for _ in range(4)]
                pr = prs[0]
                zr = C.ring_sb(2, [128, TOK], F32, "zr")
                ift_t, r_ift = zr.next()
                P.dma("sync", ift_t[0:16, :], ZT[ZIF:ZIF + 16, :], [R("ZT")], [r_ift])
                if_tm = C.sb([128, 18, 16], F32, "if_tm"); r_if = Res()
                lf_tm = C.sb([128, 18, 8], F32, "lf_tm"); r_lf = Res()
                b_tm = C.sb([128, 18, 8], F32, "b_tm"); r_b = Res()
                w_tm = C.sb([128, 18, 8], F32, "w_tm"); r_w = Res()
                for n in range(18):
                    pt, rpt = pr.next()
                    P.mm(pt[:, 0:16], ift_t[0:16, n * 128:(n + 1) * 128], cst[0:16, 0, 0:16], True, True, [r_ift, r_const], [rpt])
                    P.tt("vector", if_tm[:, n, :], pt[:, 0:16], mlb[:, l, :], ALU.add, [rpt, r_const], [r_if])
                P.act("scalar", lf_tm[:], if_tm[:, :, 8:16], AF.Exp, [r_if], [r_lf], scale=-1.0)
                P.act("scalar", lf_tm[:], lf_tm[:], AF.Ln, [r_lf, r_const], [r_lf], bias=ones_f[:, 0:1])
                P.ts("vector", lf_tm[:], lf_tm[:], -1.0, None, ALU.mult, None, [r_lf], [r_lf])
                for n in range(18):
                    pt, rpt = pr.next()
                    P.mm(pt[:, 0:4], cst[:, 3, :], lf_tm[:, n, 0:4], True, True, [r_const, r_lf], [rpt])
                    P.mm(pt[:, 4:8], cst[:, 2, :], lf_tm[:, n, 4:8], True, True, [r_const, r_lf], [rpt])
                    P.cp("vector", b_tm[:, n, :], pt[:, 0:8], [rpt], [r_b])
                P.tt("vector", w_tm[:], if_tm[:, :, 0:8], b_tm[:], ALU.subtract, [r_if, r_b], [r_w])
                P.act("scalar", w_tm[:], w_tm[:], AF.Exp, [r_w], [r_w])
                kTs = [C.sb([128, TOK], BF16, "kT") for _ in range(2)]; r_kT = [Res(), Res()]
                qTs = [C.sb([128, TOK], BF16, "qT") for _ in range(2)]; r_qT = [Res(), Res()]
                ktms = [C.sb([128, 18, 128], BF16, "ktm") for _ in range(2)]; r_ktm = [Res(), Res()]
                vtms = [C.sb([128, 18, 128], BF16, "vtm") for _ in range(2)]; r_vtm = [Res(), Res()]
                hTs = [C.sb([128, 2, TOK], BF16, "hT") for _ in range(2)]; r_hT = [Res(), Res()]
                hsum = C.ring_sb(1, [128, TOK], F32, "hsum")
                sqr = C.ring_sb(2, [128, 512], F32, "sq")
                rsr = C.ring_sb(2, [128, 512], F32, "rs")
                ysr = C.ring_sb(1, [128, TOK], BF16, "ys")

                class T_:
                    pass
                chains_t = []
                for ci in range(4):
                    t = T_()
                    mk = lambda name, shape, dt, n=2: C.ring_sb(n, shape, dt, name)
                    t.lfr = mk("lfrep", [128, 128], F32); t.er = mk("erep", [128, 128], F32)
                    t.ptr = mk("ptm", [128, 128], BF16); t.vpr = mk("vp", [128, 129], BF16)
                    t.wrr = mk("wrep", [128, 128], BF16); t.t1r = mk("t1", [128, 128], F32); t.t2r = mk("t2", [128, 128], F32)
                    t.CTa = C.sb([128, 129], F32, "CTa"); t.r_CT = Res()
                    t.CTb = C.sb([128, 128], BF16, "CTb"); t.r_CTb = Res()
                    t.Nrep = C.sb([128, 128], BF16, "Nrep"); t.r_N = Res()
                    t.pr = prs[ci]
                    chains_t.append(t)
                orders = [list(range(18)), [1, 0] + list(range(17, 1, -1))]

                def unit(t, hi, h, d_, c):
                    kT, rk = kTs[hi], r_kT[hi]
                    qT, rq = qTs[hi], r_qT[hi]
                    ktm, rktm = ktms[hi], r_ktm[hi]
                    vtm, rvtm = vtms[hi], r_vtm[hi]
                    hT, rh = hTs[hi], r_hT[hi]
                    pr_ = t.pr
                    cs = slice(c * 128, (c + 1) * 128)
                    col = d_ * 4 + h
                    U = cst[:, 3, :] if d_ == 0 else cst[:, 2, :]
                    wcol = w_tm[:, c, col:col + 1]
                    lfp, rlfp = t.lfr.next()
                    P.act("scalar", lfp[:], ones_f[:], AF.Identity, [r_const, r_lf], [rlfp], scale=lf_tm[:, c, col:col + 1])
                    vp, rvp = t.vpr.next()
                    P.ts("vector", vp[:, 0:128], vtm[:, c, :], wcol, None, ALU.mult, None, [rvtm, r_w], [rvp])
                    P.cp("gpsimd", vp[:, 128:129], wcol, [r_w], [rvp])
                    wrep, rwr = t.wrr.next()
                    P.act("scalar", wrep[:], ones_f[:], AF.Identity, [r_const, r_w], [rwr], scale=wcol)
                    pB, rpB = pr_.next()
                    P.mm(pB[:, 0:128], kT[:, cs], qT[:, cs], True, True, [rk, rq], [rpB])
                    yield
                    pA, rpA = pr_.next()
                    P.mm(pA[:, 0:128], lfp[:], U, True, True, [rlfp, r_const], [rpA])
                    ptm, rptm = t.ptr.next()
                    P.tt("vector", ptm[:], pB[:, 0:128], U, ALU.mult, [rpB, r_const], [rptm])
                    yield
                    erep, rer = t.er.next()
                    P.act("scalar", erep[:], pA[:, 0:128], AF.Exp, [rpA], [rer])
                    eb = erep[:, 127:128] if d_ == 0 else erep[:, 0:1]
                    pC, rpC = pr_.next()
                    P.mm(pC[:, 0:128], vp[:, 0:128], ptm[:], True, False, [rvp, rptm], [rpC])
                    P.mm(pC[:, 0:128], t.CTb[:], qT[:, cs], False, True, [t.r_CTb, rq], [rpC])
                    yield
                    pD, rpD = pr_.next()
                    P.mm(pD[:, 0:128], wrep[:], ptm[:], True, False, [rwr, rptm], [rpD])
                    P.mm(pD[:, 0:128], t.Nrep[:], qT[:, cs], False, True, [t.r_N, rq], [rpD])
                    t1, rt1 = t.t1r.next()
                    P.tt("vector", t1[:], pC[:, 0:128], erep[:], ALU.mult, [rpC, rer], [rt1])
                    yield
                    t2, rt2 = t.t2r.next()
                    P.tt("vector", t2[:], pD[:, 0:128], erep[:], ALU.mult, [rpD, rer], [rt2])
                    pE, rpE = pr_.next()
                    P.mm(pE[:, 0:129], ktm[:, c, :], vp[:], True, True, [rktm, rvp], [rpE])
                    yield
                    P.act("scalar", t2[:], t2[:], AF.Abs, [rt2], [rt2])
                    P.tt("vector", t.CTa[:], t.CTa[:], pE[:, 0:129], ALU.add, [rpE, t.r_CT], [t.r_CT])
                    yield
                    P.ts("vector", t2[:], t2[:], 1.0, None, ALU.max, None, [rt2], [rt2])
                    P.ts("vector", t.CTa[:], t.CTa[:], eb, None, ALU.mult, None, [t.r_CT, rer], [t.r_CT])
                    yield
                    P.add("vector", lambda e, t2=t2: e.reciprocal(t2[:], t2[:]), [rt2], [rt2])
                    P.act("scalar", t.CTb[:], t.CTa[:, 0:128], AF.Copy, [t.r_CT], [t.r_CTb])
                    P.ts("gpsimd", t.Nrep[:], ones_f[:], t.CTa[:, 128:129], None, ALU.mult, None, [r_const, t.r_CT], [t.r_N])
                    yield
                    P.tt("vector", hT[:, d_, cs], t1[:], t2[:], ALU.mult, [rt1, rt2], [rh])
                    yield

                def chain(t, hi, h, d_):
                    for step in range(18):
                        yield from unit(t, hi, h, d_, orders[d_][step])

                for heads in [(0, 1), (2, 3)]:
                    for hi, h in enumerate(heads):
                        zt, rz = zr.next()
                        P.dma("sync", zt[:], ZT[ZK + h * 128:ZK + (h + 1) * 128, :], [R("ZT")], [rz])
                        P.act("scalar", kTs[hi][:], zt[:], AF.Copy, [rz], [r_kT[hi]], scale=128 ** -0.5)
                        for n in range(18):
                            pt, rpt = prs[n % 4].next()
                            P.mm(pt[:, 0:128], zt[:, n * 128:(n + 1) * 128], cst[:, 0, :], True, True, [rz, r_const], [rpt])
                            if n % 2:
                                P.act("scalar", ktms[hi][:, n, :], pt[:, 0:128], AF.Copy, [rpt], [r_ktm[hi]], scale=128 ** -0.5)
                            else:
                                P.ts("vector", ktms[hi][:, n, :], pt[:, 0:128], 128 ** -0.5, None, ALU.mult, None, [rpt], [r_ktm[hi]])
                        zt, rz = zr.next()
                        P.dma("sync", zt[:], ZT[ZQ + h * 128:ZQ + (h + 1) * 128, :], [R("ZT")], [rz])
                        P.cp("vector", qTs[hi][:], zt[:], [rz], [r_qT[hi]])
                        zt, rz = zr.next()
                        P.dma("sync", zt[:], ZT[ZV_ + h * 128:ZV_ + (h + 1) * 128, :], [R("ZT")], [rz])
                        for n in range(18):
                            pt, rpt = prs[n % 4].next()
                            P.mm(pt[:, 0:128], zt[:, n * 128:(n + 1) * 128], cst[:, 0, :], True, True, [rz, r_const], [rpt])
                            if n % 2:
                                P.act("scalar", vtms[hi][:, n, :], pt[:, 0:128], AF.Copy, [rpt], [r_vtm[hi]])
                            else:
                                P.cp("vector", vtms[hi][:, n, :], pt[:, 0:128], [rpt], [r_vtm[hi]])
                    gens = []
                    for hi, h in enumerate(heads):
                        for d_ in range(2):
                            t = chains_t[hi * 2 + d_]
                            P.add("gpsimd", lambda e, t=t: e.memset(t.CTa[:], 0.0), [], [t.r_CT])
                            P.add("gpsimd", lambda e, t=t: e.memset(t.CTb[:], 0.0), [], [t.r_CTb])
                            P.add("gpsimd", lambda e, t=t: e.memset(t.Nrep[:], 0.0), [], [t.r_N])
                            gens.append(chain(t, hi, h, d_))
                    lockstep(gens)
                    for hi, h in enumerate(heads):
                        hT, rh = hTs[hi], r_hT[hi]
                        zt, rz = zr.next()
                        P.dma("sync", zt[:], ZT[ZO + h * 128:ZO + (h + 1) * 128, :], [R("ZT")], [rz])
                        P.act("scalar", zt[:], zt[:], AF.Sigmoid, [rz], [rz])
                        hs, rhs = hsum.next()
                        P.tt("vector", hs[:], hT[:, 0, :], hT[:, 1, :], ALU.add, [rh], [rhs])
                        ys, rys = ysr.next()
                        for (t0, w) in TOK_TILES:
                            s_, rsq = sqr.next()
                            P.act("scalar", s_[:, :w], hs[:, t0:t0 + w], AF.Square, [rhs], [rsq])
                            ps, rp = pr.next()
                            P.mm(ps[:, :w], ones_f[:], s_[:, :w], True, True, [r_const, rsq], [rp])
                            rstd, rr = rsr.next()
                            P.act("scalar", rstd[:, :w], ps[:, :w], AF.Sqrt, [rp, r_const], [rr], bias=epst[:, 0:1], scale=1.0 / 128)
                            P.add("vector", lambda e, rstd=rstd, w=w: e.reciprocal(rstd[:, :w], rstd[:, :w]), [rr], [rr])
                            P.stt("vector", rstd[:, :w], hs[:, t0:t0 + w], vec_t[:, l, 8 + h:9 + h], rstd[:, :w], ALU.mult, ALU.mult,
                                  [rhs, rr, r_const], [rr])
                            P.tt("vector", ys[:, t0:t0 + w], rstd[:, :w], zt[:, t0:t0 + w], ALU.mult, [rr, rz], [rys])
                        P.dma("sync", YT[1, h * 128:(h + 1) * 128, :], ys[:], [rys], [R(("YT", 1))])

        def gdn_phase(l):
            ZK, ZV_, ZBA, ZQ, ZZ = 1040, 1552, 2064, 4640, 5152
            NCH = 36
            with B.phase() as C:
                prs = [C.ring_ps(2, [128, 512], F32, "gp") for _ in range(4)]
                pr = prs[0]
                I64 = cst[0:64, 0, 0:64]
                zr = C.ring_sb(2, [128, TOK], F32, "zr")
                bat_t, r_bat = zr.next()
                bat = bat_t[0:16, :]
                P.dma("sync", bat, ZT[ZBA:ZBA + 16, :], [R("ZT")], [r_bat])
                ba_tm = C.sb([64, NCH, 16], F32, "ba_tm"); r_ba = Res()
                beta_tm = C.sb([64, NCH, 8], F32, "beta_tm"); r_be = Res()
                g_tm = C.sb([64, NCH, 8], F32, "g_tm"); r_g = Res()
                gc_tm = C.sb([64, NCH, 8], F32, "gc_tm"); r_gc = Res()
                bg_tm = C.sb([64, NCH, 8], F32, "bg_tm"); r_bg = Res()
                Aneg = C.sb([64, 8], F32, "Aneg"); r_A = Res()
                negm = C.sb([64, 2, 64], F32, "negm"); r_nm = Res()
                for n in range(NCH):
                    pt, rpt = pr.next()
                    P.mm(pt[0:64, 0:16], bat_t[0:16, n * 64:(n + 1) * 64], cst[0:16, 0, 0:16], True, True, [r_bat, r_const], [rpt])
                    P.add("vector", lambda e, pt=pt, n=n: e.tensor_copy(ba_tm[:, n, :], pt[0:64, 0:16]), [rpt], [r_ba])
                P.act("scalar", beta_tm[:], ba_tm[:, :, 0:8], AF.Sigmoid, [r_ba], [r_be])
                P.act("scalar", Aneg[:], gdc[0:64, l, 0:8], AF.Exp, [r_const], [r_A])
                P.add("vector", lambda e: e.tensor_scalar(Aneg[:], Aneg[:], -1.0, None, ALU.mult), [r_A], [r_A])
                for n in range(NCH):
                    P.add("vector", lambda e, n=n: e.tensor_tensor(g_tm[:, n, :], ba_tm[:, n, 8:16], gdc[0:64, l, 8:16], ALU.add),
                          [r_ba, r_const], [r_g])
                P.act("scalar", g_tm[:], g_tm[:], AF.Exp, [r_g], [r_g])
                P.act("scalar", g_tm[:], g_tm[:], AF.Ln, [r_g, r_const], [r_g], bias=ones_f[0:64, 0:1])
                for n in range(NCH):
                    P.add("vector", lambda e, n=n: e.tensor_tensor(g_tm[:, n, :], g_tm[:, n, :], Aneg[:], ALU.mult), [r_g, r_A], [r_g])
                for n in range(NCH):
                    pt, rpt = pr.next()
                    P.mm(pt[0:64, 0:4], cst[0:64, 3, 0:64], g_tm[:, n, 0:4], True, True, [r_const, r_g], [rpt])
                    P.mm(pt[0:64, 4:8], cst[0:64, 2, 0:64], g_tm[:, n, 4:8], True, True, [r_const, r_g], [rpt])
                    P.add("vector", lambda e, pt=pt, n=n: e.tensor_copy(gc_tm[:, n, :], pt[0:64, 0:8]), [rpt], [r_gc])
                P.act("scalar", bg_tm[:], gc_tm[:], AF.Exp, [r_gc], [r_bg])
                P.add("vector", lambda e: e.tensor_tensor(bg_tm[:], bg_tm[:], beta_tm[:], ALU.mult), [r_bg, r_be], [r_bg])
                P.add("vector", lambda e: e.tensor_scalar(negm[:, 0, :], cst[0:64, 2, 0:64], -1.0, 30000.0, ALU.add, ALU.mult),
                      [r_const], [r_nm])
                P.add("vector", lambda e: e.tensor_scalar(negm[:, 1, :], cst[0:64, 3, 0:64], -1.0, 30000.0, ALU.add, ALU.mult),
                      [r_const], [r_nm])
                cvr = C.ring_sb(2, [128, TOK], F32, "cv")
                kTs = [C.sb([128, TOK], BF16, "kT") for _ in range(2)]; r_kT = [Res(), Res()]
                qTs = [C.sb([128, TOK], BF16, "qT") for _ in range(2)]; r_qT = [Res(), Res()]
                ktms = [C.sb([64, NCH, 128], BF16, "ktm") for _ in range(2)]; r_ktm = [Res(), Res()]
                vtms = [C.sb([64, NCH, 128], BF16, "vtm") for _ in range(2)]; r_vtm = [Res(), Res()]
                oTs = [C.sb([128, 2, TOK], BF16, "oT") for _ in range(2)]; r_oT = [Res(), Res()]
                sqr = C.ring_sb(3, [128, 512], F32, "sq")
                rsr = C.ring_sb(3, [128, 512], F32, "rs")
                ysr = C.ring_sb(1, [128, TOK], BF16, "ys")
                class T_:
                    pass
                chains_t = []
                for ci in range(4):
                    t = T_()
                    mk = lambda name, shape, dt, n=2: C.ring_sb(n, shape, dt, name)
                    t.grr = mk("grep", [64, 128], F32); t.gcr = mk("gcrep", [128, 64], F32); t.egr = mk("egrep", [128, 64], F32)
                    t.Er = mk("E", [64, 64], F32); t.Mr = mk("M", [64, 64], F32); t.Mtr = mk("Mt", [64, 64], F32)
                    t.Br = mk("B", [64, 6, 64], F32); t.Btr = mk("Bt", [64, 5, 64], F32)
                    t.Xbr = mk("Xtb", [64, 64], BF16); t.Xgr = mk("Xtg", [64, 64], BF16)
                    t.Xr = mk("X", [64, 64], F32, 3); t.Xtr = mk("Xt", [64, 64], F32, 3)
                    t.Yr = mk("Y1", [64, 64], F32); t.Zr_ = mk("Z1", [64, 64], F32)
                    t.ur = mk("u", [64, 128], F32); t.wTr = mk("wT", [128, 64], BF16)
                    t.vnr = mk("vnew", [64, 128], BF16); t.qkr = mk("qkm", [64, 64], F32); t.qkTr = mk("qkT", [64, 64], BF16)
                    t.qdr = mk("qd", [128, 64], BF16); t.scr = mk("sc", [64, 1], F32); t.ker = mk("kend", [64, 128], BF16)
                    t.Sf = C.sb([128, 128], F32, "Sf"); t.r_S = Res()
                    t.Sb = C.sb([128, 128], BF16, "Sb"); t.r_Sb = Res()
                    t.pr = prs[ci]
                    chains_t.append(t)
                orders = [list(range(NCH)), [3, 2, 1, 0] + list(range(NCH - 1, 3, -1))]
                SEGS = [(0, T_CTX), (T_CTX, TOK)]

                def conv_silu(src, rsrc, ch):
                    y, ry = cvr.next()
                    P.add("vector", lambda e, y=y, src=src, ch=ch: e.tensor_scalar(y[:], src[:], cw[:, l, 1, ch:ch + 1], None, ALU.mult),
                          [rsrc, r_const], [ry])
                    for (a, b_) in SEGS:
                        P.add("vector", lambda e, y=y, src=src, ch=ch, a=a, b_=b_: e.scalar_tensor_tensor(
                            y[:, a + 1:b_], src[:, a:b_ - 1], cw[:, l, 0, ch:ch + 1], y[:, a + 1:b_], ALU.mult, ALU.add),
                            [rsrc, r_const, ry], [ry])
                        P.add("vector", lambda e, y=y, src=src, ch=ch, a=a, b_=b_: e.scalar_tensor_tensor(
                            y[:, a:b_ - 1], src[:, a + 1:b_], cw[:, l, 2, ch:ch + 1], y[:, a:b_ - 1], ALU.mult, ALU.add),
                            [rsrc, r_const, ry], [ry])
                    P.act("scalar", y[:], y[:], AF.Silu, [ry], [ry])
                    return y, ry

                prep_ring = Ring([t_ for r_ in prs for t_ in r_.tiles])
                prep_ring.res = [x_ for r_ in prs for x_ in r_.res]

                def l2norm(y, ry, dst, rdst, mul):
                    def tile(t0, w):
                        s_, rsq = sqr.next()
                        P.act("scalar", s_[:, :w], y[:, t0:t0 + w], AF.Square, [ry], [rsq])
                        yield
                        ps, rp = prep_ring.next()
                        P.mm(ps[:, :w], ones_f[:], s_[:, :w], True, True, [r_const, rsq], [rp])
                        yield
                        rstd, rr = rsr.next()
                        P.act("scalar", rstd[:, :w], ps[:, :w], AF.Sqrt, [rp, r_const], [rr], bias=epst[:, 0:1], scale=1.0)
                        yield
                        P.add("vector", lambda e: e.reciprocal(rstd[:, :w], rstd[:, :w]), [rr], [rr])
                        yield
                        P.tt("vector", y[:, t0:t0 + w], y[:, t0:t0 + w], rstd[:, :w], ALU.mult, [ry, rr], [ry])
                        yield
                    lockstep([tile(t0, w) for (t0, w) in TOK_TILES[0:3]])
                    lockstep([tile(t0, w) for (t0, w) in TOK_TILES[3:5]])
                    P.act("scalar", dst[:], y[:], AF.Copy, [ry], [rdst], scale=float(mul))

                def transposes(src, rsrc, dst, rdst):
                    for n in range(NCH):
                        pt, rpt = prs[n % 4].next()
                        P.mm(pt[0:64, 0:128], src[:, n * 64:(n + 1) * 64], cst[:, 0, :], True, True, [rsrc, r_const], [rpt])
                        if n % 2:
                            P.act("scalar", dst[:, n, :], pt[0:64, 0:128], AF.Copy, [rpt], [rdst])
                        else:
                            P.add("vector", lambda e, dst=dst, pt=pt, n=n: e.tensor_copy(dst[:, n, :], pt[0:64, 0:128]), [rpt], [rdst])

                def unit(t, hi, h, d_, c):
                    kT, rk = kTs[hi], r_kT[hi]
                    qT, rq = qTs[hi], r_qT[hi]
                    ktm, rktm = ktms[hi], r_ktm[hi]
                    vtm, rvtm = vtms[hi], r_vtm[hi]
                    oT, ro = oTs[hi], r_oT[hi]
                    pr_ = t.pr
                    cs = slice(c * 64, (c + 1) * 64)
                    col = d_ * 4 + h
                    U = cst[0:64, 3, 0:64] if d_ == 0 else cst[0:64, 2, 0:64]
                    Minc = cst[0:64, 2, 0:64] if d_ == 0 else cst[0:64, 3, 0:64]
                    END = 63 if d_ == 0 else 0
                    gcol = gc_tm[:, c, col:col + 1]
                    bcol = beta_tm[:, c, col:col + 1]
                    grp, rgrp = t.grr.next()
                    P.act("scalar", grp[:], ones_f[0:64, :], AF.Identity, [r_const, r_g], [rgrp], scale=g_tm[:, c, col:col + 1])
                    pk, rpk = pr_.next()
                    P.mm(pk[0:64, 0:64], kT[:, cs], kT[:, cs], True, True, [rk], [rpk])
                    yield
                    pg, rpg = pr_.next()
                    P.mm(pg[:, 0:64], grp[:], U, True, True, [rgrp, r_const], [rpg])
                    yield
                    gcrep, rgcr = t.gcr.next()
                    P.cp("vector", gcrep[:], pg[:, 0:64], [rpg], [rgcr])
                    yield
                    egrep, regr = t.egr.next()
                    P.act("scalar", egrep[:], gcrep[:], AF.Exp, [rgcr], [regr])
                    E, rE = t.Er.next()
                    P.ts("vector", E[:], gcrep[0:64, :], gcol, 0.0, ALU.subtract, ALU.max, [rgcr, r_gc], [rE])
                    sc, rsc = t.scr.next()
                    P.tt("gpsimd", sc[:], gcrep[0:64, END:END + 1], gcol, ALU.subtract, [rgcr, r_gc], [rsc])
                    yield
                    P.act("scalar", E[:], E[:], AF.Exp, [rE], [rE], scale=-1.0)
                    P.act("scalar", sc[:], sc[:], AF.Exp, [rsc], [rsc])
                    qd, rqd = t.qdr.next()
                    P.tt("gpsimd", qd[:], qT[:, cs], egrep[:], ALU.mult, [rq, regr], [rqd])
                    yield
                    P.tt("gpsimd", E[:], E[:], Minc, ALU.mult, [rE, r_const], [rE])
                    ke, rke = t.ker.next()
                    P.act("scalar", ke[:], ktm[:, c, :], AF.Identity, [rktm, rsc], [rke], scale=sc[:, 0:1])
                    yield
                    M, rM = t.Mr.next()
                    P.stt("vector", M[:], pk[0:64, 0:64], bcol, E[:], ALU.mult, ALU.mult, [rpk, rE, r_be], [rM])
                    pq, rpq = pr_.next()
                    P.mm(pq[0:64, 0:64], qT[:, cs], kT[:, cs], True, True, [rq, rk], [rpq])
                    yield
                    pmt, rpmt = pr_.next()
                    P.mm(pmt[0:64, 0:64], M[:], I64, True, True, [rM, r_const], [rpmt])
                    qkm, rqk = t.qkr.next()
                    P.tt("vector", qkm[:], pq[0:64, 0:64], E[:], ALU.mult, [rpq, rE], [rqk])
                    yield
                    pqt, rpqt = pr_.next()
                    P.mm(pqt[0:64, 0:64], qkm[:], I64, True, True, [rqk, r_const], [rpqt])
                    Ball, rB = t.Br.next()
                    P.tt("vector", Ball[:], M[:].unsqueeze(1).to_broadcast([64, 6, 64]), lm[:, 0:6, :], ALU.mult, [rM, r_const], [rB])
                    yield
                    Mt, rMt = t.Mtr.next()
                    P.cp("vector", Mt[:], pmt[0:64, 0:64], [rpmt], [rMt])
                    qkT, rqkT = t.qkTr.next()
                    P.act("scalar", qkT[:], pqt[0:64, 0:64], AF.Copy, [rpqt], [rqkT])
                    X, rX = t.Xr.next()
                    P.tt("vector", X[:], I64, Ball[:, 0, :], ALU.subtract, [rB, r_const], [rX])
                    yield
                    Btall, rBt = t.Btr.next()
                    P.tt("gpsimd", Btall[:], Mt[:].unsqueeze(1).to_broadcast([64, 5, 64]), lm[:, 0:5, :], ALU.mult, [rMt, r_const], [rBt])
                    yield
                    Xt, rXt = t.Xtr.next()
                    P.tt("gpsimd", Xt[:], I64, Btall[:, 0, :], ALU.subtract, [rBt, r_const], [rXt])
                    yield
                    for lv in range(1, 6):
                        pz1, rpz1 = pr_.next()
                        P.mm(pz1[0:64, 0:64], Ball[:, lv, :], Xt[:], True, True, [rB, rXt], [rpz1])
                        if lv < 5:
                            py1, rpy1 = pr_.next()
                            P.mm(py1[0:64, 0:64], Btall[:, lv, :], X[:], True, True, [rBt, rX], [rpy1])
                        yield
                        Z1, rZ1 = t.Zr_.next()
                        P.act("scalar", Z1[:], pz1[0:64, 0:64], AF.Copy, [rpz1], [rZ1])
                        if lv < 5:
                            Y1, rY1 = t.Yr.next()
                            P.cp("vector", Y1[:], py1[0:64, 0:64], [rpy1], [rY1])
                        yield
                        pz2, rpz2 = pr_.next()
                        P.mm(pz2[0:64, 0:64], X[:], Z1[:], True, True, [rX, rZ1], [rpz2])
                        if lv < 5:
                            py2, rpy2 = pr_.next()
                            P.mm(py2[0:64, 0:64], Xt[:], Y1[:], True, True, [rXt, rY1], [rpy2])
                        yield
                        Xtn, rXtn = t.Xtr.next()
                        P.tt("vector", Xtn[:], Xt[:], pz2[0:64, 0:64], ALU.subtract, [rXt, rpz2], [rXtn])
                        if lv < 5:
                            Xn, rXn = t.Xr.next()
                            P.tt("vector", Xn[:], X[:], py2[0:64, 0:64], ALU.subtract, [rX, rpy2], [rXn])
                            X, rX = Xn, rXn
                        Xt, rXt = Xtn, rXtn
                        yield
                    Xtb, rXtb = t.Xbr.next()
                    P.act("scalar", Xtb[:], Xt[:], AF.Identity, [rXt, r_be], [rXtb], scale=bcol)
                    Xtg, rXtg = t.Xgr.next()
                    P.act("scalar", Xtg[:], Xt[:], AF.Identity, [rXt, r_bg], [rXtg], scale=bg_tm[:, c, col:col + 1])
                    yield
                    pu, rpu = pr_.next()
                    P.mm(pu[0:64, 0:128], Xtb[:], vtm[:, c, :], True, True, [rXtb, rvtm], [rpu])
                    pw, rpw = pr_.next()
                    P.mm(pw[:, 0:64], ktm[:, c, :], Xtg[:], True, True, [rktm, rXtg], [rpw])
                    yield
                    u, ru = t.ur.next()
                    P.act("scalar", u[:], pu[0:64, 0:128], AF.Copy, [rpu], [ru])
                    wT, rwT = t.wTr.next()
                    P.cp("vector", wT[:], pw[:, 0:64], [rpw], [rwT])
                    yield
                    pws, rpws = pr_.next()
                    P.mm(pws[0:64, 0:128], wT[:], t.Sb[:], True, True, [rwT, t.r_Sb], [rpws])
                    po_, rpo = pr_.next()
                    P.mm(po_[:, 0:64], t.Sb[:], qd[:], True, False, [t.r_Sb, rqd], [rpo])
                    yield
                    vn, rvn = t.vnr.next()
                    P.tt("vector", vn[:], u[:], pws[0:64, 0:128], ALU.subtract, [ru, rpws], [rvn])
                    yield
                    P.mm(po_[:, 0:64], vn[:], qkT[:], False, True, [rvn, rqkT], [rpo])
                    pds, rpds = pr_.next()
                    P.mm(pds[:, 0:128], ke[:], vn[:], True, True, [rke, rvn], [rpds])
                    yield
                    P.act("scalar", oT[:, d_, cs], po_[:, 0:64], AF.Copy, [rpo], [ro])
                    P.stt("vector", t.Sf[:], t.Sf[:], egrep[:, END:END + 1], pds[:, 0:128], ALU.mult, ALU.add,
                          [t.r_S, regr, rpds], [t.r_S])
                    yield
                    P.act("scalar", t.Sb[:], t.Sf[:], AF.Copy, [t.r_S], [t.r_Sb])
                    yield

                def chain(t, hi, h, d_):
                    for step in range(NCH):
                        yield from unit(t, hi, h, d_, orders[d_][step])

                HP = [(0, 1), (2, 3)]
                if "gdh1" in debug:
                    HP = [(0,)]
                for heads in HP:
                    for hi, h in enumerate(heads):
                        zt, rz = zr.next()
                        P.dma("sync", zt[:], ZT[ZK + h * 128:ZK + (h + 1) * 128, :], [R("ZT")], [rz])
                        kf, rkf = conv_silu(zt, rz, 4 + h)
                        l2norm(kf, rkf, kTs[hi], r_kT[hi], 1.0)
                        transposes(kf, rkf, ktms[hi], r_ktm[hi])
                        zt, rz = zr.next()
                        P.dma("sync", zt[:], ZT[ZQ + h * 128:ZQ + (h + 1) * 128, :], [R("ZT")], [rz])
                        qf, rqf = conv_silu(zt, rz, h)
                        l2norm(qf, rqf, qTs[hi], r_qT[hi], 128 ** -0.5)
                        zt, rz = zr.next()
                        P.dma("sync", zt[:], ZT[ZV_ + h * 128:ZV_ + (h + 1) * 128, :], [R("ZT")], [rz])
                        vf, rvf = conv_silu(zt, rz, 8 + h)
                        transposes(vf, rvf, vtms[hi], r_vtm[hi])
                    gens = []
                    for hi, h in enumerate(heads):
                        for d_ in range(2):
                            t = chains_t[hi * 2 + d_]
                            P.add("gpsimd", lambda e, t=t: e.memset(t.Sf[:], 0.0), [], [t.r_S])
                            P.add("gpsimd", lambda e, t=t: e.memset(t.Sb[:], 0.0), [], [t.r_Sb])
                            gens.append(chain(t, hi, h, d_))
                    lockstep(gens)
                    for hi, h in enumerate(heads):
                        oT, ro = oTs[hi], r_oT[hi]
                        zt, rz = zr.next()
                        P.dma("sync", zt[:], ZT[ZZ + h * 128:ZZ + (h + 1) * 128, :], [R("ZT")], [rz])
                        P.act("scalar", zt[:], zt[:], AF.Silu, [rz], [rz])
                        osum, rosum = cvr.next()
                        P.add("vector", lambda e, oT=oT, osum=osum: e.tensor_tensor(osum[:], oT[:, 0, :], oT[:, 1, :], ALU.add), [ro], [rosum])
                        ys, rys = ysr.next()

                        def otile(t0, w, osum=osum, rosum=rosum, ys=ys, rys=rys, zt=zt, rz=rz):
                            s_, rsq = sqr.next()
                            P.act("scalar", s_[:, :w], osum[:, t0:t0 + w], AF.Square, [rosum], [rsq])
                            yield
                            ps, rp = prep_ring.next()
                            P.mm(ps[:, :w], ones_f[:], s_[:, :w], True, True, [r_const, rsq], [rp])
                            yield
                            rstd, rr = rsr.next()
                            P.act("scalar", rstd[:, :w], ps[:, :w], AF.Sqrt, [rp, r_const], [rr], bias=epst[:, 0:1], scale=1.0 / 128)
                            yield
                            P.add("vector", lambda e: e.reciprocal(rstd[:, :w], rstd[:, :w]), [rr], [rr])
                            yield
                            P.stt("vector", rstd[:, :w], osum[:, t0:t0 + w], vec_t[:, l, 2:3], rstd[:, :w], ALU.mult, ALU.mult,
                                  [rosum, rr, r_const], [rr])
                            yield
                            P.tt("vector", ys[:, t0:t0 + w], rstd[:, :w], zt[:, t0:t0 + w], ALU.mult, [rr, rz], [rys])
                            yield
                        lockstep([otile(t0, w) for (t0, w) in TOK_TILES[0:3]])
                        lockstep([otile(t0, w) for (t0, w) in TOK_TILES[3:5]])
                        P.dma("sync", YT[2, h * 128:(h + 1) * 128, :], ys[:], [rys], [R(("YT", 2))])

        def merge_phase(l, Xsrc, xkey, Xdst, dkey, last=False):
            ZG = 6176
            with B.phase() as C:
                yt = C.sb([128, 16, TOK], BF16, "yt"); r_yt = Res()
                mg = C.sb([128, KC, TOK], BF16, "mg"); r_mg = Res()
                P.dma("sync", yt[:], YT.rearrange("n (k p) t -> p (n k) t", p=128), [R(("YT", n)) for n in range(4)], [r_yt])
                wbr = C.ring_sb(2, [128, 16, 128], BF16, "wbr")
                gr = C.ring_sb(4, [128, 4, 256], F32, "gate")
                pm = C.ring_ps(4, [128, 512], F32, "pm")
                acr = C.ring_sb(3, [128, 256], F32, "acc")
                tilesA = NORM_TILES[1:] if last else NORM_TILES
                tilesB = TOK_TILES[1:] if last else TOK_TILES
                gq = 0
                for fc in range(KC):
                    wb, rwb = wbr.next()
                    P.dma("gpsimd", wb[:], w_branch[l, :, :, fc * 128:(fc + 1) * 128].rearrange("n (k p) c -> p (n k) c", p=128),
                          [], [rwb])
                    for (t0, w) in tilesA:
                        gt_, rgt = gr.next()
                        gq += 1
                        P.dma("scalar" if gq % 2 else "sync", gt_[:, :, :w],
                              ZT[ZG:ZG + 4 * D, t0:t0 + w].rearrange("(n r) t -> r n t", r=D)[fc * 128:(fc + 1) * 128],
                              [R("ZT")], [rgt])
                        P.act("scalar", gt_[:, :, :w], gt_[:, :, :w], AF.Sigmoid, [rgt], [rgt])
                        acc, racc = acr.next()
                        for n in range(4):
                            ps, rp = pm.next()
                            for k in range(4):
                                P.mm(ps[:, :w], wb[:, n * 4 + k, :], yt[:, n * 4 + k, t0:t0 + w], k == 0, k == 3,
                                     [rwb, r_yt], [rp])
                            if n == 0:
                                P.tt("vector", acc[:, :w], gt_[:, 0, :w], ps[:, :w], ALU.mult, [rgt, rp], [racc])
                            else:
                                P.tt("vector", gt_[:, n, :w], gt_[:, n, :w], ps[:, :w], ALU.mult, [rgt, rp], [rgt])
                                if n < 3:
                                    P.tt("gpsimd" if n == 2 else "vector", acc[:, :w], acc[:, :w], gt_[:, n, :w], ALU.add, [rgt, racc], [racc])
                                else:
                                    P.tt("vector", mg[:, fc, t0:t0 + w], acc[:, :w], gt_[:, n, :w], ALU.add, [rgt, racc], [r_mg])
                wor = C.ring_sb(2, [128, KC, 128], BF16, "wo")
                xrr = C.ring_sb(2, [128, 512], F32, "xres")
                orr = C.ring_sb(2, [128, 512], F32, "ores")
                for fc in range(KC):
                    wo, rwo = wor.next()
                    P.dma("gpsimd", wo[:], w_out_all[l, :, fc * 128:(fc + 1) * 128].rearrange("(k p) n -> p k n", p=128),
                          [], [rwo])
                    if True:
                        for (t0, w) in tilesB:
                            j = 1 if t0 < T_CTX else 0
                            y, ry = pm.next()
                            for k in range(KC):
                                P.mm(y[:, :w], wo[:, k, :], mg[:, k, t0:t0 + w], k == 0, k == KC - 1,
                                     [rwo, r_mg], [ry])
                            xr_, rxr = xrr.next()
                            P.dma("scalar", xr_[:, :w], Xsrc[fc * 128:(fc + 1) * 128, t0:t0 + w], [R((xkey, fc))], [rxr])
                            o, ro = orr.next()
                            P.add("vector", lambda e, o=o, y=y, xr_=xr_, fc=fc, j=j, w=w: e.scalar_tensor_tensor(
                                o[:, :w], y[:, :w], Hg[:, 1, fc, j:j + 1], xr_[:, :w], ALU.mult, ALU.add),
                                [ry, rxr, r_mod], [ro])
                            P.dma("sync", Xdst[fc * 128:(fc + 1) * 128, t0:t0 + w], o[:, :w], [ro], [R((dkey, fc))])

        X0 = xT_in
        for l in range(nlayers):
            if "only_gdn" in debug:
                gdn_phase(0)
                break
            if "only_ml" in debug:
                mlstm_phase(0)
                break
            if "only_attn" in debug:
                attn_phase(0)
                break
            if "only_sg" in debug:
                sg_phase(0)
                break
            mod_phase(l)
            ffn(l, 0, X0 if l == 0 else Xs[2], "X0" if l == 0 else ("X2", l - 1), Xs[0], ("Xa", l))
            if stop_after == ("ffn1", l):
                break
            if "nomix" in debug:
                ffn(l, 1, Xs[0], ("Xa", l), Xs[2], ("X2", l), final_out=(l == nlayers - 1))
                continue
            inproj(l, Xs[0], ("Xa", l))
            if stop_after == ("inproj", l):
                break
            if "zero_y12" in debug:
                with B.phase() as C:
                    zt_ = C.sb([128, TOK], BF16, "zt_"); rz_ = Res()
                    P.add("vector", lambda e: e.memset(zt_[:], 0.0), [], [rz_])
                    for n_ in ([1] if "noml" in debug else []) + ([2] if "nogd" in debug else []):
                        for k_ in range(4):
                            P.dma("sync", YT[n_, k_ * 128:(k_ + 1) * 128, :], zt_[:], [rz_], [R(("YT", n_))])
            attn_phase(l)
            sg_phase(l)
            if "noml" not in debug:
                mlstm_phase(l)
            if "nogd" not in debug:
                gdn_phase(l)
            if stop_after == ("mix", l):
                break
            merge_phase(l, Xs[0], ("Xa", l), Xs[1], ("Xb", l), last=(l == nlayers - 1))
            if stop_after == ("merge", l):
                break
            ffn(l, 1, Xs[1], ("Xb", l), Xs[2], ("X2", l), final_out=(l == nlayers - 1))
        P.emit(final=True)
    return nc


def _host_consts():
    m = {}
    cst = np.zeros((128, 4, 128), np.float32)
    cst[:, 0, :] = np.eye(128)
    pm = np.zeros((128, 128), np.float32)
    for p in range(128):
        if p % 64 < 32:
            pm[p + 32, p] = -1.0
        else:
            pm[p - 32, p] = 1.0
    cst[:, 1, :] = pm
    cst[:, 2, :] = np.tril(np.ones((128, 128)))
    cst[:, 3, :] = np.triu(np.ones((128, 128)))
    m["consts"] = cst
    n = 32
    inv = (10000.0 ** (-np.arange(n, dtype=np.float32) / n)).astype(np.float32)
    t = np.arange(T_LAT)
    ar = (t // 64).astype(np.float32)[:, None] * inv
    ac = (t % 64).astype(np.float32)[:, None] * inv
    cos = np.concatenate([np.cos(ar), np.cos(ar), np.cos(ac), np.cos(ac)], axis=1).T
    sin = np.concatenate([np.sin(ar), np.sin(ar), np.sin(ac), np.sin(ac)], axis=1).T
    m["ropetab"] = np.ascontiguousarray(np.stack([cos, sin], axis=1).astype(np.float32))
    lm = np.zeros((64, 7, 64), np.float32)
    ii = np.arange(64)
    for lv in range(6):
        lm[:, lv, :] = ((ii[:, None] >> (lv + 1)) == (ii[None, :] >> (lv + 1))) & ((ii[:, None] >> lv) != (ii[None, :] >> lv))
    lm[:, 6, :] = (ii[:, None] != ii[None, :])
    m["lm"] = lm
    return m


def _host_inputs(inp, b, shared):
    m = dict(shared)
    m["xT"] = np.ascontiguousarray(np.concatenate([inp["ctx"][b], inp["x"][b]], axis=0).T)
    m["cvec"] = np.ascontiguousarray(np.stack([inp["c"][b].reshape(KC, 128).T, inp["c_ctx"].reshape(KC, 128).T], axis=-1))
    return m


def kernel(**inp):
    inp = {k: np.asarray(v) for k, v in inp.items()}
    shared = _host_consts()
    for k in ("mod_w", "ffn1_w_in", "ffn2_w_in", "ffn1_w_out", "ffn2_w_out", "w_in", "w_branch", "w_out", "sg_b"):
        shared[k] = inp[k]
    shared["mod_b"] = np.ascontiguousarray(inp["mod_b"].reshape(DEPTH, 144, 128).transpose(0, 2, 1))
    nr = np.stack([inp["ffn1_norm"], inp["mix_norm"], inp["ffn2_norm"]], axis=1)
    shared["norms"] = np.ascontiguousarray(nr.reshape(DEPTH, 3, KC, 128).transpose(0, 3, 1, 2))
    shared["sgwT"] = np.ascontiguousarray(inp["sg_w"].transpose(0, 1, 3, 2))
    v = np.zeros((128, DEPTH, 12), np.float32)
    v[:, :, 0] = inp["at_q_norm"].T
    v[:, :, 1] = inp["at_k_norm"].T
    v[:, :, 2] = inp["gd_norm"].T
    v[:, :, 4:8] = inp["sg_norm"].reshape(DEPTH, 4, 128).transpose(2, 0, 1)
    v[:, :, 8:12] = inp["ml_norm"].reshape(DEPTH, 4, 128).transpose(2, 0, 1)
    shared["vecs"] = v
    shared["mlb"] = np.ascontiguousarray(np.broadcast_to(inp["ml_if_bias"][None], (128, DEPTH, 16))).astype(np.float32)
    g = np.concatenate([inp["gd_a_log"].reshape(DEPTH, 8), inp["gd_dt_bias"].reshape(DEPTH, 8)], axis=1)
    shared["gdc"] = np.ascontiguousarray(np.broadcast_to(g[None], (128, DEPTH, 16))).astype(np.float32)
    shared["cw"] = np.ascontiguousarray(inp["gd_conv"].reshape(DEPTH, 3, 12, 128).transpose(3, 0, 1, 2))
    nc = build_program()
    n = inp["x"].shape[0]
    in_maps = [_host_inputs(inp, b, shared) for b in range(n)]
    res = run_bass_kernel_spmd(nc, in_maps, core_ids=list(range(n)))
    out = np.stack([np.ascontiguousarray(r["outT"].T) for r in res.results], axis=0)
    return out.astype(np.float32)
```

```python
import math
from contextlib import ExitStack
import numpy as np
import concourse.bass as bass
import concourse.mybir as mybir
from concourse.bass_utils import run_bass_kernel_spmd

F32 = mybir.dt.float32
BF16 = mybir.dt.bfloat16
AF = mybir.ActivationFunctionType
ALU = mybir.AluOpType

D = 2048
T_LAT = 2048
T_CTX = 256
TOK = T_CTX + T_LAT
DEPTH = 2
DFF = 5632
KC = D // 128
IN_COLS = 14368
EPS = 1e-6


class Res:
    __slots__ = ("name", "w", "rs")

    def __init__(self, name=""):
        self.name = name
        self.w = None
        self.rs = []


class Op:
    __slots__ = ("id", "eng", "fn", "deps", "dma", "prod", "cnt", "dsem", "dval", "dprev")

    def __init__(self, id, eng, fn, deps, dma):
        self.id = id
        self.eng = eng
        self.fn = fn
        self.deps = deps
        self.dma = dma
        self.prod = False
        self.cnt = None
        self.dsem = None
        self.dval = None
        self.dprev = None


ENGS = ("tensor", "vector", "scalar", "gpsimd", "sync")


class Prog:
    def __init__(self, nc, ndsem=6):
        self.nc = nc
        self.ops = []
        self.ndsem = ndsem

    def add(self, eng, fn, reads=(), writes=(), dma=False):
        oid = len(self.ops)
        deps = set()
        for r in reads:
            if r.w is not None:
                deps.add(r.w)
        for w in writes:
            if w.w is not None:
                deps.add(w.w)
            for x in w.rs:
                deps.add(x)
        deps.discard(oid)
        op = Op(oid, eng, fn, deps, dma)
        self.ops.append(op)
        for r in reads:
            r.rs.append(oid)
        for w in writes:
            w.w = oid
            w.rs = []
        return op

    def mm(self, out, lhsT, rhs, start, stop, reads, writes):
        self.add("tensor", lambda e: e.matmul(out, lhsT, rhs, start=start, stop=stop), reads, writes)

    def tr(self, out, in_, ident, reads, writes):
        self.add("tensor", lambda e: e.transpose(out, in_, ident), reads, writes)

    def act(self, eng, out, in_, func, reads, writes, bias=None, scale=None):
        kw = {}
        if bias is not None:
            kw["bias"] = bias
        if scale is not None:
            kw["scale"] = scale
        self.add(eng, lambda e: e.activation(out, in_, func, **kw), reads, writes)

    def tt(self, eng, out, a, b, op, reads, writes):
        self.add(eng, lambda e: e.tensor_tensor(out, a, b, op), reads, writes)

    def ts(self, eng, out, a, s1, s2, op0, op1, reads, writes):
        if op1 is None:
            self.add(eng, lambda e: e.tensor_scalar(out, a, s1, s2, op0), reads, writes)
        else:
            self.add(eng, lambda e: e.tensor_scalar(out, a, s1, s2, op0, op1), reads, writes)

    def stt(self, eng, out, a, sc, b, op0, op1, reads, writes):
        self.add(eng, lambda e: e.scalar_tensor_tensor(out, a, sc, b, op0, op1), reads, writes)

    def cp(self, eng, out, a, reads, writes):
        self.add(eng, lambda e: e.tensor_copy(out, a), reads, writes)

    def dma(self, eng, out, in_, reads, writes, **kw):
        self.add(eng, lambda e: e.dma_start(out=out, in_=in_, **kw), reads, writes, dma=True)

    def setup(self, stack):
        nc = self.nc
        self.esem = {e: stack.enter_context(nc.semaphore("s_" + e)) for e in ENGS}
        self.dsems = {e: [stack.enter_context(nc.semaphore("d_%s%d" % (e, i))) for i in range(self.ndsem)]
                      for e in ("sync", "gpsimd", "scalar")}
        self.ecnt = {e: 0 for e in ENGS}
        self.stack_ = stack
        self.SEM_CAP = 12000
        self.dcount = {e: [0] * self.ndsem for e in self.dsems}
        self.dnext = {e: 0 for e in self.dsems}
        self.emitted = 0
        self.nphase = 0

    def emit(self, final=False):
        nc = self.nc
        ops = self.ops
        lo = self.emitted
        cur = ops[lo:]
        self.emitted = len(ops)
        self.nphase += 1
        bar_e = dict(self.ecnt)
        bar_sem = dict(self.esem)
        for e_ in ENGS:
            if self.ecnt[e_] > self.SEM_CAP:
                self.esem[e_] = self.stack_.enter_context(nc.semaphore("s_%s_%d" % (e_, self.nphase)))
                self.ecnt[e_] = 0
        bar_d = {q: list(v) for q, v in self.dcount.items()}
        for op in cur:
            for d in op.deps:
                if d < lo:
                    continue
                p = ops[d]
                if not p.dma:
                    if p.eng == "tensor" and op.eng == "tensor" and not op.dma:
                        continue
                    p.prod = True
        lastc = {}
        for op in cur:
            if not op.dma:
                lastc[op.eng] = op
        for op in lastc.values():
            op.prod = True
        for op in cur:
            if op.dma:
                i = self.dnext[op.eng] % self.ndsem
                self.dnext[op.eng] += 1
                op.dsem = self.dsems[op.eng][i]
                op.dprev = self.dcount[op.eng][i]
                self.dcount[op.eng][i] += 16
                op.dval = self.dcount[op.eng][i]
            elif op.prod:
                self.ecnt[op.eng] += 1
                op.cnt = self.ecnt[op.eng]
        per = {e: [o for o in cur if o.eng == e] for e in ENGS}
        esem, dsems = dict(self.esem), self.dsems

        trace = getattr(self, "trace", None)
        if trace is None:
            trace = self.trace = {e: [] for e in ENGS}
        semname = getattr(self, "semname", None)
        if semname is None:
            semname = self.semname = {}
        for e_ in ENGS:
            semname[id(self.esem[e_])] = "s_%s_%d" % (e_, id(self.esem[e_]))
        for q_ in self.dsems:
            for i_, sm in enumerate(self.dsems[q_]):
                semname[id(sm)] = "d_%s%d" % (q_, i_)

        def run(e, engname):
            waited = {}
            tr = trace[engname]

            def wait(sem, val):
                k = id(sem)
                if waited.get(k, 0) >= val:
                    return
                waited[k] = val
                tr.append(("wait", semname[k], val))
                e.wait_ge(sem, val)

            for e2 in ENGS:
                if e2 != engname and bar_e[e2] > 0:
                    wait(bar_sem[e2], bar_e[e2])
            for q in dsems:
                for i in range(self.ndsem):
                    if bar_d[q][i] > 0:
                        wait(dsems[q][i], bar_d[q][i])
            for op in per[engname]:
                for d in sorted(op.deps):
                    p = ops[d]
                    if p.dma:
                        wait(p.dsem, p.dval)
                    elif d >= lo:
                        if p.eng == "tensor" and engname == "tensor" and not op.dma:
                            continue
                        wait(esem[p.eng], p.cnt)
                if op.dma:
                    if op.dprev > 0:
                        wait(op.dsem, op.dprev)
                    op.fn(e).then_inc(op.dsem, 16)
                    tr.append(("inc", semname[id(op.dsem)], 16))
                else:
                    ins = op.fn(e)
                    if op.prod:
                        ins.then_inc(esem[engname], 1)
                        tr.append(("inc", semname[id(esem[engname])], 1))
            if final and engname == "sync":
                for q in dsems:
                    for i in range(self.ndsem):
                        if self.dcount[q][i] > 0:
                            e.wait_ge(dsems[q][i], self.dcount[q][i])
                for e2 in ENGS:
                    if e2 != "sync" and self.ecnt[e2] > 0:
                        e.wait_ge(esem[e2], self.ecnt[e2])

        with nc.Block() as block:
            @block.tensor
            def _(e):
                run(e, "tensor")

            @block.vector
            def _(e):
                run(e, "vector")

            @block.scalar
            def _(e):
                run(e, "scalar")

            @block.gpsimd
            def _(e):
                run(e, "gpsimd")

            @block.sync
            def _(e):
                run(e, "sync")


class Ring:
    def __init__(self, tiles):
        self.tiles = tiles
        self.res = [Res() for _ in tiles]
        self.i = 0

    def next(self):
        k = self.i % len(self.tiles)
        self.i += 1
        return self.tiles[k], self.res[k]


class Ctx:
    def __init__(self, nc, stack):
        self.nc = nc
        self.stack = stack
        self.P = Prog(nc)
        self.n = 0

    _uid = [0]

    def sb(self, shape, dt, name=None):
        Ctx._uid[0] += 1
        return self.stack.enter_context(self.nc.sbuf_tensor("%s_%d" % (name or "sb", Ctx._uid[0]), list(shape), dt))

    def ps(self, shape, dt=F32, name=None):
        Ctx._uid[0] += 1
        return self.stack.enter_context(self.nc.psum_tensor("%s_%d" % (name or "ps", Ctx._uid[0]), list(shape), dt))

    def ring_sb(self, n, shape, dt, name=None):
        return Ring([self.sb(shape, dt, name) for _ in range(n)])

    def ring_ps(self, n, shape, dt=F32, name=None):
        return Ring([self.ps(shape, dt, name) for _ in range(n)])

    def dram(self, name, shape, dt):
        return self.nc.dram_tensor(name, list(shape), dt).ap()


TOK_TILES = [(0, 256), (256, 512), (768, 512), (1280, 512), (1792, 512)]
NORM_TILES = [(c, 256) for c in range(0, 2304, 256)]


class Builder:
    def __init__(self, nc, stack, debug=()):
        self.nc = nc
        self.C = Ctx(nc, stack)
        self.P = self.C.P
        self.P.setup(stack)
        self.debug = debug
        self.inp = {}
        self.dres = {}

    def din(self, name, shape, dt=F32):
        if any(x in self.debug for x in ("only_gdn", "only_ml", "only_attn", "only_sg")) and int(np.prod(shape)) > 4000000:
            shape = [1] * len(shape)
        ap = self.nc.dram_tensor(name, list(shape), dt, kind="ExternalInput").ap()
        self.inp[name] = ap
        return ap

    def dscr(self, name, shape, dt=F32):
        kind = "ExternalOutput" if name in self.debug else "Internal"
        return self.nc.dram_tensor(name, list(shape), dt, kind=kind).ap()

    def R(self, key):
        r = self.dres.get(key)
        if r is None:
            r = self.dres[key] = Res(str(key))
        return r

    def phase(self):
        b = self

        class Ph:
            def __enter__(s):
                s.st = ExitStack()
                s.st.__enter__()
                s.C = Ctx(b.nc, s.st)
                s.C.P = b.P
                return s.C

            def __exit__(s, *a):
                if a[0] is None:
                    b.P.emit()
                return s.st.__exit__(*a)

        return Ph()


def build_program(debug=(), stop_after=None, nlayers=DEPTH):
    nc = bass.Bass("TRN2", target_bir_lowering=False)
    top = ExitStack()
    with top:
        B = Builder(nc, top, debug)
        P = B.P
        R = B.R
        xT_in = B.din("xT", [D, TOK])
        cvec = B.din("cvec", [128, KC, 2])
        mod_w = B.din("mod_w", [DEPTH, D, 9 * D])
        mod_b = B.din("mod_b", [DEPTH, 128, 144])
        norms = B.din("norms", [DEPTH, 128, 3, KC])
        ffn_w_in = [B.din("ffn1_w_in", [DEPTH, D, 2 * DFF]), B.din("ffn2_w_in", [DEPTH, D, 2 * DFF])]
        ffn_w_out = [B.din("ffn1_w_out", [DEPTH, DFF, D]), B.din("ffn2_w_out", [DEPTH, DFF, D])]
        w_in_all = B.din("w_in", [DEPTH, D, IN_COLS])
        w_branch = B.din("w_branch", [DEPTH, 4, 512, D])
        w_out_all = B.din("w_out", [DEPTH, D, D])
        sgwT = B.din("sgwT", [DEPTH, 4, 128, 128])
        sg_b = B.din("sg_b", [DEPTH, 4, 128])
        vecs = B.din("vecs", [128, DEPTH, 12])
        mlb_in = B.din("mlb", [128, DEPTH, 16])
        gdc_in = B.din("gdc", [128, DEPTH, 16])
        cw_in = B.din("cw", [128, DEPTH, 3, 12])
        lm_in = B.din("lm", [64, 7, 64])
        consts = B.din("consts", [128, 4, 128])
        ropetab = B.din("ropetab", [128, 2, T_LAT])
        outT = nc.dram_tensor("outT", [D, T_LAT], F32, kind="ExternalOutput").ap()
        Xs = [B.dscr("X%d" % i, [D, TOK]) for i in range(3)]
        GT = B.dscr("GT", [DFF, TOK], BF16)
        if any(x in debug for x in ("only_gdn", "only_ml", "only_attn", "only_sg")):
            ZT = nc.dram_tensor("ZT", [IN_COLS, TOK], F32, kind="ExternalInput").ap()
        else:
            ZT = B.dscr("ZT", [IN_COLS, TOK])
        YT = B.dscr("YT", [4, 512, TOK], BF16)

        PC = B.C
        ones_f = PC.sb([128, 128], F32, "ones_f"); r_const = Res()
        modT = PC.sb([128, 144, 2], F32, "modT"); r_mod = Res()
        Avec = PC.sb([128, 3, KC, 2], F32, "Avec")
        Hg = PC.sb([128, 3, KC, 2], F32, "Hg")
        nrm = PC.sb([128, DEPTH, 3, KC], F32, "nrm")
        epst = PC.sb([128, 1], F32, "epst")
        ones_b = PC.sb([128, 128], BF16, "ones_b")
        cst = PC.sb([128, 4, 128], F32, "cst")
        mlb = PC.sb([128, DEPTH, 16], F32, "mlb")
        gdc = PC.sb([128, DEPTH, 16], F32, "gdc")
        cw = PC.sb([128, DEPTH, 3, 12], F32, "cw")
        lm = PC.sb([64, 7, 64], F32, "lm")
        ident_b = PC.sb([128, 128], BF16, "ident_b")
        ident_f = cst[:, 0, :]
        pm_f = cst[:, 1, :]
        ROPE = {}
        vec_t = PC.sb([128, DEPTH, 12], F32, "vec_t")
        atn = vec_t
        sgn = vec_t[:, :, 4:8]

        with B.phase() as C:
            P.add("vector", lambda e: e.memset(ones_f[:], 1.0), [], [r_const])
            P.add("vector", lambda e: e.memset(epst[:], EPS), [], [r_const])
            P.dma("sync", nrm[:], norms.rearrange("l p i k -> p l i k"), [], [r_const])
            P.dma("sync", cst[:], consts, [], [r_const])
            P.dma("sync", vec_t[:], vecs, [], [r_const])
            P.dma("sync", mlb[:], mlb_in, [], [r_const])
            P.dma("sync", gdc[:], gdc_in, [], [r_const])
            P.dma("sync", cw[:], cw_in, [], [r_const])
            P.dma("sync", lm[:], lm_in, [], [r_const])
            P.add("vector", lambda e: e.memset(ones_b[:], 1.0), [], [r_const])
            P.add("vector", lambda e: e.tensor_copy(ident_b[:], cst[:, 0, :]), [r_const], [r_const])

        def mod_phase(l):
            with B.phase() as C:
                sc = C.sb([128, KC, 2], F32, "sc"); r_sc = Res()
                mb = C.sb([128, 144], F32, "mb"); r_mb = Res()
                wr = C.ring_sb(2, [128, KC, 512], F32, "modw")
                wrb = C.ring_sb(2, [128, KC, 512], BF16, "modwb")
                scb = C.sb([128, KC, 2], BF16, "scb"); r_scb = Res()
                pr = C.ring_ps(4, [128, 2], F32, "modps")
                P.dma("sync", sc[:], cvec, [], [r_sc])
                P.dma("sync", mb[:], mod_b[l], [], [r_mb])
                P.act("scalar", sc[:], sc[:], AF.Silu, [r_sc], [r_sc])
                P.cp("vector", scb[:], sc[:], [r_sc], [r_scb])
                nh = 0
                for blk in range(36):
                    lowp = (blk % 3 == 2)
                    if lowp:
                        wt, rw = wrb.next()
                        P.dma("gpsimd", wt[:], mod_w[l, :, blk * 512:(blk + 1) * 512].rearrange("(kc p) n -> p kc n", p=128),
                              [], [rw])
                        rhs, r_rhs = scb, r_scb
                    else:
                        wt, rw = wr.next()
                        nh += 1
                        P.dma("sync" if nh % 2 == 0 else "scalar", wt[:],
                              mod_w[l, :, blk * 512:(blk + 1) * 512].rearrange("(kc p) n -> p kc n", p=128), [], [rw])
                        rhs, r_rhs = sc, r_sc
                    for s in range(4):
                        ps, rp = pr.next()
                        for kc in range(KC):
                            P.mm(ps[:], wt[:, kc, s * 128:(s + 1) * 128], rhs[:, kc, :], kc == 0, kc == KC - 1,
                                 [rw, r_rhs], [rp])
                        ch = blk * 4 + s
                        P.ts("vector", modT[:, ch, :], ps[:], mb[:, ch:ch + 1], None, ALU.add, None, [rp, r_mb], [r_mod])
                for i in range(3):
                    P.add("vector", lambda e, i=i: e.tensor_scalar(
                        Avec[:, i, :, :], modT[:, (3 * i + 1) * KC:(3 * i + 2) * KC, :], 1.0, None, ALU.add),
                        [r_mod], [r_mod])
                    for j in range(2):
                        P.add("vector", lambda e, i=i, j=j: e.tensor_tensor(
                            Avec[:, i, :, j], Avec[:, i, :, j], nrm[:, l, i, :], ALU.mult), [r_mod, r_const], [r_mod])
                    P.add("vector", lambda e, i=i: e.tensor_scalar(
                        Hg[:, i, :, :], modT[:, (3 * i + 2) * KC:(3 * i + 3) * KC, :], 0.5 if i != 1 else 1.0, None,
                        ALU.mult), [r_mod], [r_mod])

        def norm_tiles(C, Xsrc, xkey, i, xn, r_xn, tiles=NORM_TILES):
            xr = C.ring_sb(2, [128, KC, 256], F32, "xt")
            sq = C.ring_sb(3, [128, 512], F32, "sq")
            tmp = C.ring_sb(3, [128, 512], F32, "ntmp")
            rs_ = C.ring_sb(2, [128, 512], F32, "rstd")
            pss = C.ring_ps(2, [128, 512], F32, "ssq")
            for ti, (c0, w) in enumerate(tiles):
                j = 1 if c0 < T_CTX else 0
                xt, rx = xr.next()
                P.dma("sync", xt[:, :, :w], Xsrc[:, c0:c0 + w].rearrange("(kc p) n -> p kc n", p=128),
                      [R((xkey, kc_)) for kc_ in range(KC)], [rx])
                ps, rp = pss.next()
                for kc in range(KC):
                    s, rsq = sq.next()
                    P.act("scalar", s[:, :w], xt[:, kc, :w], AF.Square, [rx], [rsq])
                    P.mm(ps[:, :w], ones_f[:], s[:, :w], kc == 0, kc == KC - 1, [r_const, rsq], [rp])
                rstd, rr = rs_.next()
                P.act("scalar", rstd[:, :w], ps[:, :w], AF.Sqrt, [rp, r_const], [rr], bias=epst[:, 0:1], scale=1.0 / D)
                P.add("vector", lambda e, rstd=rstd, w=w: e.reciprocal(rstd[:, :w], rstd[:, :w]), [rr], [rr])
                for kc in range(KC):
                    t, rt = tmp.next()
                    P.tt("vector", t[:, :w], xt[:, kc, :w], rstd[:, :w], ALU.mult, [rx, rr], [rt])
                    if kc % 2:
                        P.ts("vector", xn[:, kc, c0:c0 + w], t[:, :w], Avec[:, i, kc, j:j + 1], modT[:, 3 * i * KC + kc, j:j + 1],
                             ALU.mult, ALU.add, [rt, r_mod], [r_xn])
                    else:
                        P.act("scalar", xn[:, kc, c0:c0 + w], t[:, :w], AF.Identity, [rt, r_mod], [r_xn],
                              bias=modT[:, 3 * i * KC + kc, j:j + 1], scale=Avec[:, i, kc, j:j + 1])

        def ffn(l, f, Xsrc, xkey, Xdst, dkey, final_out=False):
            i = 0 if f == 0 else 2
            w_in = ffn_w_in[f]
            w_out = ffn_w_out[f]
            with B.phase() as C:
                xn = C.sb([128, KC, TOK], BF16, "xn"); r_xn = Res()
                ftiles = TOK_TILES[1:] if final_out else TOK_TILES
                norm_tiles(C, Xsrc, xkey, i, xn, r_xn, tiles=(NORM_TILES[1:] if final_out else NORM_TILES))
                war = C.ring_sb(2, [128, KC, 256], BF16, "wa")
                wbr = C.ring_sb(2, [128, KC, 256], BF16, "wb")
                pa = C.ring_ps(2, [128, 512], F32, "pa")
                pb = C.ring_ps(2, [128, 512], F32, "pb")
                sar = C.ring_sb(3, [128, 512], F32, "sa")
                gst = C.ring_sb(2, [128, TOK], BF16, "gst")
                for blk in range(DFF // 256):
                    wa, rwa = war.next()
                    wb, rwb = wbr.next()
                    P.dma("gpsimd", wa[:], w_in[l, :, blk * 256:(blk + 1) * 256].rearrange("(kc p) n -> p kc n", p=128),
                          [], [rwa])
                    P.dma("gpsimd", wb[:], w_in[l, :, DFF + blk * 256:DFF + (blk + 1) * 256].rearrange(
                        "(kc p) n -> p kc n", p=128), [], [rwb])
                    for s_ in range(2):
                        g, rg = gst.next()
                        for (c0, w) in ftiles:
                            a, ra = pa.next()
                            b_, rb = pb.next()
                            for kc in range(KC):
                                P.mm(a[:, :w], wa[:, kc, s_ * 128:(s_ + 1) * 128], xn[:, kc, c0:c0 + w], kc == 0,
                                     kc == KC - 1, [rwa, r_xn], [ra])
                            for kc in range(KC):
                                P.mm(b_[:, :w], wb[:, kc, s_ * 128:(s_ + 1) * 128], xn[:, kc, c0:c0 + w], kc == 0,
                                     kc == KC - 1, [rwb, r_xn], [rb])
                            sa, rsa = sar.next()
                            P.act("scalar", sa[:, :w], a[:, :w], AF.Silu, [ra], [rsa])
                            P.add("vector", lambda e, g=g, sa=sa, b_=b_, c0=c0, w=w: e.tensor_tensor(
                                g[:, c0:c0 + w], sa[:, :w], b_[:, :w], ALU.mult), [rsa, rb], [rg])
                        row = blk * 256 + s_ * 128
                        P.dma("sync", GT[row:row + 128, :], g[:], [rg], [R(("GT", row // 128))])
            groups = [[(0, 256), (256, 512), (768, 512)], [(1280, 512), (1792, 512)]]
            if final_out:
                groups = [[(256, 512), (768, 512)], [(1280, 512), (1792, 512)]]
            with B.phase() as C:
                NK = DFF // 128
                gt = C.sb([128, NK, 1280], BF16, "gt"); r_gt = Res()
                wor = C.ring_sb(3, [128, NK, 256], BF16, "wo")
                py = C.ring_ps(3, [128, 512], F32, "py")
                xrr = C.ring_sb(3, [128, 512], F32, "xres")
                orr = C.ring_sb(3, [128, 512], F32, "ores")
                for grp in groups:
                    g0 = grp[0][0]
                    gw = sum(w for _, w in grp)
                    P.dma("sync", gt[:, :, :gw], GT[:, g0:g0 + gw].rearrange("(k p) n -> p k n", p=128),
                          [R(("GT", k)) for k in range(NK)], [r_gt])
                    for nb in range(D // 256):
                        wo, rwo = wor.next()
                        P.dma("gpsimd", wo[:], w_out[l, :, nb * 256:(nb + 1) * 256].rearrange("(k p) n -> p k n", p=128),
                              [], [rwo])
                        for s_ in range(2):
                            fc = nb * 2 + s_
                            for (c0, w) in grp:
                                j = 1 if c0 < T_CTX else 0
                                y, ry = py.next()
                                for k in range(NK):
                                    P.mm(y[:, :w], wo[:, k, s_ * 128:(s_ + 1) * 128], gt[:, k, c0 - g0:c0 - g0 + w],
                                         k == 0, k == NK - 1, [rwo, r_gt], [ry])
                                xr_, rxr = xrr.next()
                                P.dma("scalar", xr_[:, :w], Xsrc[fc * 128:(fc + 1) * 128, c0:c0 + w],
                                      [R((xkey, fc))], [rxr])
                                o, ro = orr.next()
                                P.add("vector", lambda e, o=o, y=y, xr_=xr_, fc=fc, j=j, w=w: e.scalar_tensor_tensor(
                                    o[:, :w], y[:, :w], Hg[:, i, fc, j:j + 1], xr_[:, :w], ALU.mult, ALU.add),
                                    [ry, rxr, r_mod], [ro])
                                if final_out:
                                    if c0 >= T_CTX:
                                        P.dma("sync", outT[fc * 128:(fc + 1) * 128, c0 - T_CTX:c0 - T_CTX + w], o[:, :w],
                                              [ro], [R(("out", fc))])
                                else:
                                    P.dma("sync", Xdst[fc * 128:(fc + 1) * 128, c0:c0 + w], o[:, :w],
                                          [ro], [R((dkey, fc))])

        def inproj(l, Xsrc, xkey):
            with B.phase() as C:
                xn = C.sb([128, KC, TOK], BF16, "xn"); r_xn = Res()
                norm_tiles(C, Xsrc, xkey, 1, xn, r_xn)
                wr = C.ring_sb(2, [128, KC, 256], BF16, "wi")
                pz = C.ring_ps(4, [128, 512], F32, "pz")
                zst = C.ring_sb(2, [128, TOK], F32, "zst")
                cnt = 0
                for c0 in range(0, IN_COLS, 256):
                    ncol = min(256, IN_COLS - c0)
                    wt, rw = wr.next()
                    P.dma("gpsimd", wt[:, :, :ncol], w_in_all[l, :, c0:c0 + ncol].rearrange("(kc p) n -> p kc n", p=128),
                          [], [rw])
                    for s0 in range(0, ncol, 128):
                        m = min(128, ncol - s0)
                        z, rz = zst.next()
                        for (t0, w) in TOK_TILES:
                            ps, rp = pz.next()
                            for kc in range(KC):
                                P.mm(ps[:m, :w], wt[:, kc, s0:s0 + m], xn[:, kc, t0:t0 + w], kc == 0, kc == KC - 1,
                                     [rw, r_xn], [rp])
                            cnt += 1
                            if cnt % 2:
                                P.act("scalar", z[:m, t0:t0 + w], ps[:m, :w], AF.Copy, [rp], [rz])
                            else:
                                P.add("vector", lambda e, z=z, ps=ps, m=m, t0=t0, w=w: e.tensor_copy(
                                    z[:m, t0:t0 + w], ps[:m, :w]), [rp], [rz])
                        row = c0 + s0
                        P.dma("sync", ZT[row:row + m, :], z[:m, :], [rz], [R("ZT")])

        KNRES = {}

        def qk_norm_rope(C, rows0, gain_ap, dst, r_dst, rings):
            zr, sqr, p6, rsr, knr, t1r = rings
            zt, rz = zr.next()
            P.dma("sync", zt[:], ZT[rows0:rows0 + 128, :], [R("ZT")], [rz])
            kn, _ = knr.next()

            def tile(t0, w):
                rkn = KNRES.setdefault((kn.name, t0), Res())
                s, rsq = sqr.next()
                P.act("scalar", s[:, :w], zt[:, t0:t0 + w], AF.Square, [rz], [rsq])
                yield
                ps, rp = p6.next()
                P.mm(ps[:, :w], ones_f[:], s[:, :w], True, True, [r_const, rsq], [rp])
                yield
                rstd, rr = rsr.next()
                P.act("scalar", rstd[:, :w], ps[:, :w], AF.Sqrt, [rp, r_const], [rr], bias=epst[:, 0:1], scale=1.0 / 128)
                yield
                P.add("vector", lambda e: e.reciprocal(rstd[:, :w], rstd[:, :w]), [rr], [rr])
                yield
                P.stt("vector", kn[:, t0:t0 + w], zt[:, t0:t0 + w], gain_ap, rstd[:, :w], ALU.mult, ALU.mult,
                      [rz, rr, r_const], [rkn])
                yield
                if t0 < T_CTX:
                    P.cp("vector", dst[:, t0:t0 + w], kn[:, t0:t0 + w], [rkn], [r_dst])
                else:
                    pr, rpr = p6.next()
                    P.mm(pr[:, :w], pm_f[:], kn[:, t0:t0 + w], True, True, [r_const, rkn], [rpr])
                    t1, rt1 = t1r.next()
                    l0 = t0 - T_CTX
                    P.tt("gpsimd", t1[:, :w], kn[:, t0:t0 + w], ROPE['cos'][:, l0:l0 + w], ALU.mult, [rkn, ROPE['r']], [rt1])
                    yield
                    t2, rt2 = t1r.next()
                    P.tt("vector", t2[:, :w], pr[:, :w], ROPE['sin'][:, l0:l0 + w], ALU.mult, [rpr, ROPE['r']], [rt2])
                    yield
                    P.tt("vector", dst[:, t0:t0 + w], t1[:, :w], t2[:, :w], ALU.add, [rt1, rt2], [r_dst])
                yield

            lockstep([tile(t0, w) for (t0, w) in TOK_TILES])

        def attn_phase(l):
            ZK, ZV, ZQ = 2080, 2336, 5664
            with B.phase() as C:
                rtab = C.sb([128, 2, T_LAT], F32, "rtab")
                ROPE['cos'] = rtab[:, 0, :]; ROPE['sin'] = rtab[:, 1, :]; ROPE['r'] = Res()
                P.dma("sync", rtab[:], ropetab, [], [ROPE['r']])
                kT = [C.sb([128, TOK], BF16, "kT") for _ in range(2)]; r_k = [Res(), Res()]
                qT = [C.sb([128, TOK], BF16, "qT") for _ in range(4)]; r_q = [Res() for _ in range(4)]
                vtm = [C.sb([128, 18, 128], BF16, "vtm") for _ in range(2)]; r_v = [Res(), Res()]
                p6 = C.ring_ps(6, [128, 512], F32, "p6")
                psr = Ring(p6.tiles[2:4]); psr.res = p6.res[2:4]
                po, r_po = p6.tiles[4], p6.res[4]
                pd, r_pd = p6.tiles[5], p6.res[5]
                rings = (C.ring_sb(2, [128, TOK], F32, "zr"), C.ring_sb(5, [128, 512], F32, "sq"),
                         p6, C.ring_sb(5, [128, 512], F32, "rs"),
                         C.ring_sb(2, [128, TOK], F32, "kn"),
                         C.ring_sb(10, [128, 512], F32, "t1"))
                for hk in range(2):
                    qk_norm_rope(C, ZK + hk * 128, atn[:, l, 1:2], kT[hk], r_k[hk], rings)
                for h in range(4):
                    qk_norm_rope(C, ZQ + h * 128, atn[:, l, 0:1], qT[h], r_q[h], rings)
                vb = C.ring_sb(2, [128, TOK], BF16, "vb")
                ptr = C.ring_ps(2, [128, 128], BF16, "ptr")
                for hk in range(2):
                    zt, rz = rings[0].next()
                    P.dma("sync", zt[:], ZT[ZV + hk * 128:ZV + (hk + 1) * 128, :], [R("ZT")], [rz])
                    v, rv = vb.next()
                    P.add("vector", lambda e, v=v, zt=zt: e.tensor_copy(v[:], zt[:]), [rz], [rv])
                    for kc in range(18):
                        pt, rpt = ptr.next()
                        P.tr(pt[:], v[:, kc * 128:(kc + 1) * 128], ident_b[:], [rv, r_const], [rpt])
                        P.add("vector", lambda e, pt=pt, hk=hk, kc=kc: e.tensor_copy(
                            vtm[hk][:, kc, :], pt[:]), [rpt], [r_v[hk]])
                er = C.ring_sb(3, [128, 512], BF16, "e")
                rdr = C.ring_sb(2, [128, 512], F32, "rden")
                yst = C.ring_sb(2, [128, 512], BF16, "yst")
                for h in range(4):
                    hk = h // 2
                    for (t0, w) in TOK_TILES:
                        nkc = 2 if t0 < T_CTX else 18
                        for kc in range(nkc):
                            ps, rp = psr.next()
                            P.mm(ps[:, :w], kT[hk][:, kc * 128:(kc + 1) * 128], qT[h][:, t0:t0 + w], True, True,
                                 [r_k[hk], r_q[h]], [rp])
                            e_, re_ = er.next()
                            P.act("scalar", e_[:, :w], ps[:, :w], AF.Exp, [rp], [re_], scale=128 ** -0.5)
                            P.mm(po[:, :w], vtm[hk][:, kc, :], e_[:, :w], kc == 0, kc == nkc - 1, [r_v[hk], re_], [r_po])
                            P.mm(pd[:, :w], ones_b[:], e_[:, :w], kc == 0, kc == nkc - 1, [r_const, re_], [r_pd])
                        rd, rrd = rdr.next()
                        P.add("vector", lambda e, rd=rd, w=w: e.reciprocal(rd[:, :w], pd[:, :w]), [r_pd], [rrd])
                        y, ry = yst.next()
                        P.add("vector", lambda e, y=y, rd=rd, w=w: e.tensor_tensor(y[:, :w], po[:, :w], rd[:, :w], ALU.mult),
                              [r_po, rrd], [ry])
                        P.dma("sync", YT[3, h * 128:(h + 1) * 128, t0:t0 + w], y[:, :w], [ry], [R(("YT", 3))])

        def gelu_tiles(C, src, r_src, dst, r_dst, rings):
            g1 = rings
            for (t0, w) in TOK_TILES:
                a, ra = g1.next()
                P.act("scalar", a[:, :w], src[:, t0:t0 + w], AF.Square, [r_src], [ra])
                P.add("vector", lambda e, a=a, w=w: e.tensor_scalar(a[:, :w], a[:, :w], 0.044715, 1.0, ALU.mult, ALU.add),
                      [ra], [ra])
                P.add("vector", lambda e, a=a, w=w, t0=t0: e.tensor_tensor(a[:, :w], a[:, :w], src[:, t0:t0 + w], ALU.mult),
                      [ra, r_src], [ra])
                P.act("scalar", a[:, :w], a[:, :w], AF.Sigmoid, [ra], [ra], scale=1.5957691216057308)
                P.add("vector", lambda e, a=a, w=w, t0=t0: e.tensor_tensor(dst[:, t0:t0 + w], a[:, :w], src[:, t0:t0 + w],
                                                                      ALU.mult), [ra, r_src], [r_dst])

        def sg_phase(l):
            ZU, ZVV = 2592, 3104
            with B.phase() as C:
                g1 = C.ring_sb(3, [128, 512], F32, "g1")
                zr = C.ring_sb(2, [128, TOK], F32, "zr")
                gu = [C.sb([128, TOK], F32, "gu") for _ in range(4)]; r_gu = [Res() for _ in range(4)]
                gv = [C.sb([128, TOK], F32, "gv") for _ in range(4)]; r_gv = [Res() for _ in range(4)]
                wst = C.sb([128, 4, 128], BF16, "wst"); r_w = Res()
                bs = C.sb([1, 4, 128], BF16, "bs"); r_b = Res()
                P.dma("gpsimd", wst[:], sgwT[l].rearrange("g s t -> s g t"), [], [r_w])
                P.dma("gpsimd", bs[:], sg_b[l:l + 1], [], [r_b])
                for g in range(4):
                    zt, rz = zr.next()
                    P.dma("sync", zt[:], ZT[ZU + g * 128:ZU + (g + 1) * 128, :], [R("ZT")], [rz])
                    gelu_tiles(C, zt, rz, gu[g], r_gu[g], g1)
                    zt, rz = zr.next()
                    P.dma("sync", zt[:], ZT[ZVV + g * 128:ZVV + (g + 1) * 128, :], [R("ZT")], [rz])
                    gelu_tiles(C, zt, rz, gv[g], r_gv[g], g1)
                rstd = C.sb([128, TOK], F32, "rstd"); r_rs = Res()
                pss = C.ring_ps(2, [128, 512], F32, "pss")
                for (t0, w) in TOK_TILES:
                    ps, rp = pss.next()
                    for g in range(4):
                        a, ra = g1.next()
                        P.act("scalar", a[:, :w], gv[g][:, t0:t0 + w], AF.Square, [r_gv[g]], [ra])
                        P.mm(ps[:, :w], ones_f[:], a[:, :w], g == 0, g == 3, [r_const, ra], [rp])
                    P.act("scalar", rstd[:, t0:t0 + w], ps[:, :w], AF.Sqrt, [rp, r_const], [r_rs], bias=epst[:, 0:1],
                          scale=1.0 / 512)
                    P.add("vector", lambda e, t0=t0, w=w: e.reciprocal(rstd[:, t0:t0 + w], rstd[:, t0:t0 + w]), [r_rs], [r_rs])
                vnr = C.ring_sb(2, [128, TOK], BF16, "vn")
                ptr = C.ring_ps(3, [128, 128], BF16, "ptr")
                vtr = C.ring_sb(3, [128, 128], BF16, "vt")
                pso = C.ring_ps(3, [128, 128], F32, "pso")
                ysr = C.ring_sb(2, [128, TOK], BF16, "ys")
                for g in range(4):
                    vn, rvn = vnr.next()
                    P.stt("vector", vn[:], gv[g][:], sgn[:, l, g:g + 1], rstd[:], ALU.mult, ALU.mult, [r_gv[g], r_rs, r_const], [rvn])
                    ys, rys = ysr.next()

                    def sgu(g, n, vn, rvn, ys, rys):
                        pt, rpt = ptr.next()
                        P.tr(pt[:], vn[:, n * 128:(n + 1) * 128], ident_b[:], [rvn, r_const], [rpt])
                        yield
                        vt, rvt = vtr.next()
                        P.cp("vector", vt[:], pt[:], [rpt], [rvt])
                        yield
                        po_, rpo = pso.next()
                        P.mm(po_[:], vt[:], wst[:, g, :], True, False, [rvt, r_w], [rpo])
                        P.mm(po_[:], ones_b[0:1, :], bs[0:1, g, :], False, True, [r_const, r_b], [rpo])
                        yield
                        P.tt("vector", ys[:, n * 128:(n + 1) * 128], po_[:], gu[g][:, n * 128:(n + 1) * 128], ALU.mult,
                             [rpo, r_gu[g]], [rys])
                        yield
                    for n0 in range(0, 18, 3):
                        lockstep([sgu(g, n, vn, rvn, ys, rys) for n in range(n0, n0 + 3)])
                    P.dma("sync", YT[0, g * 128:(g + 1) * 128, :], ys[:], [rys], [R(("YT", 0))])

        def lockstep(gens):
            gens = list(gens)
            while gens:
                alive = []
                for g_ in gens:
                    try:
                        next(g_)
                        alive.append(g_)
                    except StopIteration:
                        pass
                gens = alive

        def mlstm_phase(l):
            ZK, ZV_, ZIF, ZQ, ZO = 0, 512, 1024, 3616, 4128
            with B.phase() as C:
                prs = [C.ring_ps(2, [128, 512], F32, "mp") for _ in range(4)]
                pr = prs[0]
                zr = C.ring_sb(2, [128, TOK], F32, "zr")
                ift_t, r_ift = zr.next()
                P.dma("sync", ift_t[0:16, :], ZT[ZIF:ZIF + 16, :], [R("ZT")], [r_ift])
                if_tm = C.sb([128, 18, 16], F32, "if_tm"); r_if = Res()
                lf_tm = C.sb([128, 18, 8], F32, "lf_tm"); r_lf = Res()
                b_tm = C.sb([128, 18, 8], F32, "b_tm"); r_b = Res()
                w_tm = C.sb([128, 18, 8], F32, "w_tm"); r_w = Res()
                for n in range(18):
                    pt, rpt = pr.next()
                    P.mm(pt[:, 0:16], ift_t[0:16, n * 128:(n + 1) * 128], cst[0:16, 0, 0:16], True, True, [r_ift, r_const], [rpt])
                    P.tt("vector", if_tm[:, n, :], pt[:, 0:16], mlb[:, l, :], ALU.add, [rpt, r_const], [r_if])
                P.act("scalar", lf_tm[:], if_tm[:, :, 8:16], AF.Exp, [r_if], [r_lf], scale=-1.0)
                P.act("scalar", lf_tm[:], lf_tm[:], AF.Ln, [r_lf, r_const], [r_lf], bias=ones_f[:, 0:1])
                P.ts("vector", lf_tm[:], lf_tm[:], -1.0, None, ALU.mult, None, [r_lf], [r_lf])
                for n in range(18):
                    pt, rpt = pr.next()
                    P.mm(pt[:, 0:4], cst[:, 3, :], lf_tm[:, n, 0:4], True, True, [r_const, r_lf], [rpt])
                    P.mm(pt[:, 4:8], cst[:, 2, :], lf_tm[:, n, 4:8], True, True, [r_const, r_lf], [rpt])
                    P.cp("vector", b_tm[:, n, :], pt[:, 0:8], [rpt], [r_b])
                P.tt("vector", w_tm[:], if_tm[:, :, 0:8], b_tm[:], ALU.subtract, [r_if, r_b], [r_w])
                P.act("scalar", w_tm[:], w_tm[:], AF.Exp, [r_w], [r_w])
                kTs = [C.sb([128, TOK], BF16, "kT") for _ in range(2)]; r_kT = [Res(), Res()]
                qTs = [C.sb([128, TOK], BF16, "qT") for _ in range(2)]; r_qT = [Res(), Res()]
                ktms = [C.sb([128, 18, 128], BF16, "ktm") for _ in range(2)]; r_ktm = [Res(), Res()]
                vtms = [C.sb([128, 18, 128], BF16, "vtm") for _ in range(2)]; r_vtm = [Res(), Res()]
                hTs = [C.sb([128, 2, TOK], BF16, "hT") for _ in range(2)]; r_hT = [Res(), Res()]
                hsum = C.ring_sb(1, [128, TOK], F32, "hsum")
                sqr = C.ring_sb(2, [128, 512], F32, "sq")
                rsr = C.ring_sb(2, [128, 512], F32, "rs")
                ysr = C.ring_sb(1, [128, TOK], BF16, "ys")

                class T_:
                    pass
                chains_t = []
                for ci in range(4):
                    t = T_()
                    mk = lambda name, shape, dt, n=2: C.ring_sb(n, shape, dt, name)
                    t.lfr = mk("lfrep", [128, 128], F32); t.er = mk("erep", [128, 128], F32)
                    t.ptr = mk("ptm", [128, 128], BF16); t.vpr = mk("vp", [128, 129], BF16)
                    t.wrr = mk("wrep", [128, 128], BF16); t.t1r = mk("t1", [128, 128], F32); t.t2r = mk("t2", [128, 128], F32)
                    t.CTa = C.sb([128, 129], F32, "CTa"); t.r_CT = Res()
                    t.CTb = C.sb([128, 128], BF16, "CTb"); t.r_CTb = Res()
                    t.Nrep = C.sb([128, 128], BF16, "Nrep"); t.r_N = Res()
                    t.pr = prs[ci]
                    chains_t.append(t)
                orders = [list(range(18)), [1, 0] + list(range(17, 1, -1))]

                def unit(t, hi, h, d_, c):
                    kT, rk = kTs[hi], r_kT[hi]
                    qT, rq = qTs[hi], r_qT[hi]
                    ktm, rktm = ktms[hi], r_ktm[hi]
                    vtm, rvtm = vtms[hi], r_vtm[hi]
                    hT, rh = hTs[hi], r_hT[hi]
                    pr_ = t.pr
                    cs = slice(c * 128, (c + 1) * 128)
                    col = d_ * 4 + h
                    U = cst[:, 3, :] if d_ == 0 else cst[:, 2, :]
                    wcol = w_tm[:, c, col:col + 1]
                    lfp, rlfp = t.lfr.next()
                    P.act("scalar", lfp[:], ones_f[:], AF.Identity, [r_const, r_lf], [rlfp], scale=lf_tm[:, c, col:col + 1])
                    vp, rvp = t.vpr.next()
                    P.ts("vector", vp[:, 0:128], vtm[:, c, :], wcol, None, ALU.mult, None, [rvtm, r_w], [rvp])
                    P.cp("gpsimd", vp[:, 128:129], wcol, [r_w], [rvp])
                    wrep, rwr = t.wrr.next()
                    P.act("scalar", wrep[:], ones_f[:], AF.Identity, [r_const, r_w], [rwr], scale=wcol)
                    pB, rpB = pr_.next()
                    P.mm(pB[:, 0:128], kT[:, cs], qT[:, cs], True, True, [rk, rq], [rpB])
                    yield
                    pA, rpA = pr_.next()
                    P.mm(pA[:, 0:128], lfp[:], U, True, True, [rlfp, r_const], [rpA])
                    ptm, rptm = t.ptr.next()
                    P.tt("vector", ptm[:], pB[:, 0:128], U, ALU.mult, [rpB, r_const], [rptm])
                    yield
                    erep, rer = t.er.next()
                    P.act("scalar", erep[:], pA[:, 0:128], AF.Exp, [rpA], [rer])
                    eb = erep[:, 127:128] if d_ == 0 else erep[:, 0:1]
                    pC, rpC = pr_.next()
                    P.mm(pC[:, 0:128], vp[:, 0:128], ptm[:], True, False, [rvp, rptm], [rpC])
                    P.mm(pC[:, 0:128], t.CTb[:], qT[:, cs], False, True, [t.r_CTb, rq], [rpC])
                    yield
                    pD, rpD = pr_.next()
                    P.mm(pD[:, 0:128], wrep[:], ptm[:], True, False, [rwr, rptm], [rpD])
                    P.mm(pD[:, 0:128], t.Nrep[:], qT[:, cs], False, True, [t.r_N, rq], [rpD])
                    t1, rt1 = t.t1r.next()
                    P.tt("vector", t1[:], pC[:, 0:128], erep[:], ALU.mult, [rpC, rer], [rt1])
                    yield
                    t2, rt2 = t.t2r.next()
                    P.tt("vector", t2[:], pD[:, 0:128], erep[:], ALU.mult, [rpD, rer], [rt2])
                    pE, rpE = pr_.next()
                    P.mm(pE[:, 0:129], ktm[:, c, :], vp[:], True, True, [rktm, rvp], [rpE])
                    yield
                    P.act("scalar", t2[:], t2[:], AF.Abs, [rt2], [rt2])
                    P.tt("vector", t.CTa[:], t.CTa[:], pE[:, 0:129], ALU.add, [rpE, t.r_CT], [t.r_CT])
                    yield
                    P.ts("vector", t2[:], t2[:], 1.0, None, ALU.max, None, [rt2], [rt2])
                    P.ts("vector", t.CTa[:], t.CTa[:], eb, None, ALU.mult, None, [t.r_CT, rer], [t.r_CT])
                    yield
                    P.add("vector", lambda e, t2=t2: e.reciprocal(t2[:], t2[:]), [rt2], [rt2])
                    P.act("scalar", t.CTb[:], t.CTa[:, 0:128], AF.Copy, [t.r_CT], [t.r_CTb])
                    P.ts("gpsimd", t.Nrep[:], ones_f[:], t.CTa[:, 128:129], None, ALU.mult, None, [r_const, t.r_CT], [t.r_N])
                    yield
                    P.tt("vector", hT[:, d_, cs], t1[:], t2[:], ALU.mult, [rt1, rt2], [rh])
                    yield

                def chain(t, hi, h, d_):
                    for step in range(18):
                        yield from unit(t, hi, h, d_, orders[d_][step])

                for heads in [(0, 1), (2, 3)]:
                    for hi, h in enumerate(heads):
                        zt, rz = zr.next()
                        P.dma("sync", zt[:], ZT[ZK + h * 128:ZK + (h + 1) * 128, :], [R("ZT")], [rz])
                        P.act("scalar", kTs[hi][:], zt[:], AF.Copy, [rz], [r_kT[hi]], scale=128 ** -0.5)
                        for n in range(18):
                            pt, rpt = prs[n % 4].next()
                            P.mm(pt[:, 0:128], zt[:, n * 128:(n + 1) * 128], cst[:, 0, :], True, True, [rz, r_const], [rpt])
                            if n % 2:
                                P.act("scalar", ktms[hi][:, n, :], pt[:, 0:128], AF.Copy, [rpt], [r_ktm[hi]], scale=128 ** -0.5)
                            else:
                                P.ts("vector", ktms[hi][:, n, :], pt[:, 0:128], 128 ** -0.5, None, ALU.mult, None, [rpt], [r_ktm[hi]])
                        zt, rz = zr.next()
                        P.dma("sync", zt[:], ZT[ZQ + h * 128:ZQ + (h + 1) * 128, :], [R("ZT")], [rz])
                        P.cp("vector", qTs[hi][:], zt[:], [rz], [r_qT[hi]])
                        zt, rz = zr.next()
                        P.dma("sync", zt[:], ZT[ZV_ + h * 128:ZV_ + (h + 1) * 128, :], [R("ZT")], [rz])
                        for n in range(18):
                            pt, rpt = prs[n % 4].next()
                            P.mm(pt[:, 0:128], zt[:, n * 128:(n + 1) * 128], cst[:, 0, :], True, True, [rz, r_const], [rpt])
                            if n % 2:
                                P.act("scalar", vtms[hi][:, n, :], pt[:, 0:128], AF.Copy, [rpt], [r_vtm[hi]])
                            else:
                                P.cp("vector", vtms[hi][:, n, :], pt[:, 0:128], [rpt], [r_vtm[hi]])
                    gens = []
                    for hi, h in enumerate(heads):
                        for d_ in range(2):
                            t = chains_t[hi * 2 + d_]
                            P.add("gpsimd", lambda e, t=t: e.memset(t.CTa[:], 0.0), [], [t.r_CT])
                            P.add("gpsimd", lambda e, t=t: e.memset(t.CTb[:], 0.0), [], [t.r_CTb])
                            P.add("gpsimd", lambda e, t=t: e.memset(t.Nrep[:], 0.0), [], [t.r_N])
                            gens.append(chain(t, hi, h, d_))
                    lockstep(gens)
                    for hi, h in enumerate(heads):
                        hT, rh = hTs[hi], r_hT[hi]
                        zt, rz = zr.next()
                        P.dma("sync", zt[:], ZT[ZO + h * 128:ZO + (h + 1) * 128, :], [R("ZT")], [rz])
                        P.act("scalar", zt[:], zt[:], AF.Sigmoid, [rz], [rz])
                        hs, rhs = hsum.next()
                        P.tt("vector", hs[:], hT[:, 0, :], hT[:, 1, :], ALU.add, [rh], [rhs])
                        ys, rys = ysr.next()
                        for (t0, w) in TOK_TILES:
                            s_, rsq = sqr.next()
                            P.act("scalar", s_[:, :w], hs[:, t0:t0 + w], AF.Square, [rhs], [rsq])
                            ps, rp = pr.next()
                            P.mm(ps[:, :w], ones_f[:], s_[:, :w], True, True, [r_const, rsq], [rp])
                            rstd, rr = rsr.next()
                            P.act("scalar", rstd[:, :w], ps[:, :w], AF.Sqrt, [rp, r_const], [rr], bias=epst[:, 0:1], scale=1.0 / 128)
                            P.add("vector", lambda e, rstd=rstd, w=w: e.reciprocal(rstd[:, :w], rstd[:, :w]), [rr], [rr])
                            P.stt("vector", rstd[:, :w], hs[:, t0:t0 + w], vec_t[:, l, 8 + h:9 + h], rstd[:, :w], ALU.mult, ALU.mult,
                                  [rhs, rr, r_const], [rr])
                            P.tt("vector", ys[:, t0:t0 + w], rstd[:, :w], zt[:, t0:t0 + w], ALU.mult, [rr, rz], [rys])
                        P.dma("sync", YT[1, h * 128:(h + 1) * 128, :], ys[:], [rys], [R(("YT", 1))])

        def gdn_phase(l):
            ZK, ZV_, ZBA, ZQ, ZZ = 1040, 1552, 2064, 4640, 5152
            NCH = 36
            with B.phase() as C:
                prs = [C.ring_ps(2, [128, 512], F32, "gp") for _ in range(4)]
                pr = prs[0]
                I64 = cst[0:64, 0, 0:64]
                zr = C.ring_sb(2, [128, TOK], F32, "zr")
                bat_t, r_bat = zr.next()
                bat = bat_t[0:16, :]
                P.dma("sync", bat, ZT[ZBA:ZBA + 16, :], [R("ZT")], [r_bat])
                ba_tm = C.sb([64, NCH, 16], F32, "ba_tm"); r_ba = Res()
                beta_tm = C.sb([64, NCH, 8], F32, "beta_tm"); r_be = Res()
                g_tm = C.sb([64, NCH, 8], F32, "g_tm"); r_g = Res()
                gc_tm = C.sb([64, NCH, 8], F32, "gc_tm"); r_gc = Res()
                bg_tm = C.sb([64, NCH, 8], F32, "bg_tm"); r_bg = Res()
                Aneg = C.sb([64, 8], F32, "Aneg"); r_A = Res()
                negm = C.sb([64, 2, 64], F32, "negm"); r_nm = Res()
                for n in range(NCH):
                    pt, rpt = pr.next()
                    P.mm(pt[0:64, 0:16], bat_t[0:16, n * 64:(n + 1) * 64], cst[0:16, 0, 0:16], True, True, [r_bat, r_const], [rpt])
                    P.add("vector", lambda e, pt=pt, n=n: e.tensor_copy(ba_tm[:, n, :], pt[0:64, 0:16]), [rpt], [r_ba])
                P.act("scalar", beta_tm[:], ba_tm[:, :, 0:8], AF.Sigmoid, [r_ba], [r_be])
                P.act("scalar", Aneg[:], gdc[0:64, l, 0:8], AF.Exp, [r_const], [r_A])
                P.add("vector", lambda e: e.tensor_scalar(Aneg[:], Aneg[:], -1.0, None, ALU.mult), [r_A], [r_A])
                for n in range(NCH):
                    P.add("vector", lambda e, n=n: e.tensor_tensor(g_tm[:, n, :], ba_tm[:, n, 8:16], gdc[0:64, l, 8:16], ALU.add),
                          [r_ba, r_const], [r_g])
                P.act("scalar", g_tm[:], g_tm[:], AF.Exp, [r_g], [r_g])
                P.act("scalar", g_tm[:], g_tm[:], AF.Ln, [r_g, r_const], [r_g], bias=ones_f[0:64, 0:1])
                for n in range(NCH):
                    P.add("vector", lambda e, n=n: e.tensor_tensor(g_tm[:, n, :], g_tm[:, n, :], Aneg[:], ALU.mult), [r_g, r_A], [r_g])
                for n in range(NCH):
                    pt, rpt = pr.next()
                    P.mm(pt[0:64, 0:4], cst[0:64, 3, 0:64], g_tm[:, n, 0:4], True, True, [r_const, r_g], [rpt])
                    P.mm(pt[0:64, 4:8], cst[0:64, 2, 0:64], g_tm[:, n, 4:8], True, True, [r_const, r_g], [rpt])
                    P.add("vector", lambda e, pt=pt, n=n: e.tensor_copy(gc_tm[:, n, :], pt[0:64, 0:8]), [rpt], [r_gc])
                P.act("scalar", bg_tm[:], gc_tm[:], AF.Exp, [r_gc], [r_bg])
                P.add("vector", lambda e: e.tensor_tensor(bg_tm[:], bg_tm[:], beta_tm[:], ALU.mult), [r_bg, r_be], [r_bg])
                P.add("vector", lambda e: e.tensor_scalar(negm[:, 0, :], cst[0:64, 2, 0:64], -1.0, 30000.0, ALU.add, ALU.mult),
                      [r_const], [r_nm])
                P.add("vector", lambda e: e.tensor_scalar(negm[:, 1, :], cst[0:64, 3, 0:64], -1.0, 30000.0, ALU.add, ALU.mult),
                      [r_const], [r_nm])
                cvr = C.ring_sb(2, [128, TOK], F32, "cv")
                kTs = [C.sb([128, TOK], BF16, "kT") for _ in range(2)]; r_kT = [Res(), Res()]
                qTs = [C.sb([128, TOK], BF16, "qT") for _ in range(2)]; r_qT = [Res(), Res()]
                ktms = [C.sb([64, NCH, 128], BF16, "ktm") for _ in range(2)]; r_ktm = [Res(), Res()]
                vtms = [C.sb([64, NCH, 128], BF16, "vtm") for _ in range(2)]; r_vtm = [Res(), Res()]
                oTs = [C.sb([128, 2, TOK], BF16, "oT") for _ in range(2)]; r_oT = [Res(), Res()]
                sqr = C.ring_sb(3, [128, 512], F32, "sq")
                rsr = C.ring_sb(3, [128, 512], F32, "rs")
                ysr = C.ring_sb(1, [128, TOK], BF16, "ys")
                class T_:
                    pass
                chains_t = []
                for ci in range(4):
                    t = T_()
                    mk = lambda name, shape, dt, n=2: C.ring_sb(n, shape, dt, name)
                    t.grr = mk("grep", [64, 128], F32); t.gcr = mk("gcrep", [128, 64], F32); t.egr = mk("egrep", [128, 64], F32)
                    t.Er = mk("E", [64, 64], F32); t.Mr = mk("M", [64, 64], F32); t.Mtr = mk("Mt", [64, 64], F32)
                    t.Br = mk("B", [64, 6, 64], F32); t.Btr = mk("Bt", [64, 5, 64], F32)
                    t.Xbr = mk("Xtb", [64, 64], BF16); t.Xgr = mk("Xtg", [64, 64], BF16)
                    t.Xr = mk("X", [64, 64], F32, 3); t.Xtr = mk("Xt", [64, 64], F32, 3)
                    t.Yr = mk("Y1", [64, 64], F32); t.Zr_ = mk("Z1", [64, 64], F32)
                    t.ur = mk("u", [64, 128], F32); t.wTr = mk("wT", [128, 64], BF16)
                    t.vnr = mk("vnew", [64, 128], BF16); t.qkr = mk("qkm", [64, 64], F32); t.qkTr = mk("qkT", [64, 64], BF16)
                    t.qdr = mk("qd", [128, 64], BF16); t.scr = mk("sc", [64, 1], F32); t.ker = mk("kend", [64, 128], BF16)
                    t.Sf = C.sb([128, 128], F32, "Sf"); t.r_S = Res()
                    t.Sb = C.sb([128, 128], BF16, "Sb"); t.r_Sb = Res()
                    t.pr = prs[ci]
                    chains_t.append(t)
                orders = [list(range(NCH)), [3, 2, 1, 0] + list(range(NCH - 1, 3, -1))]
                SEGS = [(0, T_CTX), (T_CTX, TOK)]

                def conv_silu(src, rsrc, ch):
                    y, ry = cvr.next()
                    P.add("vector", lambda e, y=y, src=src, ch=ch: e.tensor_scalar(y[:], src[:], cw[:, l, 1, ch:ch + 1], None, ALU.mult),
                          [rsrc, r_const], [ry])
                    for (a, b_) in SEGS:
                        P.add("vector", lambda e, y=y, src=src, ch=ch, a=a, b_=b_: e.scalar_tensor_tensor(
                            y[:, a + 1:b_], src[:, a:b_ - 1], cw[:, l, 0, ch:ch + 1], y[:, a + 1:b_], ALU.mult, ALU.add),
                            [rsrc, r_const, ry], [ry])
                        P.add("vector", lambda e, y=y, src=src, ch=ch, a=a, b_=b_: e.scalar_tensor_tensor(
                            y[:, a:b_ - 1], src[:, a + 1:b_], cw[:, l, 2, ch:ch + 1], y[:, a:b_ - 1], ALU.mult, ALU.add),
                            [rsrc, r_const, ry], [ry])
                    P.act("scalar", y[:], y[:], AF.Silu, [ry], [ry])
                    return y, ry

                prep_ring = Ring([t_ for r_ in prs for t_ in r_.tiles])
                prep_ring.res = [x_ for r_ in prs for x_ in r_.res]

                def l2norm(y, ry, dst, rdst, mul):
                    def tile(t0, w):
                        s_, rsq = sqr.next()
                        P.act("scalar", s_[:, :w], y[:, t0:t0 + w], AF.Square, [ry], [rsq])
                        yield
                        ps, rp = prep_ring.next()
                        P.mm(ps[:, :w], ones_f[:], s_[:, :w], True, True, [r_const, rsq], [rp])
                        yield
                        rstd, rr = rsr.next()
                        P.act("scalar", rstd[:, :w], ps[:, :w], AF.Sqrt, [rp, r_const], [rr], bias=epst[:, 0:1], scale=1.0)
                        yield
                        P.add("vector", lambda e: e.reciprocal(rstd[:, :w], rstd[:, :w]), [rr], [rr])
                        yield
                        P.tt("vector", y[:, t0:t0 + w], y[:, t0:t0 + w], rstd[:, :w], ALU.mult, [ry, rr], [ry])
                        yield
                    lockstep([tile(t0, w) for (t0, w) in TOK_TILES[0:3]])
                    lockstep([tile(t0, w) for (t0, w) in TOK_TILES[3:5]])
                    P.act("scalar", dst[:], y[:], AF.Copy, [ry], [rdst], scale=float(mul))

                def transposes(src, rsrc, dst, rdst):
                    for n in range(NCH):
                        pt, rpt = prs[n % 4].next()
                        P.mm(pt[0:64, 0:128], src[:, n * 64:(n + 1) * 64], cst[:, 0, :], True, True, [rsrc, r_const], [rpt])
                        if n % 2:
                            P.act("scalar", dst[:, n, :], pt[0:64, 0:128], AF.Copy, [rpt], [rdst])
                        else:
                            P.add("vector", lambda e, dst=dst, pt=pt, n=n: e.tensor_copy(dst[:, n, :], pt[0:64, 0:128]), [rpt], [rdst])

                def unit(t, hi, h, d_, c):
                    kT, rk = kTs[hi], r_kT[hi]
                    qT, rq = qTs[hi], r_qT[hi]
                    ktm, rktm = ktms[hi], r_ktm[hi]
                    vtm, rvtm = vtms[hi], r_vtm[hi]
                    oT, ro = oTs[hi], r_oT[hi]
                    pr_ = t.pr
                    cs = slice(c * 64, (c + 1) * 64)
                    col = d_ * 4 + h
                    U = cst[0:64, 3, 0:64] if d_ == 0 else cst[0:64, 2, 0:64]
                    Minc = cst[0:64, 2, 0:64] if d_ == 0 else cst[0:64, 3, 0:64]
                    END = 63 if d_ == 0 else 0
                    gcol = gc_tm[:, c, col:col + 1]
                    bcol = beta_tm[:, c, col:col + 1]
                    grp, rgrp = t.grr.next()
                    P.act("scalar", grp[:], ones_f[0:64, :], AF.Identity, [r_const, r_g], [rgrp], scale=g_tm[:, c, col:col + 1])
                    pk, rpk = pr_.next()
                    P.mm(pk[0:64, 0:64], kT[:, cs], kT[:, cs], True, True, [rk], [rpk])
                    yield
                    pg, rpg = pr_.next()
                    P.mm(pg[:, 0:64], grp[:], U, True, True, [rgrp, r_const], [rpg])
                    yield
                    gcrep, rgcr = t.gcr.next()
                    P.cp("vector", gcrep[:], pg[:, 0:64], [rpg], [rgcr])
                    yield
                    egrep, regr = t.egr.next()
                    P.act("scalar", egrep[:], gcrep[:], AF.Exp, [rgcr], [regr])
                    E, rE = t.Er.next()
                    P.ts("vector", E[:], gcrep[0:64, :], gcol, 0.0, ALU.subtract, ALU.max, [rgcr, r_gc], [rE])
                    sc, rsc = t.scr.next()
                    P.tt("gpsimd", sc[:], gcrep[0:64, END:END + 1], gcol, ALU.subtract, [rgcr, r_gc], [rsc])
                    yield
                    P.act("scalar", E[:], E[:], AF.Exp, [rE], [rE], scale=-1.0)
                    P.act("scalar", sc[:], sc[:], AF.Exp, [rsc], [rsc])
                    qd, rqd = t.qdr.next()
                    P.tt("gpsimd", qd[:], qT[:, cs], egrep[:], ALU.mult, [rq, regr], [rqd])
                    yield
                    P.tt("gpsimd", E[:], E[:], Minc, ALU.mult, [rE, r_const], [rE])
                    ke, rke = t.ker.next()
                    P.act("scalar", ke[:], ktm[:, c, :], AF.Identity, [rktm, rsc], [rke], scale=sc[:, 0:1])
                    yield
                    M, rM = t.Mr.next()
                    P.stt("vector", M[:], pk[0:64, 0:64], bcol, E[:], ALU.mult, ALU.mult, [rpk, rE, r_be], [rM])
                    pq, rpq = pr_.next()
                    P.mm(pq[0:64, 0:64], qT[:, cs], kT[:, cs], True, True, [rq, rk], [rpq])
                    yield
                    pmt, rpmt = pr_.next()
                    P.mm(pmt[0:64, 0:64], M[:], I64, True, True, [rM, r_const], [rpmt])
                    qkm, rqk = t.qkr.next()
                    P.tt("vector", qkm[:], pq[0:64, 0:64], E[:], ALU.mult, [rpq, rE], [rqk])
                    yield
                    pqt, rpqt = pr_.next()
                    P.mm(pqt[0:64, 0:64], qkm[:], I64, True, True, [rqk, r_const], [rpqt])
                    Ball, rB = t.Br.next()
                    P.tt("vector", Ball[:], M[:].unsqueeze(1).to_broadcast([64, 6, 64]), lm[:, 0:6, :], ALU.mult, [rM, r_const], [rB])
                    yield
                    Mt, rMt = t.Mtr.next()
                    P.cp("vector", Mt[:], pmt[0:64, 0:64], [rpmt], [rMt])
                    qkT, rqkT = t.qkTr.next()
                    P.act("scalar", qkT[:], pqt[0:64, 0:64], AF.Copy, [rpqt], [rqkT])
                    X, rX = t.Xr.next()
                    P.tt("vector", X[:], I64, Ball[:, 0, :], ALU.subtract, [rB, r_const], [rX])
                    yield
                    Btall, rBt = t.Btr.next()
                    P.tt("gpsimd", Btall[:], Mt[:].unsqueeze(1).to_broadcast([64, 5, 64]), lm[:, 0:5, :], ALU.mult, [rMt, r_const], [rBt])
                    yield
                    Xt, rXt = t.Xtr.next()
                    P.tt("gpsimd", Xt[:], I64, Btall[:, 0, :], ALU.subtract, [rBt, r_const], [rXt])
                    yield
                    for lv in range(1, 6):
                        pz1, rpz1 = pr_.next()
                        P.mm(pz1[0:64, 0:64], Ball[:, lv, :], Xt[:], True, True, [rB, rXt], [rpz1])
                        if lv < 5:
                            py1, rpy1 = pr_.next()
                            P.mm(py1[0:64, 0:64], Btall[:, lv, :], X[:], True, True, [rBt, rX], [rpy1])
                        yield
                        Z1, rZ1 = t.Zr_.next()
                        P.act("scalar", Z1[:], pz1[0:64, 0:64], AF.Copy, [rpz1], [rZ1])
                        if lv < 5:
                            Y1, rY1 = t.Yr.next()
                            P.cp("vector", Y1[:], py1[0:64, 0:64], [rpy1], [rY1])
                        yield
                        pz2, rpz2 = pr_.next()
                        P.mm(pz2[0:64, 0:64], X[:], Z1[:], True, True, [rX, rZ1], [rpz2])
                        if lv < 5:
                            py2, rpy2 = pr_.next()
                            P.mm(py2[0:64, 0:64], Xt[:], Y1[:], True, True, [rXt, rY1], [rpy2])
                        yield
                        Xtn, rXtn = t.Xtr.next()
                        P.tt("vector", Xtn[:], Xt[:], pz2[0:64, 0:64], ALU.subtract, [rXt, rpz2], [rXtn])
                        if lv < 5:
                            Xn, rXn = t.Xr.next()
                            P.tt("vector", Xn[:], X[:], py2[0:64, 0:64], ALU.subtract, [rX, rpy2], [rXn])
                            X, rX = Xn, rXn
                        Xt, rXt = Xtn, rXtn
                        yield
                    Xtb, rXtb = t.Xbr.next()
                    P.act("scalar", Xtb[:], Xt[:], AF.Identity, [rXt, r_be], [rXtb], scale=bcol)
                    Xtg, rXtg = t.Xgr.next()
                    P.act("scalar", Xtg[:], Xt[:], AF.Identity, [rXt, r_bg], [rXtg], scale=bg_tm[:, c, col:col + 1])
                    yield
                    pu, rpu = pr_.next()
                    P.mm(pu[0:64, 0:128], Xtb[:], vtm[:, c, :], True, True, [rXtb, rvtm], [rpu])
                    pw, rpw = pr_.next()
                    P.mm(pw[:, 0:64], ktm[:, c, :], Xtg[:], True, True, [rktm, rXtg], [rpw])
                    yield
                    u, ru = t.ur.next()
                    P.act("scalar", u[:], pu[0:64, 0:128], AF.Copy, [rpu], [ru])
                    wT, rwT = t.wTr.next()
                    P.cp("vector", wT[:], pw[:, 0:64], [rpw], [rwT])
                    yield
                    pws, rpws = pr_.next()
                    P.mm(pws[0:64, 0:128], wT[:], t.Sb[:], True, True, [rwT, t.r_Sb], [rpws])
                    po_, rpo = pr_.next()
                    P.mm(po_[:, 0:64], t.Sb[:], qd[:], True, False, [t.r_Sb, rqd], [rpo])
                    yield
                    vn, rvn = t.vnr.next()
                    P.tt("vector", vn[:], u[:], pws[0:64, 0:128], ALU.subtract, [ru, rpws], [rvn])
                    yield
                    P.mm(po_[:, 0:64], vn[:], qkT[:], False, True, [rvn, rqkT], [rpo])
                    pds, rpds = pr_.next()
                    P.mm(pds[:, 0:128], ke[:], vn[:], True, True, [rke, rvn], [rpds])
                    yield
                    P.act("scalar", oT[:, d_, cs], po_[:, 0:64], AF.Copy, [rpo], [ro])
                    P.stt("vector", t.Sf[:], t.Sf[:], egrep[:, END:END + 1], pds[:, 0:128], ALU.mult, ALU.add,
                          [t.r_S, regr, rpds], [t.r_S])
                    yield
                    P.act("scalar", t.Sb[:], t.Sf[:], AF.Copy, [t.r_S], [t.r_Sb])
                    yield

                def chain(t, hi, h, d_):
                    for step in range(NCH):
                        yield from unit(t, hi, h, d_, orders[d_][step])

                HP = [(0, 1), (2, 3)]
                if "gdh1" in debug:
                    HP = [(0,)]
                for heads in HP:
                    for hi, h in enumerate(heads):
                        zt, rz = zr.next()
                        P.dma("sync", zt[:], ZT[ZK + h * 128:ZK + (h + 1) * 128, :], [R("ZT")], [rz])
                        kf, rkf = conv_silu(zt, rz, 4 + h)
                        l2norm(kf, rkf, kTs[hi], r_kT[hi], 1.0)
                        transposes(kf, rkf, ktms[hi], r_ktm[hi])
                        zt, rz = zr.next()
                        P.dma("sync", zt[:], ZT[ZQ + h * 128:ZQ + (h + 1) * 128, :], [R("ZT")], [rz])
                        qf, rqf = conv_silu(zt, rz, h)
                        l2norm(qf, rqf, qTs[hi], r_qT[hi], 128 ** -0.5)
                        zt, rz = zr.next()
                        P.dma("sync", zt[:], ZT[ZV_ + h * 128:ZV_ + (h + 1) * 128, :], [R("ZT")], [rz])
                        vf, rvf = conv_silu(zt, rz, 8 + h)
                        transposes(vf, rvf, vtms[hi], r_vtm[hi])
                    gens = []
                    for hi, h in enumerate(heads):
                        for d_ in range(2):
                            t = chains_t[hi * 2 + d_]
                            P.add("gpsimd", lambda e, t=t: e.memset(t.Sf[:], 0.0), [], [t.r_S])
                            P.add("gpsimd", lambda e, t=t: e.memset(t.Sb[:], 0.0), [], [t.r_Sb])
                            gens.append(chain(t, hi, h, d_))
                    lockstep(gens)
                    for hi, h in enumerate(heads):
                        oT, ro = oTs[hi], r_oT[hi]
                        zt, rz = zr.next()
                        P.dma("sync", zt[:], ZT[ZZ + h * 128:ZZ + (h + 1) * 128, :], [R("ZT")], [rz])
                        P.act("scalar", zt[:], zt[:], AF.Silu, [rz], [rz])
                        osum, rosum = cvr.next()
                        P.add("vector", lambda e, oT=oT, osum=osum: e.tensor_tensor(osum[:], oT[:, 0, :], oT[:, 1, :], ALU.add), [ro], [rosum])
                        ys, rys = ysr.next()
                        for (t0, w) in TOK_TILES:
                            s_, rsq = sqr.next()
                            P.act("scalar", s_[:, :w], osum[:, t0:t0 + w], AF.Square, [rosum], [rsq])
                            ps, rp = pr.next()
                            P.mm(ps[:, :w], ones_f[:], s_[:, :w], True, True, [r_const, rsq], [rp])
                            rstd, rr = rsr.next()
                            P.act("scalar", rstd[:, :w], ps[:, :w], AF.Sqrt, [rp, r_const], [rr], bias=epst[:, 0:1], scale=1.0 / 128)
                            P.add("vector", lambda e, rstd=rstd, w=w: e.reciprocal(rstd[:, :w], rstd[:, :w]), [rr], [rr])
                            P.add("vector", lambda e, rstd=rstd, osum=osum, t0=t0, w=w: e.scalar_tensor_tensor(
                                rstd[:, :w], osum[:, t0:t0 + w], vec_t[:, l, 2:3], rstd[:, :w], ALU.mult, ALU.mult),
                                [rosum, rr, r_const], [rr])
                            P.add("vector", lambda e, ys=ys, rstd=rstd, zt=zt, t0=t0, w=w: e.tensor_tensor(
                                ys[:, t0:t0 + w], rstd[:, :w], zt[:, t0:t0 + w], ALU.mult), [rr, rz], [rys])
                        P.dma("sync", YT[2, h * 128:(h + 1) * 128, :], ys[:], [rys], [R(("YT", 2))])

        def merge_phase(l, Xsrc, xkey, Xdst, dkey, last=False):
            ZG = 6176
            with B.phase() as C:
                yt = C.sb([128, 16, TOK], BF16, "yt"); r_yt = Res()
                mg = C.sb([128, KC, TOK], BF16, "mg"); r_mg = Res()
                P.dma("sync", yt[:], YT.rearrange("n (k p) t -> p (n k) t", p=128), [R(("YT", n)) for n in range(4)], [r_yt])
                wbr = C.ring_sb(2, [128, 16, 128], BF16, "wbr")
                gr = C.ring_sb(4, [128, 4, 256], F32, "gate")
                pm = C.ring_ps(4, [128, 512], F32, "pm")
                acr = C.ring_sb(3, [128, 256], F32, "acc")
                tilesA = NORM_TILES[1:] if last else NORM_TILES
                tilesB = TOK_TILES[1:] if last else TOK_TILES
                gq = 0
                for fc in range(KC):
                    wb, rwb = wbr.next()
                    P.dma("gpsimd", wb[:], w_branch[l, :, :, fc * 128:(fc + 1) * 128].rearrange("n (k p) c -> p (n k) c", p=128),
                          [], [rwb])
                    for (t0, w) in tilesA:
                        gt_, rgt = gr.next()
                        gq += 1
                        P.dma("scalar" if gq % 2 else "sync", gt_[:, :, :w],
                              ZT[ZG:ZG + 4 * D, t0:t0 + w].rearrange("(n r) t -> r n t", r=D)[fc * 128:(fc + 1) * 128],
                              [R("ZT")], [rgt])
                        P.act("scalar", gt_[:, :, :w], gt_[:, :, :w], AF.Sigmoid, [rgt], [rgt])
                        acc, racc = acr.next()
                        for n in range(4):
                            ps, rp = pm.next()
                            for k in range(4):
                                P.mm(ps[:, :w], wb[:, n * 4 + k, :], yt[:, n * 4 + k, t0:t0 + w], k == 0, k == 3,
                                     [rwb, r_yt], [rp])
                            if n == 0:
                                P.tt("vector", acc[:, :w], gt_[:, 0, :w], ps[:, :w], ALU.mult, [rgt, rp], [racc])
                            else:
                                P.tt("vector", gt_[:, n, :w], gt_[:, n, :w], ps[:, :w], ALU.mult, [rgt, rp], [rgt])
                                if n < 3:
                                    P.tt("gpsimd" if n == 2 else "vector", acc[:, :w], acc[:, :w], gt_[:, n, :w], ALU.add, [rgt, racc], [racc])
                                else:
                                    P.tt("vector", mg[:, fc, t0:t0 + w], acc[:, :w], gt_[:, n, :w], ALU.add, [rgt, racc], [r_mg])
                wor = C.ring_sb(2, [128, KC, 128], BF16, "wo")
                xrr = C.ring_sb(2, [128, 512], F32, "xres")
                orr = C.ring_sb(2, [128, 512], F32, "ores")
                for fc in range(KC):
                    wo, rwo = wor.next()
                    P.dma("gpsimd", wo[:], w_out_all[l, :, fc * 128:(fc + 1) * 128].rearrange("(k p) n -> p k n", p=128),
                          [], [rwo])
                    if True:
                        for (t0, w) in tilesB:
                            j = 1 if t0 < T_CTX else 0
                            y, ry = pm.next()
                            for k in range(KC):
                                P.mm(y[:, :w], wo[:, k, :], mg[:, k, t0:t0 + w], k == 0, k == KC - 1,
                                     [rwo, r_mg], [ry])
                            xr_, rxr = xrr.next()
                            P.dma("scalar", xr_[:, :w], Xsrc[fc * 128:(fc + 1) * 128, t0:t0 + w], [R((xkey, fc))], [rxr])
                            o, ro = orr.next()
                            P.add("vector", lambda e, o=o, y=y, xr_=xr_, fc=fc, j=j, w=w: e.scalar_tensor_tensor(
                                o[:, :w], y[:, :w], Hg[:, 1, fc, j:j + 1], xr_[:, :w], ALU.mult, ALU.add),
                                [ry, rxr, r_mod], [ro])
                            P.dma("sync", Xdst[fc * 128:(fc + 1) * 128, t0:t0 + w], o[:, :w], [ro], [R((dkey, fc))])

        X0 = xT_in
        for l in range(nlayers):
            if "only_gdn" in debug:
                gdn_phase(0)
                break
            if "only_ml" in debug:
                mlstm_phase(0)
                break
            if "only_attn" in debug:
                attn_phase(0)
                break
            if "only_sg" in debug:
                sg_phase(0)
                break
            mod_phase(l)
            ffn(l, 0, X0 if l == 0 else Xs[2], "X0" if l == 0 else ("X2", l - 1), Xs[0], ("Xa", l))
            if stop_after == ("ffn1", l):
                break
            if "nomix" in debug:
                ffn(l, 1, Xs[0], ("Xa", l), Xs[2], ("X2", l), final_out=(l == nlayers - 1))
                continue
            inproj(l, Xs[0], ("Xa", l))
            if stop_after == ("inproj", l):
                break
            if "zero_y12" in debug:
                with B.phase() as C:
                    zt_ = C.sb([128, TOK], BF16, "zt_"); rz_ = Res()
                    P.add("vector", lambda e: e.memset(zt_[:], 0.0), [], [rz_])
                    for n_ in ([1] if "noml" in debug else []) + ([2] if "nogd" in debug else []):
                        for k_ in range(4):
                            P.dma("sync", YT[n_, k_ * 128:(k_ + 1) * 128, :], zt_[:], [rz_], [R(("YT", n_))])
            attn_phase(l)
            sg_phase(l)
            if "noml" not in debug:
                mlstm_phase(l)
            if "nogd" not in debug:
                gdn_phase(l)
            if stop_after == ("mix", l):
                break
            merge_phase(l, Xs[0], ("Xa", l), Xs[1], ("Xb", l), last=(l == nlayers - 1))
            if stop_after == ("merge", l):
                break
            ffn(l, 1, Xs[1], ("Xb", l), Xs[2], ("X2", l), final_out=(l == nlayers - 1))
        P.emit(final=True)
    return nc


def _host_consts():
    m = {}
    cst = np.zeros((128, 4, 128), np.float32)
    cst[:, 0, :] = np.eye(128)
    pm = np.zeros((128, 128), np.float32)
    for p in range(128):
        if p % 64 < 32:
            pm[p + 32, p] = -1.0
        else:
            pm[p - 32, p] = 1.0
    cst[:, 1, :] = pm
    cst[:, 2, :] = np.tril(np.ones((128, 128)))
    cst[:, 3, :] = np.triu(np.ones((128, 128)))
    m["consts"] = cst
    n = 32
    inv = (10000.0 ** (-np.arange(n, dtype=np.float32) / n)).astype(np.float32)
    t = np.arange(T_LAT)
    ar = (t // 64).astype(np.float32)[:, None] * inv
    ac = (t % 64).astype(np.float32)[:, None] * inv
    cos = np.concatenate([np.cos(ar), np.cos(ar), np.cos(ac), np.cos(ac)], axis=1).T
    sin = np.concatenate([np.sin(ar), np.sin(ar), np.sin(ac), np.sin(ac)], axis=1).T
    m["ropetab"] = np.ascontiguousarray(np.stack([cos, sin], axis=1).astype(np.float32))
    lm = np.zeros((64, 7, 64), np.float32)
    ii = np.arange(64)
    for lv in range(6):
        lm[:, lv, :] = ((ii[:, None] >> (lv + 1)) == (ii[None, :] >> (lv + 1))) & ((ii[:, None] >> lv) != (ii[None, :] >> lv))
    lm[:, 6, :] = (ii[:, None] != ii[None, :])
    m["lm"] = lm
    return m


def _host_inputs(inp, b, shared):
    m = dict(shared)
    m["xT"] = np.ascontiguousarray(np.concatenate([inp["ctx"][b], inp["x"][b]], axis=0).T)
    m["cvec"] = np.ascontiguousarray(np.stack([inp["c"][b].reshape(KC, 128).T, inp["c_ctx"].reshape(KC, 128).T], axis=-1))
    return m


def kernel(**inp):
    inp = {k: np.asarray(v) for k, v in inp.items()}
    shared = _host_consts()
    for k in ("mod_w", "ffn1_w_in", "ffn2_w_in", "ffn1_w_out", "ffn2_w_out", "w_in", "w_branch", "w_out", "sg_b"):
        shared[k] = inp[k]
    shared["mod_b"] = np.ascontiguousarray(inp["mod_b"].reshape(DEPTH, 144, 128).transpose(0, 2, 1))
    nr = np.stack([inp["ffn1_norm"], inp["mix_norm"], inp["ffn2_norm"]], axis=1)
    shared["norms"] = np.ascontiguousarray(nr.reshape(DEPTH, 3, KC, 128).transpose(0, 3, 1, 2))
    shared["sgwT"] = np.ascontiguousarray(inp["sg_w"].transpose(0, 1, 3, 2))
    v = np.zeros((128, DEPTH, 12), np.float32)
    v[:, :, 0] = inp["at_q_norm"].T
    v[:, :, 1] = inp["at_k_norm"].T
    v[:, :, 2] = inp["gd_norm"].T
    v[:, :, 4:8] = inp["sg_norm"].reshape(DEPTH, 4, 128).transpose(2, 0, 1)
    v[:, :, 8:12] = inp["ml_norm"].reshape(DEPTH, 4, 128).transpose(2, 0, 1)
    shared["vecs"] = v
    shared["mlb"] = np.ascontiguousarray(np.broadcast_to(inp["ml_if_bias"][None], (128, DEPTH, 16))).astype(np.float32)
    g = np.concatenate([inp["gd_a_log"].reshape(DEPTH, 8), inp["gd_dt_bias"].reshape(DEPTH, 8)], axis=1)
    shared["gdc"] = np.ascontiguousarray(np.broadcast_to(g[None], (128, DEPTH, 16))).astype(np.float32)
    shared["cw"] = np.ascontiguousarray(inp["gd_conv"].reshape(DEPTH, 3, 12, 128).transpose(3, 0, 1, 2))
    nc = build_program()
    n = inp["x"].shape[0]
    in_maps = [_host_inputs(inp, b, shared) for b in range(n)]
    res = run_bass_kernel_spmd(nc, in_maps, core_ids=list(range(n)))
    out = np.stack([np.ascontiguousarray(r["outT"].T) for r in res.results], axis=0)
    return out.astype(np.float32)
```
